# Optimizing a Trainium2 kernel written in Bass

```python
import math
import jax
import jax.numpy as jnp
from jax import lax
import numpy as np

D_MODEL = 2048
BATCH = 16
SEQ = 2048
DEPTH = 4

CTX_LEN = 256
GRID_W = 64
HEAD_DIM = 128
N_HEADS_A = 8
N_KV_A = 2
N_HEADS_B = 8
N_KV_B = 2
BLOCK_Q = 128
WINDOW = 128
ROPE_THETA = 10000.0
D_FF = 5632
N_MOD = 9
HYENA_ORDER = 2
HYENA_SHORT_K = 3
HYENA_HIDDEN = 64
HYENA_BANDS = 16
HYENA_POS_DIM = 1 + 2 * HYENA_BANDS
HYENA_TARGET = 1e-2
HYENA_FAST_PCT = 0.3
HYENA_SLOW_PCT = 1.5
EPS = 1e-6
NEG_INF = -1e30

Q_A = N_HEADS_A * HEAD_DIM
Q_B = N_HEADS_B * HEAD_DIM
KV_A = N_KV_A * HEAD_DIM
KV_B = N_KV_B * HEAD_DIM
KV_OFF = Q_A + Q_B
ATTN_IN = KV_OFF + 2 * KV_A + 2 * KV_B
ATTN_OUT = Q_A + Q_B

kernel_name = "hybrid_dit_gqa_window_hyena_macaron"


def rms_norm(x, g):
    xf = x.astype(jnp.float32)
    y = xf * lax.rsqrt(jnp.mean(xf * xf, axis=-1, keepdims=True) + EPS)
    return (y * g.astype(jnp.float32)).astype(x.dtype)


def modulate(x, g, m, k):
    shift = m[:, 3 * k][:, None, :]
    scale = m[:, 3 * k + 1][:, None, :]
    return rms_norm(x, g) * (1.0 + scale) + shift


def gate_of(m, k):
    return m[:, 3 * k + 2][:, None, :]


def swiglu(h, w_in, w_out):
    g, u = jnp.split(h @ w_in, 2, axis=-1)
    return (jax.nn.silu(g) * u) @ w_out


def axial_rope(n_tok):
    rows = n_tok // GRID_W
    r, col = jnp.meshgrid(jnp.arange(rows), jnp.arange(GRID_W), indexing="ij")
    r = r.reshape(-1).astype(jnp.float32)
    col = col.reshape(-1).astype(jnp.float32)
    half = HEAD_DIM // 2
    inv = ROPE_THETA ** (-jnp.arange(0, half, 2, dtype=jnp.float32) / half)
    ang = jnp.concatenate([r[:, None] * inv, col[:, None] * inv], axis=-1)
    return jnp.cos(ang), jnp.sin(ang)


def apply_rope(t, rope):
    cos, sin = rope
    tf = t.astype(jnp.float32).reshape(t.shape[:-1] + (HEAD_DIM // 2, 2))
    t0, t1 = tf[..., 0], tf[..., 1]
    c = cos[None, :, None, :]
    s = sin[None, :, None, :]
    out = jnp.stack([t0 * c - t1 * s, t0 * s + t1 * c], axis=-1)
    return out.reshape(t.shape).astype(t.dtype)


def heads(t, n):
    return t.reshape(t.shape[:2] + (n, HEAD_DIM))


def groups(q, n_kv):
    return q.reshape(q.shape[:2] + (n_kv, q.shape[2] // n_kv, HEAD_DIM))


def merge_heads(o):
    return o.reshape(o.shape[:2] + (-1,))


def gqa_attend(q, k, v, mask, sink):
    s = jnp.einsum("bqhgd,bkhd->bhgqk", q.astype(jnp.float32), k.astype(jnp.float32)) * (HEAD_DIM ** -0.5)
    if mask is not None:
        s = jnp.where(mask, s, NEG_INF)
    if sink is None:
        p = jax.nn.softmax(s, axis=-1)
    else:
        sk = jnp.broadcast_to(sink.astype(jnp.float32)[None, :, :, None, None], s.shape[:-1] + (1,))
        p = jax.nn.softmax(jnp.concatenate([s, sk], axis=-1), axis=-1)[..., :-1]
    o = jnp.einsum("bhgqk,bkhd->bqhgd", p, v.astype(jnp.float32))
    return o.astype(q.dtype)


def global_attention(q, k, v):
    b, n = q.shape[:2]
    nb = n // BLOCK_Q
    qb = jnp.moveaxis(q.reshape((b, nb, BLOCK_Q) + q.shape[2:]), 1, 0)
    ob = lax.map(lambda qq: gqa_attend(qq, k, v, None, None), qb)
    return jnp.moveaxis(ob, 0, 1).reshape(q.shape)


def window_attention(q, k, v, k_ctx, v_ctx, sink):
    b, n = q.shape[:2]
    nb = n // BLOCK_Q
    n_ctx = k_ctx.shape[1]
    pad = ((0, 0), (BLOCK_Q, BLOCK_Q), (0, 0), (0, 0))

    def band(t):
        tp = jnp.pad(t, pad).reshape((b, nb + 2, BLOCK_Q) + t.shape[2:])
        tw = jnp.concatenate([tp[:, :-2], tp[:, 1:-1], tp[:, 2:]], axis=2)
        return jnp.moveaxis(tw, 1, 0)

    kw, vw = band(k), band(v)
    blk = jnp.arange(nb)[:, None, None]
    q_pos = blk * BLOCK_Q + jnp.arange(BLOCK_Q)[None, :, None]
    k_pos = (blk - 1) * BLOCK_Q + jnp.arange(3 * BLOCK_Q)[None, None, :]
    lat_mask = (jnp.abs(k_pos - q_pos) <= WINDOW) & (k_pos >= 0) & (k_pos < n)
    mask = jnp.concatenate([jnp.ones((nb, BLOCK_Q, n_ctx), dtype=bool), lat_mask], axis=-1)
    qb = jnp.moveaxis(q.reshape((b, nb, BLOCK_Q) + q.shape[2:]), 1, 0)

    def one(args):
        qq, kk, vv, mm = args
        return gqa_attend(qq, jnp.concatenate([k_ctx, kk], axis=1), jnp.concatenate([v_ctx, vv], axis=1), mm, sink)

    ob = lax.map(one, (qb, kw, vw, mask))
    return jnp.moveaxis(ob, 0, 1).reshape(q.shape)


def attn_project(h, w_in, g_q, g_k, rope, with_q):
    if with_q:
        p = h @ w_in
        q_a, q_b, rest = p[..., :Q_A], p[..., Q_A:KV_OFF], p[..., KV_OFF:]
    else:
        rest = h @ w_in[:, KV_OFF:]
    k_a, v_a, k_b, v_b = jnp.split(rest, [KV_A, 2 * KV_A, 2 * KV_A + KV_B], axis=-1)
    k_a = rms_norm(heads(k_a, N_KV_A), g_k[0])
    k_b = rms_norm(heads(k_b, N_KV_B), g_k[1])
    v_a = heads(v_a, N_KV_A)
    v_b = heads(v_b, N_KV_B)
    if rope is not None:
        k_a = apply_rope(k_a, rope)
        k_b = apply_rope(k_b, rope)
    if not with_q:
        return None, k_a, v_a, None, k_b, v_b
    q_a = rms_norm(heads(q_a, N_HEADS_A), g_q[0])
    q_b = rms_norm(heads(q_b, N_HEADS_B), g_q[1])
    if rope is not None:
        q_a = apply_rope(q_a, rope)
        q_b = apply_rope(q_b, rope)
    return groups(q_a, N_KV_A), k_a, v_a, groups(q_b, N_KV_B), k_b, v_b


def attn_mixer(h, hc, w_in, w_out, g_q, g_k, sink, rope, ctx_out):
    q_a, k_a, v_a, q_b, k_b, v_b = attn_project(h, w_in, g_q, g_k, rope, True)
    cq_a, ck_a, cv_a, cq_b, ck_b, cv_b = attn_project(hc, w_in, g_q, g_k, None, ctx_out)
    sink_b = sink.reshape(N_KV_B, N_HEADS_B // N_KV_B)
    o_a = global_attention(q_a, jnp.concatenate([ck_a, k_a], axis=1), jnp.concatenate([cv_a, v_a], axis=1))
    o_b = window_attention(q_b, k_b, v_b, ck_b, cv_b, sink_b)
    y = jnp.concatenate([merge_heads(o_a), merge_heads(o_b)], axis=-1) @ w_out
    if not ctx_out:
        return y, None
    co_a = gqa_attend(cq_a, ck_a, cv_a, None, None)
    co_b = gqa_attend(cq_b, ck_b, cv_b, None, sink_b)
    yc = jnp.concatenate([merge_heads(co_a), merge_heads(co_b)], axis=-1) @ w_out
    return y, yc


def hyena_filter(n, w1, b1, fr1, w2, b2, fr2, w3):
    f32 = jnp.float32
    t = jnp.linspace(0.0, 1.0, n, dtype=f32)[:, None]
    w = (2.0 * math.pi / n) * jnp.arange(n, dtype=f32)[:, None]
    bands = jnp.linspace(1e-4, HYENA_BANDS - 1, HYENA_BANDS, dtype=f32)
    ang = w * bands[None, :]
    feats = jnp.concatenate([t, jnp.cos(ang), -jnp.sin(ang)], axis=-1)
    hdn = jnp.sin(fr1.astype(f32) * (feats @ w1.astype(f32) + b1.astype(f32)))
    hdn = jnp.sin(fr2.astype(f32) * (hdn @ w2.astype(f32) + b2.astype(f32)))
    k = (hdn @ w3.astype(f32)).reshape(n, HYENA_ORDER, 2, D_MODEL)
    max_decay = math.log(HYENA_TARGET) / HYENA_FAST_PCT
    min_decay = math.log(HYENA_TARGET) / HYENA_SLOW_PCT
    deltas = jnp.linspace(min_decay, max_decay, D_MODEL, dtype=f32)
    k = k * jnp.exp(-t * jnp.abs(deltas)[None, :])[:, None, None, :]
    fwd, bwd = k[:, :, 0], k[:, :, 1]
    two = jnp.concatenate([fwd, jnp.zeros_like(fwd[:1]), bwd[:0:-1]], axis=0)
    two = two * lax.rsqrt(jnp.sum(two * two, axis=0, keepdims=True) + EPS)
    return jnp.fft.rfft(two, axis=0)


def long_conv(u, kf, bias):
    n = u.shape[1]
    uf32 = u.astype(jnp.float32)
    uf = jnp.fft.rfft(uf32, n=2 * n, axis=1)
    y = jnp.fft.irfft(uf * kf[None], n=2 * n, axis=1)[:, :n]
    return (y + uf32 * bias.astype(jnp.float32)).astype(u.dtype)


def hyena_mixer(h, w_in, b_in, w_conv, b_conv, hf_w1, hf_b1, hf_freq1, hf_w2, hf_b2, hf_freq2, hf_w3, bias, w_out, b_out):
    n = h.shape[1]
    kf = hyena_filter(n, hf_w1, hf_b1, hf_freq1, hf_w2, hf_b2, hf_freq2, hf_w3)
    p = h @ w_in + b_in
    half = HYENA_SHORT_K // 2
    pp = jnp.pad(p, ((0, 0), (half, half), (0, 0)))
    pc = sum((pp[:, j:j + n] * w_conv[j] for j in range(HYENA_SHORT_K)), b_conv)
    v, x1, x2 = jnp.split(pc, 3, axis=-1)
    z = v
    for o, g in enumerate((x1, x2)):
        z = g * long_conv(z, kf[:, o], bias[o])
    return z @ w_out + b_out


def setup_inputs(seed: int = 0) -> dict:
    key = jax.random.key(seed)
    ks = iter(jax.random.split(key, 32))

    def nrm(shape, scale):
        return jax.random.normal(next(ks), shape, jnp.float32) * scale

    D = D_MODEL
    n_even = (DEPTH + 1) // 2
    n_odd = DEPTH // 2
    return {
        "x": nrm((BATCH, SEQ, D), 1.0),
        "c": nrm((BATCH, D), 1.0),
        "ctx": nrm((BATCH, CTX_LEN, D), 1.0),
        "c_ctx": nrm((D,), 1.0),
        "w_mod": nrm((DEPTH, D, N_MOD * D), 0.5 * D ** -0.5),
        "b_mod": nrm((DEPTH, N_MOD * D), 0.02),
        "g_norm": 1.0 + nrm((DEPTH, 3, D), 0.05),
        "w_ffn_in": nrm((DEPTH, 2, D, 2 * D_FF), D ** -0.5),
        "w_ffn_out": nrm((DEPTH, 2, D_FF, D), D_FF ** -0.5),
        "w_attn_in": nrm((n_even, D, ATTN_IN), D ** -0.5),
        "w_attn_out": nrm((n_even, ATTN_OUT, D), ATTN_OUT ** -0.5),
        "g_q": 1.0 + nrm((n_even, 2, HEAD_DIM), 0.05),
        "g_k": 1.0 + nrm((n_even, 2, HEAD_DIM), 0.05),
        "sink": nrm((n_even, N_HEADS_B), 0.5),
        "w_hy_in": nrm((n_odd, D, 3 * D), D ** -0.5),
        "b_hy_in": nrm((n_odd, 3 * D), 0.02),
        "w_hy_conv": nrm((n_odd, HYENA_SHORT_K, 3 * D), HYENA_SHORT_K ** -0.5),
        "b_hy_conv": nrm((n_odd, 3 * D), 0.02),
        "hf_w1": nrm((n_odd, HYENA_POS_DIM, HYENA_HIDDEN), HYENA_POS_DIM ** -0.5),
        "hf_b1": nrm((n_odd, HYENA_HIDDEN), 0.1),
        "hf_freq1": 1.0 + nrm((n_odd, HYENA_HIDDEN), 0.1),
        "hf_w2": nrm((n_odd, HYENA_HIDDEN, HYENA_HIDDEN), HYENA_HIDDEN ** -0.5),
        "hf_b2": nrm((n_odd, HYENA_HIDDEN), 0.1),
        "hf_freq2": 1.0 + nrm((n_odd, HYENA_HIDDEN), 0.1),
        "hf_w3": nrm((n_odd, HYENA_HIDDEN, HYENA_ORDER * 2 * D), HYENA_HIDDEN ** -0.5),
        "hy_bias": nrm((n_odd, HYENA_ORDER, D), 0.1),
        "w_hy_out": nrm((n_odd, D, D), D ** -0.5),
        "b_hy_out": nrm((n_odd, D), 0.02),
    }


def reference(x, c, ctx, c_ctx, w_mod, b_mod, g_norm, w_ffn_in, w_ffn_out, w_attn_in, w_attn_out, g_q, g_k, sink,
              w_hy_in, b_hy_in, w_hy_conv, b_hy_conv, hf_w1, hf_b1, hf_freq1, hf_w2, hf_b2, hf_freq2, hf_w3,
              hy_bias, w_hy_out, b_hy_out):
    n_tok = x.shape[1]
    rope = axial_rope(n_tok)
    s_lat = jax.nn.silu(c)
    s_ctx = jax.nn.silu(c_ctx)[None, :]
    last_ctx = max(l for l in range(DEPTH) if l % 2 == 0)
    xc = ctx
    for l in range(DEPTH):
        i = l // 2
        ctx_live = l <= last_ctx
        ctx_full = l < last_ctx
        m = (s_lat @ w_mod[l] + b_mod[l]).reshape(-1, N_MOD, D_MODEL)
        if ctx_live:
            mc = (s_ctx @ w_mod[l] + b_mod[l]).reshape(-1, N_MOD, D_MODEL)
        x = x + 0.5 * gate_of(m, 0) * swiglu(modulate(x, g_norm[l, 0], m, 0), w_ffn_in[l, 0], w_ffn_out[l, 0])
        if ctx_live:
            xc = xc + 0.5 * gate_of(mc, 0) * swiglu(modulate(xc, g_norm[l, 0], mc, 0), w_ffn_in[l, 0], w_ffn_out[l, 0])
        h = modulate(x, g_norm[l, 1], m, 1)
        if l % 2 == 0:
            hc = modulate(xc, g_norm[l, 1], mc, 1)
            y, yc = attn_mixer(h, hc, w_attn_in[i], w_attn_out[i], g_q[i], g_k[i], sink[i], rope, ctx_full)
        else:
            y = hyena_mixer(h, w_hy_in[i], b_hy_in[i], w_hy_conv[i], b_hy_conv[i], hf_w1[i], hf_b1[i], hf_freq1[i],
                            hf_w2[i], hf_b2[i], hf_freq2[i], hf_w3[i], hy_bias[i], w_hy_out[i], b_hy_out[i])
            if ctx_full:
                hc = modulate(xc, g_norm[l, 1], mc, 1)
                yc = hyena_mixer(hc, w_hy_in[i], b_hy_in[i], w_hy_conv[i], b_hy_conv[i], hf_w1[i], hf_b1[i],
                                 hf_freq1[i], hf_w2[i], hf_b2[i], hf_freq2[i], hf_w3[i], hy_bias[i], w_hy_out[i],
                                 b_hy_out[i])
        x = x + gate_of(m, 1) * y
        if ctx_full:
            xc = xc + gate_of(mc, 1) * yc
        x = x + 0.5 * gate_of(m, 2) * swiglu(modulate(x, g_norm[l, 2], m, 2), w_ffn_in[l, 1], w_ffn_out[l, 1])
        if ctx_full:
            xc = xc + 0.5 * gate_of(mc, 2) * swiglu(modulate(xc, g_norm[l, 2], mc, 2), w_ffn_in[l, 1], w_ffn_out[l, 1])
    return x
```

```python
import math
from contextlib import ExitStack
import numpy as np
import concourse.bass as bass
import concourse.mybir as mybir
from concourse.bass_utils import run_bass_kernel_spmd

F32 = mybir.dt.float32
BF16 = mybir.dt.bfloat16
ALU = mybir.AluOpType
AF = mybir.ActivationFunctionType
AX = mybir.AxisListType

D = 2048
KC = D // 128
SEQ = 2048
CTX = 256
NCORES = 8
NB = 16 // NCORES
NR = 8
NLAT = NB * SEQ
NTOK = NLAT + NB * CTX
DFF = 5632
FC = DFF // 128
NMOD = 9
DEPTH = 4
HD = 128
EPS = 1e-6
TT = 512


class Buf:
    __slots__ = ("name", "w", "r")

    def __init__(self, name):
        self.name = name
        self.w = {}
        self.r = {}


class Sched:
    def __init__(self, nc, es, ndma=12):
        self.nc = nc
        self.E = {"pe": nc.tensor, "act": nc.scalar, "dve": nc.vector, "pool": nc.gpsimd, "sp": nc.sync}
        self.sems = {}
        self.cnt = {}
        self.waited = {e: {} for e in self.E}
        self.es = es
        for e in ("pe", "act", "dve", "pool"):
            self._mk("c_" + e)
        self.dq = {}
        for q in ("sp", "pool", "bg", "act"):
            self.dq[q] = [0, [self._mk(f"d_{q}_{i}") for i in range(ndma)]]
        self.nops = 0

    def _mk(self, name):
        self.sems[name] = self.es.enter_context(self.nc.semaphore(name))
        self.cnt[name] = 0
        return name

    def _deps(self, reads, writes):
        deps = {}
        for b in reads:
            for s, v in b.w.items():
                if deps.get(s, 0) < v:
                    deps[s] = v
        for b in writes:
            for s, v in b.w.items():
                if deps.get(s, 0) < v:
                    deps[s] = v
            for s, v in b.r.items():
                if deps.get(s, 0) < v:
                    deps[s] = v
        return deps

    def _wait(self, e, deps, skip=None):
        eng = self.E[e]
        wd = self.waited[e]
        for s, v in deps.items():
            if s == skip:
                continue
            if wd.get(s, 0) < v:
                eng.wait_ge(self.sems[s], v)
                wd[s] = v

    def op(self, e, reads, writes, fn, self_dep=True):
        own = "c_" + e
        deps = self._deps(reads, writes)
        self._wait(e, deps, skip=own if (e == "pe" or not self_dep) else None)
        inst = fn(self.E[e])
        self.cnt[own] += 1
        v = self.cnt[own]
        inst.then_inc(self.sems[own], 1)
        for b in reads:
            b.r[own] = v
        for b in writes:
            b.w = {own: v}
            b.r = {}
        self.nops += 1
        return inst

    def dma(self, q, out, in_, reads, writes, partial=False):
        e = q if q in ("sp", "act") else "pool"
        st = self.dq[q]
        name = st[1][st[0] % len(st[1])]
        st[0] += 1
        deps = self._deps(reads, writes)
        prev = self.cnt[name]
        if prev and deps.get(name, 0) < prev:
            deps[name] = prev
        self._wait(e, deps)
        inst = self.E[e].dma_start(out=out, in_=in_)
        self.cnt[name] += 16
        v = self.cnt[name]
        inst.then_inc(self.sems[name], 16)
        for b in reads:
            b.r[name] = v
        for b in writes:
            if partial:
                b.w[name] = v
            else:
                b.w = {name: v}
                b.r = {}
        self.nops += 1
        return inst

    def barrier(self, include_bg=False):
        deps = {}
        for s, v in self.cnt.items():
            if v and (include_bg or not s.startswith("d_bg")):
                deps[s] = v
        for e in self.E:
            self._wait(e, deps)


class Ctx:
    def __init__(self):
        self.decl = {}
        self.dbg = {}
        self.uid = 0

    def sbuf(self, name, shape, dtype):
        self.uid += 1
        return self.nc.sbuf_tensor(f"{name}_u{self.uid}", shape, dtype)

    def psum(self, name, shape, dtype):
        self.uid += 1
        return self.nc.psum_tensor(f"{name}_u{self.uid}", shape, dtype)

    def win(self, name, shape, dtype=F32):
        if name not in self.decl:
            self.decl[name] = self.nc.dram_tensor(name, list(shape), dtype, kind="ExternalInput").ap()
        return self.decl[name]

    def dump(self, name, src_ap, shape, dtype, reads):
        if not self.debug:
            return
        t = self.nc.dram_tensor("dbg_" + name, list(shape), dtype, kind="ExternalOutput").ap()
        self.dbg[name] = t
        self.S.dma("sp", t, src_ap, reads, [Buf("dbg_" + name)])


def _mm(S, out_ap, lhsT, rhs, start, stop, reads, writes):
    return S.op("pe", reads, writes, lambda e: e.matmul(out_ap, lhsT, rhs, start=start, stop=stop))


def emit_mod(C, l):
    nc, S = C.nc, C.S
    NCH = NMOD * KC
    GRP = 4
    NG = NCH // GRP
    with ExitStack() as es:
        wsl = [es.enter_context(C.sbuf(f"modw{i}", [128, KC, GRP * 128], F32)) for i in range(2)]
        wb = [Buf(f"modw{i}") for i in range(2)]
        ps = es.enter_context(C.psum("modps", [128, 3, 512], F32))
        psb = Buf("modps")
        bm = es.enter_context(C.sbuf("modb", [128, NCH], F32))
        bmb = Buf("modb")
        gn = es.enter_context(C.sbuf("modg", [128, 3, KC], F32))
        gnb = Buf("modg")
        S.dma("sp", bm[:], C.win(f"b_mod{l}", [128, NMOD * KC]), [], [bmb])
        S.dma("sp", gn[:], C.win(f"g_norm{l}", [128, 3, KC]), [], [gnb])
        wsrc = C.win(f"w_mod{l}", [D, NMOD * D]).rearrange("(k p) n -> p k n", p=128)

        def load(g):
            S.dma("sp", wsl[g % 2][:], wsrc[:, :, g * 512:(g + 1) * 512], [], [wb[g % 2]])

        load(0)
        for g in range(NG):
            if g + 1 < NG:
                load(g + 1)
            for jj in range(GRP):
                n = g * GRP + jj
                o = ps[:, n // 64, (n % 64) * NR:(n % 64) * NR + NR]
                for k in range(KC):
                    _mm(S, o, wsl[g % 2][:, k, jj * 128:(jj + 1) * 128], C.s_sb[:, k, :], k == 0, k == KC - 1,
                        [wb[g % 2], C.s_b], [psb])
        for r in range(NB + 1):
            for hb in range(3):
                n0 = hb * 64
                cn = min(64, NCH - n0)
                S.op("dve", [psb, bmb], [C.modv_b],
                     lambda e, r=r, hb=hb, n0=n0, cn=cn: e.tensor_tensor(
                         C.modv[:, n0:n0 + cn, r],
                         ps[:, hb, 0:cn * NR].rearrange("p (n r) -> p n r", r=NR)[:, :, r],
                         bm[:, n0:n0 + cn], ALU.add), self_dep=False)
        for k in range(3):
            for r in range(NB + 1):
                S.op("dve", [C.modv_b, gnb], [C.gs_b],
                     lambda e, k=k, r=r: e.scalar_tensor_tensor(
                         C.gs[:, k, :, r], C.modv[:, (3 * k + 1) * KC:(3 * k + 2) * KC, r], 1.0, gn[:, k, :],
                         ALU.add, ALU.mult), self_dep=(k == 0 and r == 0))
            S.op("dve", [C.modv_b], [C.hg_b],
                 lambda e, k=k: e.tensor_scalar(
                     C.hg[:, k, :, :], C.modv[:, (3 * k + 2) * KC:(3 * k + 3) * KC, :],
                     0.5 if k != 1 else 1.0, None, ALU.mult), self_dep=(k == 0))
        C.dump(f"modv{l}", C.modv[:], [128, NMOD * KC, NR], F32, [C.modv_b])
        C.dump(f"gs{l}", C.gs[:], [128, 3, KC, NR], F32, [C.gs_b])
        C.dump(f"hg{l}", C.hg[:], [128, 3, KC, NR], F32, [C.hg_b])
        S.barrier()


def emit_convert_ffn(C, l, w, slot):
    S = C.S
    wi = C.win(f"w_ffn_in{l}_{w}", [D, 2 * DFF])
    wo = C.win(f"w_ffn_out{l}_{w}", [DFF, D])
    for k in range(KC):
        S.dma("bg", C.winb[slot][k * 128:(k + 1) * 128, :], wi[k * 128:(k + 1) * 128, :], [], [C.winb_b[slot]],
              partial=True)
    wo_v = wo.rearrange("(j p) (m c) -> m p j c", p=128, c=128)
    for m in range(KC):
        S.dma("bg", C.woutb[slot][m].rearrange("p (j c) -> p j c", c=128), wo_v[m], [], [C.woutb_b[slot]], partial=True)


def emit_ffn(C, l, w, slot, tiles):
    nc, S = C.nc, C.S
    k_mod = 0 if w == 0 else 2
    JG = 4
    NJG = FC // JG
    winv = C.winb[slot].rearrange("(k p) n -> p k n", p=128)
    with ExitStack() as es:
        x = es.enter_context(C.sbuf("ffn_x", [128, KC, TT], F32))
        xb = [Buf(f"ffn_x{k}") for k in range(KC)]
        h = es.enter_context(C.sbuf("ffn_h", [128, KC, TT], BF16))
        hb = Buf("ffn_h")
        a = es.enter_context(C.sbuf("ffn_a", [128, FC, TT], BF16))
        ab = Buf("ffn_a")
        wins = [es.enter_context(C.sbuf(f"ffn_wi{i}", [128, 2, KC, JG * 128], BF16)) for i in range(2)]
        winb = [Buf(f"ffn_wi{i}") for i in range(2)]
        wouts = [es.enter_context(C.sbuf(f"ffn_wo{i}", [128, FC, 128], BF16)) for i in range(2)]
        woutb = [Buf(f"ffn_wo{i}") for i in range(2)]
        sq = [es.enter_context(C.sbuf(f"ffn_sq{i}", [128, TT], BF16)) for i in range(2)]
        sqb = [Buf(f"ffn_sq{i}") for i in range(2)]
        tmp = [es.enter_context(C.sbuf(f"ffn_tmp{i}", [128, TT], F32)) for i in range(2)]
        tmpb = [Buf(f"ffn_tmp{i}") for i in range(2)]
        sg = [es.enter_context(C.sbuf(f"ffn_sg{i}", [128, TT], F32)) for i in range(2)]
        sgb = [Buf(f"ffn_sg{i}") for i in range(2)]
        rstd = es.enter_context(C.sbuf("ffn_rstd", [128, TT], F32))
        rstdb = Buf("ffn_rstd")
        psn = es.enter_context(C.psum("ffn_psn", [128, TT], F32))
        psnb = Buf("ffn_psn")
        psg = [es.enter_context(C.psum(f"ffn_psg{i}", [128, TT], F32)) for i in range(2)]
        psgb = [Buf(f"ffn_psg{i}") for i in range(2)]
        psu = [es.enter_context(C.psum(f"ffn_psu{i}", [128, TT], F32)) for i in range(2)]
        psub = [Buf(f"ffn_psu{i}") for i in range(2)]
        psy = [es.enter_context(C.psum(f"ffn_psy{i}", [128, TT], F32)) for i in range(2)]
        psyb = [Buf(f"ffn_psy{i}") for i in range(2)]

        stv = C.st.rearrange("(k p) t -> p k t", p=128)

        tasks = []
        for ti in range(len(tiles)):
            for jg in range(NJG):
                tasks.append(("in", jg))
            for m in range(KC):
                tasks.append(("out", m))
        issued = [0]
        cnts = {"in": 0, "out": 0}
        slot_of = {}

        def ensure(upto):
            while issued[0] <= min(upto, len(tasks) - 1):
                i = issued[0]
                kind, idx = tasks[i]
                sl = cnts[kind] % 2
                cnts[kind] += 1
                slot_of[i] = sl
                if kind == "in":
                    for gu in range(2):
                        c0 = gu * DFF + idx * JG * 128
                        S.dma("sp", wins[sl][:, gu, :, :], winv[:, :, c0:c0 + JG * 128], [C.winb_b[slot]], [winb[sl]],
                              partial=(gu == 1))
                else:
                    S.dma("sp", wouts[sl][:], C.woutb[slot][idx].rearrange("p (j c) -> p j c", c=128),
                          [C.woutb_b[slot]], [woutb[sl]])
                issued[0] += 1

        def load_x(ti, k):
            col0 = tiles[ti][0]
            S.dma("act", x[:, k, :], stv[:, k, col0:col0 + TT], [C.st_b], [xb[k]])

        for k in range(KC):
            load_x(0, k)
        ensure(1)
        tcount = 0
        nsq = 0
        ntmp = 0
        nps = 0
        npy = 0
        for ti, (col0, r) in enumerate(tiles):
            for k in range(KC):
                i = nsq % 2
                nsq += 1
                S.op("act", [xb[k]], [sqb[i]], lambda e, k=k, i=i: e.activation(out=sq[i][:], in_=x[:, k, :], func=AF.Square))
                _mm(S, psn[:], C.ones_bf[:], sq[i][:], k == 0, k == KC - 1, [sqb[i], C.const_b], [psnb])
            S.op("act", [psnb], [rstdb], lambda e: e.activation(out=rstd[:], in_=psn[:], func=AF.Sqrt, bias=C.eps_sb[:, 0:1], scale=1.0))
            S.op("dve", [rstdb], [rstdb], lambda e: e.reciprocal(rstd[:], rstd[:]))
            for k in range(KC):
                i = ntmp % 2
                ntmp += 1
                S.op("dve", [xb[k], rstdb, C.gs_b], [tmpb[i]],
                     lambda e, k=k, i=i: e.scalar_tensor_tensor(tmp[i][:], x[:, k, :], C.gs[:, k_mod, k, r:r + 1], rstd[:],
                                                                ALU.mult, ALU.mult))
                S.op("act", [tmpb[i], C.modv_b], [hb],
                     lambda e, k=k, i=i: e.activation(out=h[:, k, :], in_=tmp[i][:], func=AF.Identity,
                                                      bias=C.modv[:, (3 * k_mod) * KC + k, r:r + 1], scale=1.0),
                     self_dep=False)
            if ti == 0:
                C.dump(f"rstd{l}{w}", rstd[:], [128, TT], F32, [rstdb])
                C.dump(f"h{l}{w}", h[:], [128, KC, TT], BF16, [hb])
            for jg in range(NJG):
                ensure(tcount + 1)
                sl = slot_of[tcount]
                tcount += 1
                if ti == 0 and jg == 0:
                    C.dump(f"wins{l}{w}", wins[sl][:], [128, 2, KC, JG * 128], BF16, [winb[sl]])
                for jj in range(JG):
                    j = jg * JG + jj
                    pi = nps % 2
                    nps += 1
                    for k in range(KC):
                        _mm(S, psg[pi][:], wins[sl][:, 0, k, jj * 128:(jj + 1) * 128], h[:, k, :], k == 0, k == KC - 1,
                            [winb[sl], hb], [psgb[pi]])
                    for k in range(KC):
                        _mm(S, psu[pi][:], wins[sl][:, 1, k, jj * 128:(jj + 1) * 128], h[:, k, :], k == 0, k == KC - 1,
                            [winb[sl], hb], [psub[pi]])
                    S.op("act", [psgb[pi]], [sgb[pi]], lambda e, pi=pi: e.activation(out=sg[pi][:], in_=psg[pi][:], func=AF.Silu))
                    if ti == 0 and j == 0:
                        C.dump(f"sg{l}{w}", sg[pi][:], [128, TT], F32, [sgb[pi]])
                    S.op("dve", [sgb[pi], psub[pi]], [ab],
                         lambda e, pi=pi, j=j: e.tensor_tensor(a[:, j, :], sg[pi][:], psu[pi][:], ALU.mult), self_dep=False)
            if ti == 0:
                C.dump(f"a{l}{w}", a[:], [128, FC, TT], BF16, [ab])
            for m in range(KC):
                ensure(tcount + 1)
                sl = slot_of[tcount]
                tcount += 1
                pi = npy % 2
                npy += 1
                for j in range(FC):
                    _mm(S, psy[pi][:], wouts[sl][:, j, :], a[:, j, :], j == 0, j == FC - 1, [woutb[sl], ab], [psyb[pi]])
                S.op("dve", [psyb[pi], xb[m], C.hg_b], [xb[m]],
                     lambda e, pi=pi, m=m: e.scalar_tensor_tensor(x[:, m, :], psy[pi][:], C.hg[:, k_mod, m, r:r + 1], x[:, m, :],
                                                                  ALU.mult, ALU.add))
                S.dma("act", stv[:, m, col0:col0 + TT], x[:, m, :], [xb[m]], [C.st_b], partial=True)
                if ti + 1 < len(tiles):
                    load_x(ti + 1, m)
        S.barrier()


class NormKit:
    def __init__(self, C, es, pfx):
        nc = C.nc
        sb = lambda n, sh, dt_: es.enter_context(C.sbuf(f"{pfx}_{n}", sh, dt_))
        self.xk = [sb(f"xk{i}", [128, TT], F32) for i in range(3)]
        self.xkb = [Buf(f"xk{i}") for i in range(3)]
        self.sq = [sb(f"sq{i}", [128, TT], BF16) for i in range(2)]
        self.sqb = [Buf(f"sq{i}") for i in range(2)]
        self.tmp = [sb(f"tmp{i}", [128, TT], F32) for i in range(2)]
        self.tmpb = [Buf(f"tmp{i}") for i in range(2)]
        self.rstd = sb("rstd", [128, TT], F32)
        self.rstdb = Buf("rstd")
        self.psn = es.enter_context(C.psum(f"{pfx}_psn", [128, TT], F32))
        self.psnb = Buf("psn")
        self.nx = 0
        self.ns = 0
        self.nt = 0

    def emit(self, C, col0, n, r, k_mod, h, hb, hoff=0):
        S = C.S
        stv = C.st.rearrange("(k p) t -> p k t", p=128)
        for k in range(KC):
            i = self.nx % 3
            self.nx += 1
            S.dma("pool", self.xk[i][:, :n], stv[:, k, col0:col0 + n], [C.st_b], [self.xkb[i]])
            j = self.ns % 2
            self.ns += 1
            S.op("act", [self.xkb[i]], [self.sqb[j]],
                 lambda e, i=i, j=j: e.activation(out=self.sq[j][:, :n], in_=self.xk[i][:, :n], func=AF.Square))
            _mm(S, self.psn[:, :n], C.ones_bf[:], self.sq[j][:, :n], k == 0, k == KC - 1, [self.sqb[j], C.const_b], [self.psnb])
        S.op("act", [self.psnb], [self.rstdb],
             lambda e: e.activation(out=self.rstd[:, :n], in_=self.psn[:, :n], func=AF.Sqrt, bias=C.eps_sb[:, 0:1], scale=1.0))
        S.op("dve", [self.rstdb], [self.rstdb], lambda e: e.reciprocal(self.rstd[:, :n], self.rstd[:, :n]))
        for k in range(KC):
            i = self.nx % 3
            self.nx += 1
            S.dma("pool", self.xk[i][:, :n], stv[:, k, col0:col0 + n], [C.st_b], [self.xkb[i]])
            j = self.nt % 2
            self.nt += 1
            S.op("dve", [self.xkb[i], self.rstdb, C.gs_b], [self.tmpb[j]],
                 lambda e, i=i, j=j, k=k: e.scalar_tensor_tensor(self.tmp[j][:, :n], self.xk[i][:, :n], C.gs[:, k_mod, k, r:r + 1],
                                                                 self.rstd[:, :n], ALU.mult, ALU.mult))
            S.op("act", [self.tmpb[j], C.modv_b], [hb],
                 lambda e, j=j, k=k: e.activation(out=h[:, k, hoff:hoff + n], in_=self.tmp[j][:, :n], func=AF.Identity,
                                                  bias=C.modv[:, (3 * k_mod) * KC + k, r:r + 1], scale=1.0), self_dep=False)


class WStream:
    def __init__(self, S, slots, seq, src_buf, q="sp"):
        self.S = S
        self.slots = slots
        self.bufs = [Buf(f"ws{i}") for i in range(len(slots))]
        self.seq = seq
        self.src_buf = src_buf
        self.q = q
        self.issued = 0
        self.i = 0

    def _issue(self):
        k = self.issued
        item = self.seq[k]
        if not isinstance(item, list):
            item = [item]
        sl = k % 2
        for ii, (src, dst_fn) in enumerate(item):
            self.S.dma(self.q, dst_fn(self.slots[sl]), src, [self.src_buf], [self.bufs[sl]], partial=(ii > 0))
        self.issued += 1

    def get(self):
        while self.issued <= min(self.i + 1, len(self.seq) - 1):
            self._issue()
        sl = self.i % 2
        self.i += 1
        return sl


A_QW = 2048
A_KOFF = 2048
A_VOFF = 2560
NKT = (CTX + SEQ) // 128


def emit_attn(C, l):
    nc, S = C.nc, C.S
    i = l // 2
    ctx_out = (l == 0)
    w_in = C.win(f"w_attn_in{i}", [D, 3072])
    w_out = C.win(f"w_attn_out{i}", [D, D])
    gqk_d = C.win(f"gqk{i}", [128, 4])
    sink_d = C.win(f"sink{i}", [128, 8])
    rope_d = C.win("rope_cs", [2, 128, SEQ])
    rotm_d = C.win("rotm", [128, 128], BF16)
    mask_d = C.win("wmask", [128, 2, 128], BF16)
    for k in range(KC):
        S.dma("bg", C.wainb[k * 128:(k + 1) * 128, :], w_in[k * 128:(k + 1) * 128, :], [], [C.wainb_b], partial=True)
    wo_v = w_out.rearrange("(j p) (m c) -> m p j c", p=128, c=128)
    for m in range(KC):
        S.dma("bg", C.waob[m].rearrange("p (j c) -> p j c", c=128), wo_v[m], [], [C.waob_b], partial=True)
    wainv = C.wainb.rearrange("(k p) n -> p k n", p=128)
    stv = C.st.rearrange("(k p) t -> p k t", p=128)
    SCALE = float(HD) ** -0.5

    with ExitStack() as es:
        sbt = lambda n, sh, dt_: es.enter_context(C.sbuf("at_" + n, sh, dt_))
        pst = lambda n: es.enter_context(C.psum("at_" + n, [128, TT], F32))
        kit = NormKit(C, es, "at")
        h = sbt("h", [128, KC, TT], BF16)
        hb = Buf("h")
        KT = [sbt(f"kt{s_}", [128, 2, CTX + SEQ], BF16) for s_ in range(2)]
        KTb = [Buf(f"kt{s_}") for s_ in range(2)]
        V = sbt("v", [128, NKT, 512], BF16)
        Vb = Buf("v")
        OT = sbt("ot", [128, 16, TT], BF16)
        OTb = Buf("ot")
        wsl = [sbt(f"w{j}", [128, KC, 512], BF16) for j in range(2)]
        wos = [sbt(f"wo{j}", [128, 16, 128], BF16) for j in range(2)]
        ropeC = sbt("ropec", [128, SEQ], F32)
        ropeS = sbt("ropes", [128, SEQ], F32)
        rotm = sbt("rotm", [128, 128], BF16)
        masks = sbt("masks", [128, 2, 128], BF16)
        gqk = sbt("gqk", [128, 4], F32)
        esink = sbt("esink", [128, 8], F32)
        ones1 = sbt("ones1", [128, 128], BF16)
        onesh = sbt("onesh", [128, 128], BF16)
        cb = Buf("at_const")
        S.dma("sp", ropeC[:], rope_d[0], [], [cb])
        S.dma("sp", ropeS[:], rope_d[1], [], [cb], partial=True)
        S.dma("sp", rotm[:], rotm_d, [], [cb], partial=True)
        S.dma("sp", masks[:], mask_d, [], [cb], partial=True)
        S.dma("sp", gqk[:], gqk_d, [], [cb], partial=True)
        S.dma("sp", esink[:], sink_d, [], [cb], partial=True)
        S.op("act", [cb], [cb], lambda e: e.activation(out=esink[:], in_=esink[:], func=AF.Exp))
        S.op("dve", [cb], [cb], lambda e: e.memset(ones1[:], 1.0))
        S.op("dve", [cb], [cb], lambda e: e.memset(onesh[:], 1.0 / HD))
        sqh = [sbt(f"sqh{j}", [128, TT], BF16) for j in range(2)]
        sqhb = [Buf(f"sqh{j}") for j in range(2)]
        rs = sbt("rs", [128, TT], F32)
        rsb = Buf("rs")
        tgb = [sbt(f"tgb{j}", [128, TT], BF16) for j in range(2)]
        tgbb = [Buf(f"tgb{j}") for j in range(2)]
        u1 = sbt("u1", [128, TT], F32)
        u1b = Buf("u1")
        u2 = sbt("u2", [128, TT], F32)
        u2b = Buf("u2")
        qb_ = [sbt(f"q{j}", [128, TT], BF16) for j in range(2)]
        qbb = [Buf(f"q{j}") for j in range(2)]
        pT = [sbt(f"pT{j}", [128, TT], BF16) for j in range(2)]
        pTb = [Buf(f"pT{j}") for j in range(2)]
        rden = sbt("rden", [128, TT], F32)
        rdenb = Buf("rden")
        xres = [sbt(f"xres{j}", [128, TT], F32) for j in range(2)]
        xresb = [Buf(f"xres{j}") for j in range(2)]
        pt = [pst(f"pt{j}") for j in range(2)]
        ptb = [Buf(f"pt{j}") for j in range(2)]
        pr = pst("pr")
        prb = Buf("pr")
        ps = [pst(f"ps{j}") for j in range(2)]
        psb = [Buf(f"ps{j}") for j in range(2)]
        po = pst("po")
        pob = Buf("po")
        pd = pst("pd")
        pdb = Buf("pd")
        cnt = {"pt": 0, "sqh": 0, "tgb": 0, "q": 0, "ps": 0, "xres": 0}

        def rot(name, nbuf=2):
            v = cnt[name] % nbuf
            cnt[name] += 1
            return v

        def p1_tiles(be):
            t = [(NLAT + be * CTX, CTX, NB, 0, None)]
            for q in range(SEQ // TT):
                t.append((be * SEQ + q * TT, TT, be, CTX + q * TT, q * TT))
            return t

        def p2_tiles(be):
            t = []
            if ctx_out:
                t.append((NLAT + be * CTX, CTX, NB, None, None))
            for q in range(SEQ // TT):
                t.append((be * SEQ + q * TT, TT, be, q * TT, q))
            return t

        in_seq = []
        out_seq = []
        for be in C.be_list:
            for _ in p1_tiles(be):
                for c in range(4):
                    in_seq.append((wainv[:, :, A_KOFF + c * 128:A_KOFF + (c + 1) * 128], lambda sl: sl[:, :, 0:128]))
                in_seq.append((wainv[:, :, A_VOFF:A_VOFF + 512], lambda sl: sl[:, :, :]))
            for _ in p2_tiles(be):
                for hd in range(16):
                    in_seq.append((wainv[:, :, hd * 128:(hd + 1) * 128], lambda sl: sl[:, :, 0:128]))
                for m in range(KC):
                    out_seq.append((C.waob[m].rearrange("p (j c) -> p j c", c=128), lambda sl: sl[:, :, :]))
        win_s = WStream(S, wsl, in_seq, C.wainb_b)
        wo_s = WStream(S, wos, out_seq, C.waob_b)

        def qk_finish(pi, n, gcol, rope_off, dst_ap, dst_buf):
            j = rot("sqh")
            S.op("act", [ptb[pi]], [sqhb[j]], lambda e: e.activation(out=sqh[j][:, :n], in_=pt[pi][:, :n], func=AF.Square))
            _mm(S, kit.psn[:, :n], onesh[:], sqh[j][:, :n], True, True, [sqhb[j], cb], [kit.psnb])
            S.op("act", [kit.psnb], [rsb],
                 lambda e: e.activation(out=rs[:, :n], in_=kit.psn[:, :n], func=AF.Sqrt, bias=C.eps_sb[:, 0:1], scale=1.0))
            S.op("dve", [rsb], [rsb], lambda e: e.reciprocal(rs[:, :n], rs[:, :n]))
            if rope_off is None:
                S.op("dve", [ptb[pi], rsb, cb], [dst_buf],
                     lambda e: e.scalar_tensor_tensor(dst_ap, pt[pi][:, :n], gqk[:, gcol:gcol + 1], rs[:, :n], ALU.mult, ALU.mult))
                return
            t = rot("tgb")
            S.op("dve", [ptb[pi], rsb, cb], [tgbb[t]],
                 lambda e: e.scalar_tensor_tensor(tgb[t][:, :n], pt[pi][:, :n], gqk[:, gcol:gcol + 1], rs[:, :n], ALU.mult, ALU.mult))
            _mm(S, pr[:, :n], rotm[:], tgb[t][:, :n], True, True, [tgbb[t], cb], [prb])
            S.op("dve", [tgbb[t], cb], [u1b], lambda e: e.tensor_tensor(u1[:, :n], tgb[t][:, :n], ropeC[:, rope_off:rope_off + n], ALU.mult))
            S.op("dve", [prb, cb], [u2b], lambda e: e.tensor_tensor(u2[:, :n], pr[:, :n], ropeS[:, rope_off:rope_off + n], ALU.mult))
            S.op("dve", [u1b, u2b], [dst_buf], lambda e: e.tensor_tensor(dst_ap, u1[:, :n], u2[:, :n], ALU.add))

        for be in C.be_list:
            for (col0, n, r, key_off, rope_off) in p1_tiles(be):
                kit.emit(C, col0, n, r, 1, h, hb)
                for c in range(4):
                    sl = win_s.get()
                    pi = rot("pt")
                    for k in range(KC):
                        _mm(S, pt[pi][:, :n], wsl[sl][:, k, 0:128], h[:, k, :n], k == 0, k == KC - 1, [win_s.bufs[sl], hb], [ptb[pi]])
                    sset = c // 2
                    qk_finish(pi, n, 2 + sset, rope_off, KT[sset][:, c % 2, key_off:key_off + n], KTb[sset])
                sl = win_s.get()
                for sub in range(n // 128):
                    pi = rot("pt")
                    for k in range(KC):
                        _mm(S, pt[pi][:, :], h[:, k, sub * 128:(sub + 1) * 128], wsl[sl][:, k, :], k == 0, k == KC - 1,
                            [win_s.bufs[sl], hb], [ptb[pi]])
                    kt = key_off // 128 + sub
                    S.op("act", [ptb[pi]], [Vb], lambda e, pi=pi, kt=kt: e.copy(V[:, kt, :], pt[pi][:, :]), self_dep=False)
            for (col0, n, r, rope_off, qt) in p2_tiles(be):
                is_ctx = (qt is None)
                kit.emit(C, col0, n, r, 1, h, hb)
                for hd in range(16):
                    sset = 0 if hd < 8 else 1
                    kv = (hd % 8) // 4
                    sl = win_s.get()
                    pi = rot("pt")
                    for k in range(KC):
                        _mm(S, pt[pi][:, :n], wsl[sl][:, k, 0:128], h[:, k, :n], k == 0, k == KC - 1, [win_s.bufs[sl], hb], [ptb[pi]])
                    qi = rot("q")
                    qk_finish(pi, n, sset, rope_off, qb_[qi][:, :n], qbb[qi])
                    keys = []
                    if is_ctx:
                        keys = [(0, 0, n, []), (1, 0, n, [])]
                    elif sset == 0:
                        keys = [(kt, 0, n, []) for kt in range(NKT)]
                    else:
                        keys = [(0, 0, n, []), (1, 0, n, [])]
                        qb0 = qt * 4
                        for kb in range(max(0, qb0 - 1), min(15, qb0 + 4) + 1):
                            lo = max(kb - 1, qb0)
                            hi = min(kb + 1, qb0 + 3)
                            if lo > hi:
                                continue
                            mk = []
                            for qbk in range(lo, hi + 1):
                                if qbk == kb + 1:
                                    mk.append(((qbk - lo) * 128, 0))
                                elif qbk == kb - 1:
                                    mk.append(((qbk - lo) * 128, 1))
                            keys.append((2 + kb, (lo - qb0) * 128, (hi - qb0 + 1) * 128, mk))
                    for ki, (kt, c0, c1, mk) in enumerate(keys):
                        w_ = c1 - c0
                        a = rot("ps")
                        _mm(S, ps[a][:, :w_], KT[sset][:, kv, kt * 128:(kt + 1) * 128], qb_[qi][:, c0:c1], True, True,
                            [KTb[sset], qbb[qi]], [psb[a]])
                        S.op("act", [psb[a]], [pTb[a]],
                             lambda e, a=a, w_=w_: e.activation(out=pT[a][:, :w_], in_=ps[a][:, :w_], func=AF.Exp, scale=SCALE))
                        for (off, which) in mk:
                            S.op("pool", [pTb[a], cb], [pTb[a]],
                                 lambda e, a=a, off=off, which=which: e.tensor_tensor(pT[a][:, off:off + 128], pT[a][:, off:off + 128],
                                                                                   masks[:, which, :], ALU.mult))
                        vcol = sset * 256 + kv * 128
                        _mm(S, po[:, c0:c1], V[:, kt, vcol:vcol + 128], pT[a][:, :w_], ki == 0, ki == len(keys) - 1, [Vb, pTb[a]], [pob])
                        _mm(S, pd[:, c0:c1], ones1[:], pT[a][:, :w_], ki == 0, ki == len(keys) - 1, [cb, pTb[a]], [pdb])
                    if sset == 1:
                        S.op("dve", [pdb, cb], [rdenb],
                             lambda e, hd=hd: e.tensor_scalar(rden[:, :n], pd[:, :n], esink[:, hd - 8:hd - 7], None, ALU.add))
                        S.op("dve", [rdenb], [rdenb], lambda e: e.reciprocal(rden[:, :n], rden[:, :n]))
                    else:
                        S.op("dve", [pdb], [rdenb], lambda e: e.reciprocal(rden[:, :n], pd[:, :n]))
                    S.op("dve", [pob, rdenb], [OTb], lambda e, hd=hd: e.tensor_tensor(OT[:, hd, :n], po[:, :n], rden[:, :n], ALU.mult))
                for m in range(KC):
                    sl = wo_s.get()
                    pi = rot("pt")
                    for hd in range(16):
                        _mm(S, pt[pi][:, :n], wos[sl][:, hd, :], OT[:, hd, :n], hd == 0, hd == 15, [wo_s.bufs[sl], OTb], [ptb[pi]])
                    xi = rot("xres")
                    S.dma("pool", xres[xi][:, :n], stv[:, m, col0:col0 + n], [C.st_b], [xresb[xi]])
                    S.op("dve", [ptb[pi], xresb[xi], C.hg_b], [xresb[xi]],
                         lambda e, pi=pi, xi=xi, m=m: e.scalar_tensor_tensor(xres[xi][:, :n], pt[pi][:, :n], C.hg[:, 1, m, r:r + 1],
                                                                            xres[xi][:, :n], ALU.mult, ALU.add))
                    S.dma("pool", stv[:, m, col0:col0 + n], xres[xi][:, :n], [xresb[xi]], [C.st_b], partial=True)
        S.barrier()


def _hy_sets(l):
    sets = [(0, SEQ)]
    if l == 1:
        sets.append((1, CTX))
    return sets


def _hy_seq_info(C, nset, be):
    if nset == 0:
        return be, be * SEQ, be
    return NB + be, NLAT + be * CTX, NB


def _hy_dft(C, n):
    nch = n // 128
    t = {}
    for nm in ("cf", "sf", "si"):
        t[nm] = C.win(f"dft_{nm}{n}", [nch, 128, nch * 128], BF16)
    for nm in ("cr", "sr"):
        t[nm] = C.win(f"dft_{nm}{n}", [n, n], BF16)
    return t


def emit_hyena(C, l):
    nc, S = C.nc, C.S
    i = l // 2
    sets = _hy_sets(l)
    w_in = C.win(f"w_hy_in{i}", [D, 3 * D])
    w_out = C.win(f"w_hy_out{i}", [D, D])
    for k in range(KC):
        S.dma("bg", C.whinb[k * 128:(k + 1) * 128, :], w_in[k * 128:(k + 1) * 128, :], [], [C.whinb_b], partial=True)
    wo_v = w_out.rearrange("(j p) (m c) -> m p j c", p=128, c=128)
    for m in range(KC):
        S.dma("bg", C.waob[m].rearrange("p (j c) -> p j c", c=128), wo_v[m], [], [C.waob_b], partial=True)
    emit_hy_filter(C, l)
    emit_hy_inproj(C, l)
    emit_hy_conv(C, l)
    emit_hy_outproj(C, l)


def _dft_forward(C, S, fw, u, ub, n, psA, psAb, psB, psBb, consumer, need_a=True):
    nch = n // 128
    for fc in range(nch):
        sl = fw.get()
        a = fc % 2
        if need_a:
            for sc in range(nch):
                _mm(S, psA[a][:, :], fw.slots[sl][:, 0, sc * 128:(sc + 1) * 128], u[:, sc, :], sc == 0, sc == nch - 1,
                    [fw.bufs[sl], ub], [psAb[a]])
        for sc in range(nch):
            _mm(S, psB[a][:, :], fw.slots[sl][:, 1, sc * 128:(sc + 1) * 128], u[:, sc, :], sc == 0, sc == nch - 1,
                [fw.bufs[sl], ub], [psBb[a]])
        consumer(fc, a)


def emit_hy_filter(C, l):
    nc, S = C.nc, C.S
    i = l // 2
    w1_d = C.win(f"hf_w1_{i}", [33, 64])
    w2_d = C.win(f"hf_w2_{i}", [64, 64])
    w3_d = C.win(f"hf_w3_{i}", [64, 4 * D])
    hfv_d = C.win(f"hf_vec_{i}", [64, 4])
    bias_d = C.win(f"hy_bias_{i}", [128, 2, D])
    absd_d = C.win("hy_absdelta", [128, D])
    for (nset, n) in _hy_sets(l):
        nch = n // 128
        NN = 2 * n
        dft = _hy_dft(C, n)
        feats_d = C.win(f"hy_feats{n}", [33, n])
        negt_d = C.win(f"hy_negt{n}", [128, nch])
        wf_d = C.win(f"hy_wf{n}", [128, 2, nch])
        W = min(512, n)
        with ExitStack() as es:
            sbt = lambda nm, sh, dt_: es.enter_context(C.sbuf("hf_" + nm, sh, dt_))
            pst = lambda nm: es.enter_context(C.psum("hf_" + nm, [128, 512], F32))
            cb = Buf("hf_const")
            w1 = sbt("w1", [33, 64], F32)
            w2 = sbt("w2", [64, 64], F32)
            w3 = sbt("w3", [64, 2, 512], F32)
            w3b = Buf("w3")
            hfv = sbt("hfv", [64, 4], F32)
            sc = sbt("sc", [64, 4], F32)
            feats = sbt("feats", [33, n], F32)
            bias = sbt("bias", [128, 512], F32)
            absd = sbt("absd", [128, 512], F32)
            negt = sbt("negt", [128, nch], F32)
            wf = sbt("wf", [128, 2, nch], F32)
            ones1 = sbt("ones1", [128, 128], BF16)
            for dst, src in ((w1, w1_d), (w2, w2_d), (hfv, hfv_d), (feats, feats_d), (negt, negt_d), (wf, wf_d)):
                S.dma("sp", dst[:], src, [], [cb], partial=True)
            S.op("dve", [cb], [cb], lambda e: e.memset(ones1[:], 1.0))
            for j in range(2):
                S.op("dve", [cb], [cb], lambda e, j=j: e.tensor_scalar(sc[:, 2 * j:2 * j + 1], hfv[:, 2 * j + 1:2 * j + 2], 1.0 / 3.0, None, ALU.mult))
                S.op("dve", [cb], [cb], lambda e, j=j: e.tensor_tensor(sc[:, 2 * j + 1:2 * j + 2], sc[:, 2 * j:2 * j + 1], hfv[:, 2 * j:2 * j + 1], ALU.mult))
            h1 = sbt("h1", [64, n], F32)
            h1b = Buf("h1")
            h2 = sbt("h2", [64, n], F32)
            h2b = Buf("h2")
            s3 = sbt("s3", [64, W], F32)
            s3b = Buf("s3")
            tq = sbt("tq", [64, W], F32)
            tqb = Buf("tq")
            pz = pst("pz")
            pzb = Buf("pz")
            for (lhs, src, srcb, dst, dstb, j) in ((w1, feats, cb, h1, h1b, 0), (w2, h1, h1b, h2, h2b, 1)):
                for c0 in range(0, n, W):
                    _mm(S, pz[0:64, :W], lhs[:], src[:, c0:c0 + W], True, True, [cb, srcb], [pzb])
                    S.op("act", [pzb, cb], [s3b], lambda e, j=j: e.activation(out=s3[:], in_=pz[0:64, :W], func=AF.Sin,
                                                                              bias=sc[:, 2 * j + 1:2 * j + 2], scale=sc[:, 2 * j:2 * j + 1]))
                    S.op("dve", [s3b], [tqb], lambda e: e.tensor_tensor(tq[:], s3[:], s3[:], ALU.mult))
                    S.op("dve", [tqb], [tqb], lambda e: e.tensor_scalar(tq[:], tq[:], -4.0, 3.0, ALU.mult, ALU.add))
                    S.op("dve", [tqb, s3b], [dstb], lambda e, dst=dst, c0=c0: e.tensor_tensor(dst[:, c0:c0 + W], s3[:], tq[:], ALU.mult))
            kf = sbt("kf", [128, nch, 512], F32)
            kfb = Buf("kf")
            kb = sbt("kb", [128, nch, 512], F32)
            kbb = Buf("kb")
            eT = sbt("e", [128, nch, 512], BF16)
            eb = Buf("e")
            gT = sbt("g", [128, nch, 512], BF16)
            gb = Buf("g")
            kre2 = [sbt(f"kre{j}", [128, 512], F32) for j in range(2)]
            kre2b = [Buf(f"kre{j}") for j in range(2)]
            kim2 = [sbt(f"kim{j}", [128, 512], F32) for j in range(2)]
            kim2b = [Buf(f"kim{j}") for j in range(2)]
            krs = sbt("krs", [128, 512], F32)
            krsb = Buf("krs")
            dec = sbt("dec", [128, 512], F32)
            decb = Buf("dec")
            sq = [sbt(f"sq{j}", [128, 512], BF16) for j in range(2)]
            sqb = [Buf(f"sq{j}") for j in range(2)]
            scl = sbt("scl", [128, 512], F32)
            sclb = Buf("scl")
            tmp = sbt("tmp", [128, 512], F32)
            tmpb = Buf("tmp")
            slots = [sbt(f"dsl{j}", [128, 2, nch * 128], BF16) for j in range(2)]
            pk = [pst(f"pk{j}") for j in range(2)]
            pkb = [Buf(f"pk{j}") for j in range(2)]
            pn = pst("pn")
            pnb = Buf("pn")
            psA = [pst(f"pa{j}") for j in range(2)]
            psAb = [Buf(f"pa{j}") for j in range(2)]
            psB = [pst(f"pb{j}") for j in range(2)]
            psBb = [Buf(f"pb{j}") for j in range(2)]
            seq = []
            for o in range(2):
                for dt in range(4):
                    for rep in range(2):
                        for fc in range(nch):
                            seq.append([(dft["cf"][fc], lambda sl: sl[:, 0, :]), (dft["sf"][fc], lambda sl: sl[:, 1, :])])
            fw = WStream(S, slots, seq, Buf("dftc"))
            nsq = [0]
            for o in range(2):
                for dt in range(4):
                    dsl = slice(dt * 512, (dt + 1) * 512)
                    ob = Buf("hf_odt")
                    for dr in range(2):
                        c0 = o * 2 * D + dr * D + dt * 512
                        S.dma("sp", w3[:, dr, :], w3_d[:, c0:c0 + 512], [], [w3b], partial=(dr == 1))
                    S.dma("sp", bias[:], bias_d[:, o, dsl], [], [cb], partial=True)
                    S.dma("sp", absd[:], absd_d[:, dsl], [], [cb], partial=True)
                    for lc in range(nch):
                        S.op("act", [cb], [decb], lambda e, lc=lc: e.activation(out=dec[:], in_=absd[:], func=AF.Exp, scale=negt[:, lc:lc + 1]))
                        for dr, (kt_, ktb_) in enumerate(((kf, kfb), (kb, kbb))):
                            a = (2 * lc + dr) % 2
                            _mm(S, pk[a][:, :], h2[:, lc * 128:(lc + 1) * 128], w3[:, dr, :], True, True, [h2b, w3b], [pkb[a]])
                            S.op("dve", [pkb[a], decb], [ktb_], lambda e, kt_=kt_, a=a, lc=lc: e.tensor_tensor(kt_[:, lc, :], pk[a][:, :], dec[:], ALU.mult))
                    S.op("dve", [kbb], [kbb], lambda e: e.memset(kb[0:1, 0, :], 0.0))
                    tot = 2 * nch
                    cnt_ = 0
                    for (kt_, ktb_) in ((kf, kfb), (kb, kbb)):
                        for lc in range(nch):
                            j = nsq[0] % 2
                            nsq[0] += 1
                            S.op("act", [ktb_], [sqb[j]], lambda e, kt_=kt_, lc=lc, j=j: e.activation(out=sq[j][:], in_=kt_[:, lc, :], func=AF.Square))
                            _mm(S, pn[:, :], ones1[:], sq[j][:], cnt_ == 0, cnt_ == tot - 1, [sqb[j], cb], [pnb])
                            cnt_ += 1
                    S.op("act", [pnb], [sclb], lambda e: e.activation(out=scl[:], in_=pn[:, :], func=AF.Sqrt, bias=C.eps_sb[:, 0:1], scale=1.0))
                    S.op("dve", [sclb], [sclb], lambda e: e.reciprocal(scl[:], scl[:]))
                    for lc in range(nch):
                        S.op("dve", [kfb, kbb], [tmpb], lambda e, lc=lc: e.tensor_tensor(tmp[:], kf[:, lc, :], kb[:, lc, :], ALU.add))
                        S.op("dve", [tmpb, sclb], [eb], lambda e, lc=lc: e.tensor_tensor(eT[:, lc, :], tmp[:], scl[:], ALU.mult))
                        S.op("dve", [kfb, kbb], [tmpb], lambda e, lc=lc: e.tensor_tensor(tmp[:], kf[:, lc, :], kb[:, lc, :], ALU.subtract))
                        S.op("dve", [tmpb, sclb], [gb], lambda e, lc=lc: e.tensor_tensor(gT[:, lc, :], tmp[:], scl[:], ALU.mult))

                    ktb = C.ktab_b[nset]

                    def cons_e(fc, a, o=o, dt=dt):
                        j = fc % 2
                        S.op("dve", [psAb[a], cb], [tmpb], lambda e: e.tensor_tensor(tmp[:], psA[a][:, :], bias[:], ALU.add))
                        S.op("dve", [tmpb, cb], [kre2b[j]], lambda e: e.tensor_scalar(kre2[j][:], tmp[:], wf[:, 0, fc:fc + 1], None, ALU.mult))
                        if fc == 0:
                            S.op("dve", [kre2b[j]], [krsb], lambda e: e.tensor_copy(krs[:], kre2[j][:]))
                            S.op("dve", [psBb[a], cb], [tmpb], lambda e: e.tensor_tensor(tmp[0:1, :], psB[a][0:1, :], bias[0:1, :], ALU.add))
                            S.op("dve", [tmpb], [krsb], lambda e: e.tensor_scalar(krs[0:1, :], tmp[0:1, :], 1.0 / NN, None, ALU.mult))
                            S.dma("sp", C.krs0[nset, o, dt], krs[:], [krsb], [ktb], partial=True)
                        S.dma("sp", C.ktab[nset, o, 0, dt][:, fc * 512:(fc + 1) * 512], kre2[j][:], [kre2b[j]], [ktb], partial=True)

                    def cons_g(fc, a, o=o, dt=dt):
                        j = fc % 2
                        S.op("dve", [psBb[a], cb], [kim2b[j]], lambda e: e.tensor_scalar(kim2[j][:], psB[a][:, :], wf[:, 1, fc:fc + 1], None, ALU.mult))
                        if fc == 0:
                            S.op("dve", [kim2b[j]], [kim2b[j]], lambda e: e.memset(kim2[j][0:1, :], 0.0))
                        S.dma("sp", C.ktab[nset, o, 1, dt][:, fc * 512:(fc + 1) * 512], kim2[j][:], [kim2b[j]], [ktb], partial=True)

                    _dft_forward(C, S, fw, eT, eb, n, psA, psAb, psB, psBb, cons_e, need_a=True)
                    _dft_forward(C, S, fw, gT, gb, n, psA, psAb, psB, psBb, cons_g, need_a=False)
            S.barrier()


def emit_hy_inproj(C, l):
    nc, S = C.nc, C.S
    i = l // 2
    bin_d = C.win(f"hy_bin_{i}", [128, 48])
    wc_d = C.win(f"hy_wc_{i}", [128, 3, 48])
    bc_d = C.win(f"hy_bc_{i}", [128, 48])
    ident_d = C.win("ident_bf", [128, 128], BF16)
    whv = C.whinb.rearrange("(k p) n -> p k n", p=128)
    for (nset, n) in _hy_sets(l):
        nch = n // 128
        W = min(512, n)
        with ExitStack() as es:
            sbt = lambda nm, sh, dt_: es.enter_context(C.sbuf("hi_" + nm, sh, dt_))
            pst = lambda nm: es.enter_context(C.psum("hi_" + nm, [128, 512], F32))
            kit = NormKit(C, es, "hi")
            cb = Buf("hi_const")
            binv = sbt("bin", [128, 48], F32)
            wcv = sbt("wc", [128, 3, 48], F32)
            bcv = sbt("bc", [128, 48], F32)
            ident = sbt("ident", [128, 128], BF16)
            for dst, src in ((binv, bin_d), (wcv, wc_d), (bcv, bc_d), (ident, ident_d)):
                S.dma("sp", dst[:], src, [], [cb], partial=True)
            h = sbt("h", [128, KC, n], BF16)
            hb = Buf("h")
            pf = sbt("pf", [128, n + 2], F32)
            pfb = Buf("pf")
            pc = sbt("pc", [128, n], F32)
            pcb = Buf("pc")
            pcb16 = sbt("pc16", [128, n], BF16)
            pc16b = Buf("pc16")
            stg = sbt("stg", [128, nch, 512], BF16)
            stgb = Buf("stg")
            wsl = [sbt(f"w{j}", [128, KC, 128], BF16) for j in range(2)]
            pp = [pst(f"pp{j}") for j in range(2)]
            ppb = [Buf(f"pp{j}") for j in range(2)]
            ptr = [es.enter_context(C.psum(f"hi_ptr{j}", [128, 512], BF16)) for j in range(2)]
            ptrb = [Buf(f"ptr{j}") for j in range(2)]
            S.op("dve", [], [pfb], lambda e: e.memset(pf[:], 0.0))
            seq = []
            for be in C.be_list:
                for c in range(48):
                    seq.append((whv[:, :, c * 128:(c + 1) * 128], lambda sl: sl[:, :, :]))
            ws = WStream(S, wsl, seq, C.whinb_b)
            npp = [0]
            ntr = [0]
            for be in C.be_list:
                q, col0, r = _hy_seq_info(C, nset, be)
                for t0 in range(0, n, W):
                    kit.emit(C, col0 + t0, W, r, 1, h, hb, hoff=t0)
                for c in range(48):
                    sl = ws.get()
                    for t0 in range(0, n, W):
                        a = npp[0] % 2
                        npp[0] += 1
                        for k in range(KC):
                            _mm(S, pp[a][:, :W], wsl[sl][:, k, :], h[:, k, t0:t0 + W], k == 0, k == KC - 1, [ws.bufs[sl], hb], [ppb[a]])
                        S.op("act", [ppb[a], cb], [pfb], lambda e, a=a, t0=t0, c=c: e.activation(out=pf[:, 1 + t0:1 + t0 + W], in_=pp[a][:, :W],
                                                                                              func=AF.Identity, bias=binv[:, c:c + 1], scale=1.0),
                             self_dep=False)
                    S.op("act", [pfb, cb], [pcb], lambda e, c=c: e.activation(out=pc[:], in_=pf[:, 1:n + 1], func=AF.Identity,
                                                                             bias=bcv[:, c:c + 1], scale=wcv[:, 1, c:c + 1]))
                    S.op("dve", [pfb, pcb, cb], [pcb], lambda e, c=c: e.scalar_tensor_tensor(pc[:], pf[:, 0:n], wcv[:, 0, c:c + 1], pc[:], ALU.mult, ALU.add))
                    S.op("dve", [pfb, pcb, cb], [pc16b], lambda e, c=c: e.scalar_tensor_tensor(pcb16[:], pf[:, 2:n + 2], wcv[:, 2, c:c + 1], pc[:], ALU.mult, ALU.add))
                    if c >= 32:
                        S.dma("pool", C.hy_x2T[q, c - 32][:, :n], pcb16[:], [pc16b], [C.hy_x2T_b], partial=True)
                        continue
                    dcol = (c % 4) * 128
                    for g0 in range(0, nch, 4):
                        a = ntr[0] % 2
                        ntr[0] += 1
                        ng = min(4, nch - g0)
                        for gi in range(ng):
                            tcn = g0 + gi
                            S.op("pe", [pc16b, cb], [ptrb[a]], lambda e, a=a, gi=gi, tcn=tcn: e.transpose(ptr[a][:, gi * 128:(gi + 1) * 128],
                                                                                                      pcb16[:, tcn * 128:(tcn + 1) * 128], ident[:]))
                        S.op("act", [ptrb[a]], [stgb], lambda e, a=a, g0=g0, ng=ng, dcol=dcol: e.copy(
                            stg[:, g0:g0 + ng, dcol:dcol + 128], ptr[a][:, :ng * 128].rearrange("p (g c) -> p g c", c=128)), self_dep=False)
                    if c % 4 == 3:
                        dt = (c % 16) // 4
                        dst = C.hy_u if c < 16 else C.hy_x1
                        dstb = C.hy_u_b if c < 16 else C.hy_x1_b
                        S.dma("pool", dst[q, dt][:, :nch * 512].rearrange("p (c d) -> p c d", d=512), stg[:], [stgb], [dstb], partial=True)
            S.barrier()


def emit_hy_conv(C, l):
    nc, S = C.nc, C.S
    for (nset, n) in _hy_sets(l):
        nch = n // 128
        W = min(256, n)
        dft = _hy_dft(C, n)
        crv = dft["cr"].rearrange("(fk p) t -> p fk t", p=128)
        srv = dft["sr"].rearrange("(fk p) t -> p fk t", p=128)
        with ExitStack() as es:
            sbt = lambda nm, sh, dt_: es.enter_context(C.sbuf("hc_" + nm, sh, dt_))
            pst = lambda nm: es.enter_context(C.psum("hc_" + nm, [128, 512], F32))
            kre = sbt("kre", [128, nch, 512], F32)
            kim = sbt("kim", [128, nch, 512], F32)
            krs = sbt("krs", [128, 512], F32)
            ktb = Buf("hc_ktab")
            u = sbt("u", [128, nch, 512], BF16)
            ub = Buf("u")
            P = sbt("P", [128, nch, 512], BF16)
            Pb = Buf("P")
            Q = sbt("Q", [128, nch, 512], BF16)
            Qb = Buf("Q")
            t1 = sbt("t1", [128, 512], F32)
            t1b = Buf("t1")
            t2 = sbt("t2", [128, 512], F32)
            t2b = Buf("t2")
            t3 = sbt("t3", [128, 512], F32)
            t3b = Buf("t3")
            t4 = sbt("t4", [128, 512], F32)
            t4b = Buf("t4")
            xg = [sbt(f"xg{j}", [128, 512], BF16) for j in range(2)]
            xgb = [Buf(f"xg{j}") for j in range(2)]
            zo = [sbt(f"zo{j}", [128, 512], BF16) for j in range(2)]
            zob = [Buf(f"zo{j}") for j in range(2)]
            slots = [sbt(f"dsl{j}", [128, 2, nch * 128], BF16) for j in range(2)]
            wslots = [sbt(f"wsl{j}", [128, 2, nch, W], BF16) for j in range(2)]
            psA = [pst(f"pa{j}") for j in range(2)]
            psAb = [Buf(f"pa{j}") for j in range(2)]
            psB = [pst(f"pb{j}") for j in range(2)]
            psBb = [Buf(f"pb{j}") for j in range(2)]
            psY = [pst(f"py{j}") for j in range(2)]
            psYb = [Buf(f"py{j}") for j in range(2)]
            seqs = [_hy_seq_info(C, nset, be)[0] for be in C.be_list]
            seq = []
            wseq = []
            for dt in range(4):
                for o in range(2):
                    for q in seqs:
                        for fc in range(nch):
                            seq.append([(dft["cf"][fc], lambda sl: sl[:, 0, :]), (dft["sf"][fc], lambda sl: sl[:, 1, :])])
                        if o == 0:
                            for tc in range(nch):
                                seq.append([(dft["cf"][tc], lambda sl: sl[:, 0, :]), (dft["si"][tc], lambda sl: sl[:, 1, :])])
                        else:
                            for t0 in range(0, n, W):
                                wseq.append([(crv[:, :, t0:t0 + W], lambda sl: sl[:, 0, :, :]), (srv[:, :, t0:t0 + W], lambda sl: sl[:, 1, :, :])])
            fw = WStream(S, slots, seq, Buf("dftc"))
            ww = WStream(S, wslots, wseq, Buf("dftw"))
            cnt = {"xg": 0, "zo": 0, "py": 0}

            def rot(nm):
                v = cnt[nm] % 2
                cnt[nm] += 1
                return v

            for dt in range(4):
                for o in range(2):
                    S.dma("sp", kre[:], C.ktab[nset, o, 0, dt][:, :nch * 512].rearrange("p (c d) -> p c d", d=512), [C.ktab_b[nset]], [ktb])
                    S.dma("sp", kim[:], C.ktab[nset, o, 1, dt][:, :nch * 512].rearrange("p (c d) -> p c d", d=512), [C.ktab_b[nset]], [ktb], partial=True)
                    S.dma("sp", krs[:], C.krs0[nset, o, dt], [C.ktab_b[nset]], [ktb], partial=True)
                    for q in seqs:
                        usrc = C.hy_u[q, dt][:, :nch * 512].rearrange("p (c d) -> p c d", d=512)
                        S.dma("sp", u[:], usrc, [C.hy_u_b], [ub])

                        def cons(fc, a):
                            krx = krs[:] if fc == 0 else kre[:, fc, :]
                            S.op("dve", [psAb[a], ktb], [t1b], lambda e: e.tensor_tensor(t1[:], psA[a][:, :], kre[:, fc, :], ALU.mult))
                            S.op("dve", [psBb[a], ktb], [t2b], lambda e: e.tensor_tensor(t2[:], psB[a][:, :], kim[:, fc, :], ALU.mult))
                            S.op("pool", [t1b, t2b], [Pb], lambda e: e.tensor_tensor(P[:, fc, :], t1[:], t2[:], ALU.add))
                            S.op("dve", [psBb[a], ktb], [t3b], lambda e: e.tensor_tensor(t3[:], psB[a][:, :], krx, ALU.mult))
                            S.op("dve", [psAb[a], ktb], [t4b], lambda e: e.tensor_tensor(t4[:], psA[a][:, :], kim[:, fc, :], ALU.mult))
                            S.op("pool", [t3b, t4b], [Qb], lambda e: e.tensor_tensor(Q[:, fc, :], t3[:], t4[:], ALU.subtract))

                        _dft_forward(C, S, fw, u, ub, n, psA, psAb, psB, psBb, cons, need_a=True)
                        if o == 0:
                            for tc in range(nch):
                                sl = fw.get()
                                a = rot("py")
                                for fk in range(nch):
                                    _mm(S, psY[a][:, :], slots[sl][:, 0, fk * 128:(fk + 1) * 128], P[:, fk, :], fk == 0, False,
                                        [fw.bufs[sl], Pb], [psYb[a]])
                                for fk in range(nch):
                                    _mm(S, psY[a][:, :], slots[sl][:, 1, fk * 128:(fk + 1) * 128], Q[:, fk, :], False, fk == nch - 1,
                                        [fw.bufs[sl], Qb], [psYb[a]])
                                xi = rot("xg")
                                S.dma("pool", xg[xi][:], C.hy_x1[q, dt][:, tc * 512:(tc + 1) * 512], [C.hy_x1_b], [xgb[xi]])
                                zi = rot("zo")
                                S.op("dve", [psYb[a], xgb[xi]], [zob[zi]], lambda e: e.tensor_tensor(zo[zi][:], psY[a][:, :], xg[xi][:], ALU.mult))
                                S.dma("pool", C.hy_u[q, dt][:, tc * 512:(tc + 1) * 512], zo[zi][:], [zob[zi]], [C.hy_u_b], partial=True)
                        else:
                            for t0 in range(0, n, W):
                                sl = ww.get()
                                for dc in range(4):
                                    a = rot("py")
                                    for fk in range(nch):
                                        _mm(S, psY[a][:, :W], P[:, fk, dc * 128:(dc + 1) * 128], wslots[sl][:, 0, fk, :], fk == 0, False,
                                            [ww.bufs[sl], Pb], [psYb[a]])
                                    for fk in range(nch):
                                        _mm(S, psY[a][:, :W], Q[:, fk, dc * 128:(dc + 1) * 128], wslots[sl][:, 1, fk, :], False, fk == nch - 1,
                                            [ww.bufs[sl], Qb], [psYb[a]])
                                    xi = rot("xg")
                                    S.dma("pool", xg[xi][:, :W], C.hy_x2T[q, dt * 4 + dc][:, t0:t0 + W], [C.hy_x2T_b], [xgb[xi]])
                                    zi = rot("zo")
                                    S.op("dve", [psYb[a], xgb[xi]], [zob[zi]], lambda e: e.tensor_tensor(zo[zi][:, :W], psY[a][:, :W], xg[xi][:, :W], ALU.mult))
                                    S.dma("pool", C.hy_z2T[q, dt * 4 + dc][:, t0:t0 + W], zo[zi][:, :W], [zob[zi]], [C.hy_z2T_b], partial=True)
            S.barrier()


def emit_hy_outproj(C, l):
    nc, S = C.nc, C.S
    i = l // 2
    bo_d = C.win(f"hy_bout_{i}", [128, KC])
    stv = C.st.rearrange("(k p) t -> p k t", p=128)
    with ExitStack() as es:
        sbt = lambda nm, sh, dt_: es.enter_context(C.sbuf("ho_" + nm, sh, dt_))
        pst = lambda nm: es.enter_context(C.psum("ho_" + nm, [128, 512], F32))
        bo = sbt("bo", [128, KC], F32)
        cb = Buf("ho_const")
        S.dma("sp", bo[:], bo_d, [], [cb])
        z = sbt("z", [128, KC, TT], BF16)
        zb = Buf("z")
        wos = [sbt(f"wo{j}", [128, KC, 128], BF16) for j in range(2)]
        xres = [sbt(f"xres{j}", [128, TT], F32) for j in range(2)]
        xresb = [Buf(f"xres{j}") for j in range(2)]
        yb_ = sbt("y", [128, TT], F32)
        ybb = Buf("y")
        pt = [pst(f"pt{j}") for j in range(2)]
        ptb = [Buf(f"pt{j}") for j in range(2)]
        tiles = []
        for (nset, n) in _hy_sets(l):
            W = min(TT, n)
            for be in C.be_list:
                q, col0, r = _hy_seq_info(C, nset, be)
                for t0 in range(0, n, W):
                    tiles.append((q, col0, r, t0, W))
        seq = []
        for _ in tiles:
            for m in range(KC):
                seq.append((C.waob[m].rearrange("p (j c) -> p j c", c=128), lambda sl: sl[:, :, :]))
        ws = WStream(S, wos, seq, C.waob_b)
        n_ = [0]
        for (q, col0, r, t0, W) in tiles:
            S.dma("sp", z[:, :, :W], C.hy_z2T[q][:, :, t0:t0 + W].rearrange("k p t -> p k t"), [C.hy_z2T_b], [zb])
            for m in range(KC):
                sl = ws.get()
                a = n_[0] % 2
                n_[0] += 1
                for k in range(KC):
                    _mm(S, pt[a][:, :W], wos[sl][:, k, :], z[:, k, :W], k == 0, k == KC - 1, [ws.bufs[sl], zb], [ptb[a]])
                S.dma("pool", xres[a][:, :W], stv[:, m, col0 + t0:col0 + t0 + W], [C.st_b], [xresb[a]])
                S.op("act", [ptb[a], cb], [ybb], lambda e: e.activation(out=yb_[:, :W], in_=pt[a][:, :W], func=AF.Identity, bias=bo[:, m:m + 1], scale=1.0))
                S.op("dve", [ybb, xresb[a], C.hg_b], [xresb[a]],
                     lambda e: e.scalar_tensor_tensor(xres[a][:, :W], yb_[:, :W], C.hg[:, 1, m, r:r + 1], xres[a][:, :W], ALU.mult, ALU.add))
                S.dma("pool", stv[:, m, col0 + t0:col0 + t0 + W], xres[a][:, :W], [xresb[a]], [C.st_b], partial=True)
        S.barrier()


def build_program(phases, dbg=False, be_list=None):
    nc = bass.Bass("TRN2", target_bir_lowering=False)
    C = Ctx()
    C.nc = nc
    C.debug = dbg
    C.be_list = list(range(NB)) if be_list is None else be_list
    dt = nc.dram_tensor
    C.xin = dt("xin", [D, NTOK], F32, kind="ExternalInput").ap()
    C.cT = dt("cT", [128, KC, NR], F32, kind="ExternalInput").ap()
    C.st = dt("st", [D, NTOK], F32, kind="ExternalOutput").ap()
    C.winb = [dt(f"winb{i}", [D, 2 * DFF], BF16, kind="Internal").ap() for i in range(2)]
    C.woutb = [dt(f"woutb{i}", [KC, 128, DFF], BF16, kind="Internal").ap() for i in range(2)]
    C.winb_b = [Buf(f"winb{i}") for i in range(2)]
    C.woutb_b = [Buf(f"woutb{i}") for i in range(2)]
    C.st_b = Buf("st")
    C.wainb = dt("wainb", [D, 3072], BF16, kind="Internal").ap()
    C.waob = dt("waob", [KC, 128, D], BF16, kind="Internal").ap()
    C.whinb = dt("whinb", [D, 3 * D], BF16, kind="Internal").ap()
    C.whinb_b = Buf("whinb")
    NQ = 2 * NB
    C.hy_u = dt("hy_u", [NQ, 4, 128, KC * 512], BF16, kind="Internal").ap()
    C.hy_x1 = dt("hy_x1", [NQ, 4, 128, KC * 512], BF16, kind="Internal").ap()
    C.hy_x2T = dt("hy_x2T", [NQ, KC, 128, SEQ], BF16, kind="Internal").ap()
    C.hy_z2T = dt("hy_z2T", [NQ, KC, 128, SEQ], BF16, kind="Internal").ap()
    C.ktab = dt("hy_ktab", [2, 2, 2, 4, 128, KC * 512], F32, kind="Internal").ap()
    C.krs0 = dt("hy_krs0", [2, 2, 4, 128, 512], F32, kind="Internal").ap()
    C.hy_u_b = Buf("hy_u")
    C.hy_x1_b = Buf("hy_x1")
    C.hy_x2T_b = Buf("hy_x2T")
    C.hy_z2T_b = Buf("hy_z2T")
    C.ktab_b = [Buf("ktab0"), Buf("ktab1")]
    C.wainb_b = Buf("wainb")
    C.waob_b = Buf("waob")

    with ExitStack() as es:
        nc.allow_low_precision("bf16 matmuls with fp32 accumulation")
        S = Sched(nc, es)
        C.S = S
        sb = lambda name, shape, dtype: es.enter_context(C.sbuf(name, shape, dtype))
        C.ones_bf = sb("ones_bf", [128, 128], BF16)
        C.const_b = Buf("const")
        C.s_sb = sb("s_sb", [128, KC, NR], F32)
        C.s_b = Buf("s")
        C.modv = sb("modv", [128, NMOD * KC, NR], F32)
        C.modv_b = Buf("modv")
        C.gs = sb("gs", [128, 3, KC, NR], F32)
        C.gs_b = Buf("gs")
        C.hg = sb("hg", [128, 3, KC, NR], F32)
        C.hg_b = Buf("hg")

        S.op("dve", [], [C.const_b], lambda e: e.memset(C.ones_bf[:], 1.0 / D))
        C.eps_sb = sb("eps_sb", [128, 2], F32)
        S.op("dve", [], [C.const_b], lambda e: e.memset(C.eps_sb[:], EPS))
        S.dma("sp", C.s_sb[:], C.cT, [], [C.s_b])
        S.op("act", [C.s_b], [C.s_b], lambda e: e.activation(out=C.s_sb[:], in_=C.s_sb[:], func=AF.Silu))
        for k in range(KC):
            S.dma("pool", C.st[k * 128:(k + 1) * 128, :], C.xin[k * 128:(k + 1) * 128, :], [], [C.st_b], partial=True)

        lat_tiles = [(t * TT, t // (SEQ // TT)) for t in range(NLAT // TT)]
        ctx_tile = [(NLAT + i * TT, NB) for i in range(NB * CTX // TT)]
        for ph in phases:
            kind = ph[0]
            if kind == "conv":
                emit_convert_ffn(C, ph[1], ph[2], ph[3])
            elif kind == "mod":
                emit_mod(C, ph[1])
            elif kind == "hyena":
                emit_hyena(C, ph[1])
            elif kind == "attn":
                emit_attn(C, ph[1])
            elif kind == "ffn":
                _, l, w, slot, with_ctx, ntl = ph
                tiles = lat_tiles[:ntl] + (ctx_tile if with_ctx else [])
                emit_ffn(C, l, w, slot, tiles)
        S.barrier(include_bg=True)
        C.n_ops = S.nops
    return nc, C


def _const_tables():
    import ml_dtypes
    f32 = np.float32
    t = {}
    rows = SEQ // 64
    r, col = np.meshgrid(np.arange(rows), np.arange(64), indexing="ij")
    r = r.reshape(-1).astype(f32)
    col = col.reshape(-1).astype(f32)
    half = HD // 2
    inv = (f32(10000.0) ** (-np.arange(0, half, 2, dtype=f32) / f32(half))).astype(f32)
    ang = np.concatenate([r[:, None] * inv, col[:, None] * inv], axis=-1).astype(f32)
    cs = np.stack([np.cos(ang), np.sin(ang)]).astype(f32)
    cs = np.concatenate([cs, cs], axis=-1)
    t["rope_cs"] = np.ascontiguousarray(cs.transpose(0, 2, 1))
    rm = np.zeros((128, 128), f32)
    for m in range(64):
        rm[m + 64, m] = -1.0
        rm[m, m + 64] = 1.0
    t["rotm"] = rm.astype(ml_dtypes.bfloat16)
    kj = np.arange(128)[:, None]
    qi = np.arange(128)[None, :]
    t["wmask"] = np.ascontiguousarray(np.stack([(kj >= qi), (kj <= qi)], axis=1).astype(f32)).astype(ml_dtypes.bfloat16)
    return t


_HY_CACHE = {}


def _hyena_tables(n):
    if n in _HY_CACHE:
        return _HY_CACHE[n]
    import ml_dtypes
    f32 = np.float32
    bf = ml_dtypes.bfloat16
    t = {}
    nch = n // 128
    N = 2 * n
    tt = np.linspace(0.0, 1.0, n, dtype=f32)
    w = (f32(2.0 * math.pi / n) * np.arange(n, dtype=f32))[:, None]
    bands = np.linspace(1e-4, 15, 16, dtype=f32)
    ang = (w * bands[None, :]).astype(f32)
    feats = np.concatenate([tt[:, None], np.cos(ang), -np.sin(ang)], axis=-1).astype(f32)
    t[f"hy_feats{n}"] = np.ascontiguousarray(feats.T)
    t[f"hy_negt{n}"] = np.ascontiguousarray((-tt).reshape(nch, 128).T)
    wfv = np.full(n, 2.0 / N, f32)
    wfv[0] = 1.0 / N
    wf = wfv.reshape(nch, 128).T
    t[f"hy_wf{n}"] = np.ascontiguousarray(np.stack([wf, -wf], axis=1))
    max_decay = math.log(1e-2) / 0.3
    min_decay = math.log(1e-2) / 1.5
    deltas = np.abs(np.linspace(min_decay, max_decay, D, dtype=f32))
    t["hy_absdelta"] = np.ascontiguousarray(np.broadcast_to(deltas[None, :], (128, D)))
    idx = np.arange(n, dtype=np.int64)
    prod = (idx[:, None] * idx[None, :]) % N
    angm = prod.astype(np.float64) * (2.0 * math.pi / N)
    Cm = np.cos(angm)
    Sm = np.sin(angm)
    Sm[0, :] = np.where(idx % 2 == 0, 1.0, -1.0)
    Cb = Cm.astype(f32).astype(bf)
    Sb = Sm.astype(f32).astype(bf)
    C4 = Cb.reshape(nch, 128, nch, 128)
    S4 = Sb.reshape(nch, 128, nch, 128)
    t[f"dft_cf{n}"] = np.ascontiguousarray(C4.transpose(2, 1, 0, 3)).reshape(nch, 128, nch * 128)
    t[f"dft_sf{n}"] = np.ascontiguousarray(S4.transpose(0, 3, 2, 1)).reshape(nch, 128, nch * 128)
    t[f"dft_si{n}"] = np.ascontiguousarray(S4.transpose(2, 1, 0, 3)).reshape(nch, 128, nch * 128)
    t[f"dft_cr{n}"] = Cb
    t[f"dft_sr{n}"] = Sb
    t["ident_bf"] = np.eye(128, dtype=f32).astype(bf)
    _HY_CACHE[n] = t
    return t


def _prep_inputs(inputs, layers=range(DEPTH)):
    x = inputs["x"]
    ctx = inputs["ctx"]
    c = inputs["c"]
    c_ctx = inputs["c_ctx"]
    shared = {}
    for l in layers:
        shared[f"w_mod{l}"] = inputs["w_mod"][l]
        shared[f"b_mod{l}"] = np.ascontiguousarray(inputs["b_mod"][l].reshape(NMOD * KC, 128).T)
        shared[f"g_norm{l}"] = np.ascontiguousarray(inputs["g_norm"][l].reshape(3, KC, 128).transpose(2, 0, 1))
        for w in range(2):
            shared[f"w_ffn_in{l}_{w}"] = inputs["w_ffn_in"][l, w]
            shared[f"w_ffn_out{l}_{w}"] = inputs["w_ffn_out"][l, w]
    perm = np.concatenate([np.arange(0, HD, 2), np.arange(1, HD, 2)])
    for l in layers:
        if l % 2 == 0:
            i = l // 2
            wi = inputs["w_attn_in"][i]
            qcols = wi[:, :2048].reshape(D, 16, HD)[:, :, perm].reshape(D, 2048)
            kA = wi[:, 2048:2304].reshape(D, 2, HD)[:, :, perm].reshape(D, 256)
            vA = wi[:, 2304:2560]
            kB = wi[:, 2560:2816].reshape(D, 2, HD)[:, :, perm].reshape(D, 256)
            vB = wi[:, 2816:3072]
            shared[f"w_attn_in{i}"] = np.ascontiguousarray(np.concatenate([qcols, kA, kB, vA, vB], axis=1))
            shared[f"w_attn_out{i}"] = inputs["w_attn_out"][i]
            gq = inputs["g_q"][i][:, perm]
            gk = inputs["g_k"][i][:, perm]
            shared[f"gqk{i}"] = np.ascontiguousarray(np.stack([gq[0], gq[1], gk[0], gk[1]], axis=1))
            shared[f"sink{i}"] = np.ascontiguousarray(np.broadcast_to(inputs["sink"][i][None, :], (128, 8)))
    for l in layers:
        if l % 2 == 1:
            i = l // 2
            shared[f"w_hy_in{i}"] = inputs["w_hy_in"][i]
            shared[f"w_hy_out{i}"] = inputs["w_hy_out"][i]
            shared[f"hf_w1_{i}"] = inputs["hf_w1"][i]
            shared[f"hf_w2_{i}"] = inputs["hf_w2"][i]
            shared[f"hf_w3_{i}"] = inputs["hf_w3"][i]
            shared[f"hf_vec_{i}"] = np.ascontiguousarray(np.stack(
                [inputs["hf_b1"][i], inputs["hf_freq1"][i], inputs["hf_b2"][i], inputs["hf_freq2"][i]], axis=1))
            shared[f"hy_bias_{i}"] = np.ascontiguousarray(np.broadcast_to(inputs["hy_bias"][i][None], (128, 2, D)))
            shared[f"hy_bin_{i}"] = np.ascontiguousarray(inputs["b_hy_in"][i].reshape(48, 128).T)
            shared[f"hy_wc_{i}"] = np.ascontiguousarray(inputs["w_hy_conv"][i].reshape(3, 48, 128).transpose(2, 0, 1))
            shared[f"hy_bc_{i}"] = np.ascontiguousarray(inputs["b_hy_conv"][i].reshape(48, 128).T)
            shared[f"hy_bout_{i}"] = np.ascontiguousarray(inputs["b_hy_out"][i].reshape(KC, 128).T)
    shared.update(_const_tables())
    shared.update(_hyena_tables(SEQ))
    if 1 in layers:
        shared.update(_hyena_tables(CTX))
    maps = []
    for core in range(NCORES):
        b0 = core * NB
        xin = np.empty((D, NTOK), np.float32)
        for be in range(NB):
            xin[:, be * SEQ:(be + 1) * SEQ] = x[b0 + be].T
            xin[:, NLAT + be * CTX:NLAT + (be + 1) * CTX] = ctx[b0 + be].T
        cT = np.zeros((128, KC, NR), np.float32)
        for be in range(NB):
            cT[:, :, be] = c[b0 + be].reshape(KC, 128).T
        cT[:, :, NB] = c_ctx.reshape(KC, 128).T
        m = dict(shared)
        m["xin"] = xin
        m["cT"] = cT
        maps.append(m)
    return maps


def full_phases(ntl=NB * 4):
    ph = []
    ph.append(("conv", 0, 0, 0))
    for l in range(DEPTH):
        ph.append(("mod", l))
        ph.append(("conv", l, 1, 1))
        ph.append(("ffn", l, 0, 0, l <= 2, ntl))
        if l + 1 < DEPTH:
            ph.append(("conv", l + 1, 0, 0))
        ph.append(("attn" if l % 2 == 0 else "hyena", l))
        ph.append(("ffn", l, 1, 1, l < 2, ntl))
    return ph


def kernel(**inputs):
    maps = _prep_inputs(inputs)
    nc, C = build_program(full_phases())
    names = list(C.decl.keys()) + ["xin", "cT"]
    res = run_bass_kernel_spmd(nc, [{k: m[k] for k in names} for m in maps], core_ids=list(range(NCORES)))
    out = np.empty((16, SEQ, D), np.float32)
    for core in range(NCORES):
        st = res.results[core]["st"]
        for be in range(NB):
            out[core * NB + be] = st[:, be * SEQ:(be + 1) * SEQ].T
    return out
```

```python
import math
from contextlib import ExitStack
import numpy as np
import concourse.bass as bass
import concourse.mybir as mybir
from concourse.bass_utils import run_bass_kernel_spmd

F32 = mybir.dt.float32
BF16 = mybir.dt.bfloat16
ALU = mybir.AluOpType
AF = mybir.ActivationFunctionType
AX = mybir.AxisListType

D = 2048
KC = D // 128
SEQ = 2048
CTX = 256
NCORES = 8
NB = 16 // NCORES
NR = 8
NLAT = NB * SEQ
NTOK = NLAT + NB * CTX
DFF = 5632
FC = DFF // 128
NMOD = 9
DEPTH = 4
HD = 128
EPS = 1e-6
TT = 512


class Buf:
    __slots__ = ("name", "w", "r")

    def __init__(self, name):
        self.name = name
        self.w = {}
        self.r = {}


class Sched:
    def __init__(self, nc, es, ndma=12):
        self.nc = nc
        self.E = {"pe": nc.tensor, "act": nc.scalar, "dve": nc.vector, "pool": nc.gpsimd, "sp": nc.sync}
        self.sems = {}
        self.cnt = {}
        self.waited = {e: {} for e in self.E}
        self.es = es
        for e in ("pe", "act", "dve", "pool"):
            self._mk("c_" + e)
        self.dq = {}
        for q in ("sp", "pool", "bg", "act"):
            self.dq[q] = [0, [self._mk(f"d_{q}_{i}") for i in range(ndma)]]
        self.nops = 0

    def _mk(self, name):
        self.sems[name] = self.es.enter_context(self.nc.semaphore(name))
        self.cnt[name] = 0
        return name

    def _deps(self, reads, writes):
        deps = {}
        for b in reads:
            for s, v in b.w.items():
                if deps.get(s, 0) < v:
                    deps[s] = v
        for b in writes:
            for s, v in b.w.items():
                if deps.get(s, 0) < v:
                    deps[s] = v
            for s, v in b.r.items():
                if deps.get(s, 0) < v:
                    deps[s] = v
        return deps

    def _wait(self, e, deps, skip=None):
        eng = self.E[e]
        wd = self.waited[e]
        for s, v in deps.items():
            if s == skip:
                continue
            if wd.get(s, 0) < v:
                eng.wait_ge(self.sems[s], v)
                wd[s] = v

    def op(self, e, reads, writes, fn, self_dep=True):
        own = "c_" + e
        deps = self._deps(reads, writes)
        self._wait(e, deps, skip=own if (e == "pe" or not self_dep) else None)
        inst = fn(self.E[e])
        self.cnt[own] += 1
        v = self.cnt[own]
        inst.then_inc(self.sems[own], 1)
        for b in reads:
            b.r[own] = v
        for b in writes:
            b.w = {own: v}
            b.r = {}
        self.nops += 1
        return inst

    def dma(self, q, out, in_, reads, writes, partial=False):
        e = q if q in ("sp", "act") else "pool"
        st = self.dq[q]
        name = st[1][st[0] % len(st[1])]
        st[0] += 1
        deps = self._deps(reads, writes)
        prev = self.cnt[name]
        if prev and deps.get(name, 0) < prev:
            deps[name] = prev
        self._wait(e, deps)
        inst = self.E[e].dma_start(out=out, in_=in_)
        self.cnt[name] += 16
        v = self.cnt[name]
        inst.then_inc(self.sems[name], 16)
        for b in reads:
            b.r[name] = v
        for b in writes:
            if partial:
                b.w[name] = v
            else:
                b.w = {name: v}
                b.r = {}
        self.nops += 1
        return inst

    def barrier(self, include_bg=False):
        deps = {}
        for s, v in self.cnt.items():
            if v and (include_bg or not s.startswith("d_bg")):
                deps[s] = v
        for e in self.E:
            self._wait(e, deps)


class Ctx:
    def __init__(self):
        self.decl = {}
        self.dbg = {}
        self.uid = 0

    def sbuf(self, name, shape, dtype):
        self.uid += 1
        return self.nc.sbuf_tensor(f"{name}_u{self.uid}", shape, dtype)

    def psum(self, name, shape, dtype):
        self.uid += 1
        return self.nc.psum_tensor(f"{name}_u{self.uid}", shape, dtype)

    def win(self, name, shape, dtype=F32):
        if name not in self.decl:
            self.decl[name] = self.nc.dram_tensor(name, list(shape), dtype, kind="ExternalInput").ap()
        return self.decl[name]

    def dump(self, name, src_ap, shape, dtype, reads):
        if not self.debug:
            return
        t = self.nc.dram_tensor("dbg_" + name, list(shape), dtype, kind="ExternalOutput").ap()
        self.dbg[name] = t
        self.S.dma("sp", t, src_ap, reads, [Buf("dbg_" + name)])


def _mm(S, out_ap, lhsT, rhs, start, stop, reads, writes):
    return S.op("pe", reads, writes, lambda e: e.matmul(out_ap, lhsT, rhs, start=start, stop=stop))


def emit_mod(C, l):
    nc, S = C.nc, C.S
    NCH = NMOD * KC
    GRP = 4
    NG = NCH // GRP
    with ExitStack() as es:
        wsl = [es.enter_context(C.sbuf(f"modw{i}", [128, KC, GRP * 128], F32)) for i in range(2)]
        wb = [Buf(f"modw{i}") for i in range(2)]
        ps = es.enter_context(C.psum("modps", [128, 3, 512], F32))
        psb = Buf("modps")
        bm = es.enter_context(C.sbuf("modb", [128, NCH], F32))
        bmb = Buf("modb")
        gn = es.enter_context(C.sbuf("modg", [128, 3, KC], F32))
        gnb = Buf("modg")
        S.dma("sp", bm[:], C.win(f"b_mod{l}", [128, NMOD * KC]), [], [bmb])
        S.dma("sp", gn[:], C.win(f"g_norm{l}", [128, 3, KC]), [], [gnb])
        wsrc = C.win(f"w_mod{l}", [D, NMOD * D]).rearrange("(k p) n -> p k n", p=128)

        def load(g):
            S.dma("sp", wsl[g % 2][:], wsrc[:, :, g * 512:(g + 1) * 512], [], [wb[g % 2]])

        load(0)
        for g in range(NG):
            if g + 1 < NG:
                load(g + 1)
            for jj in range(GRP):
                n = g * GRP + jj
                o = ps[:, n // 64, (n % 64) * NR:(n % 64) * NR + NR]
                for k in range(KC):
                    _mm(S, o, wsl[g % 2][:, k, jj * 128:(jj + 1) * 128], C.s_sb[:, k, :], k == 0, k == KC - 1,
                        [wb[g % 2], C.s_b], [psb])
        for r in range(NB + 1):
            for hb in range(3):
                n0 = hb * 64
                cn = min(64, NCH - n0)
                S.op("dve", [psb, bmb], [C.modv_b],
                     lambda e, r=r, hb=hb, n0=n0, cn=cn: e.tensor_tensor(
                         C.modv[:, n0:n0 + cn, r],
                         ps[:, hb, 0:cn * NR].rearrange("p (n r) -> p n r", r=NR)[:, :, r],
                         bm[:, n0:n0 + cn], ALU.add), self_dep=False)
        for k in range(3):
            for r in range(NB + 1):
                S.op("dve", [C.modv_b, gnb], [C.gs_b],
                     lambda e, k=k, r=r: e.scalar_tensor_tensor(
                         C.gs[:, k, :, r], C.modv[:, (3 * k + 1) * KC:(3 * k + 2) * KC, r], 1.0, gn[:, k, :],
                         ALU.add, ALU.mult), self_dep=(k == 0 and r == 0))
            S.op("dve", [C.modv_b], [C.hg_b],
                 lambda e, k=k: e.tensor_scalar(
                     C.hg[:, k, :, :], C.modv[:, (3 * k + 2) * KC:(3 * k + 3) * KC, :],
                     0.5 if k != 1 else 1.0, None, ALU.mult), self_dep=(k == 0))
        C.dump(f"modv{l}", C.modv[:], [128, NMOD * KC, NR], F32, [C.modv_b])
        C.dump(f"gs{l}", C.gs[:], [128, 3, KC, NR], F32, [C.gs_b])
        C.dump(f"hg{l}", C.hg[:], [128, 3, KC, NR], F32, [C.hg_b])
        S.barrier()


def emit_convert_ffn(C, l, w, slot):
    S = C.S
    wi = C.win(f"w_ffn_in{l}_{w}", [D, 2 * DFF])
    wo = C.win(f"w_ffn_out{l}_{w}", [DFF, D])
    for k in range(KC):
        S.dma("bg", C.winb[slot][k * 128:(k + 1) * 128, :], wi[k * 128:(k + 1) * 128, :], [], [C.winb_b[slot]],
              partial=True)
    wo_v = wo.rearrange("(j p) (m c) -> m p j c", p=128, c=128)
    for m in range(KC):
        S.dma("bg", C.woutb[slot][m].rearrange("p (j c) -> p j c", c=128), wo_v[m], [], [C.woutb_b[slot]], partial=True)


def emit_ffn(C, l, w, slot, tiles):
    nc, S = C.nc, C.S
    k_mod = 0 if w == 0 else 2
    JG = 4
    NJG = FC // JG
    winv = C.winb[slot].rearrange("(k p) n -> p k n", p=128)
    with ExitStack() as es:
        x = es.enter_context(C.sbuf("ffn_x", [128, KC, TT], F32))
        xb = [Buf(f"ffn_x{k}") for k in range(KC)]
        h = es.enter_context(C.sbuf("ffn_h", [128, KC, TT], BF16))
        hb = Buf("ffn_h")
        a = es.enter_context(C.sbuf("ffn_a", [128, FC, TT], BF16))
        ab = Buf("ffn_a")
        wins = [es.enter_context(C.sbuf(f"ffn_wi{i}", [128, 2, KC, JG * 128], BF16)) for i in range(2)]
        winb = [Buf(f"ffn_wi{i}") for i in range(2)]
        wouts = [es.enter_context(C.sbuf(f"ffn_wo{i}", [128, FC, 128], BF16)) for i in range(2)]
        woutb = [Buf(f"ffn_wo{i}") for i in range(2)]
        sq = [es.enter_context(C.sbuf(f"ffn_sq{i}", [128, TT], BF16)) for i in range(2)]
        sqb = [Buf(f"ffn_sq{i}") for i in range(2)]
        tmp = [es.enter_context(C.sbuf(f"ffn_tmp{i}", [128, TT], F32)) for i in range(2)]
        tmpb = [Buf(f"ffn_tmp{i}") for i in range(2)]
        sg = [es.enter_context(C.sbuf(f"ffn_sg{i}", [128, TT], F32)) for i in range(2)]
        sgb = [Buf(f"ffn_sg{i}") for i in range(2)]
        rstd = es.enter_context(C.sbuf("ffn_rstd", [128, TT], F32))
        rstdb = Buf("ffn_rstd")
        psn = es.enter_context(C.psum("ffn_psn", [128, TT], F32))
        psnb = Buf("ffn_psn")
        psg = [es.enter_context(C.psum(f"ffn_psg{i}", [128, TT], F32)) for i in range(2)]
        psgb = [Buf(f"ffn_psg{i}") for i in range(2)]
        psu = [es.enter_context(C.psum(f"ffn_psu{i}", [128, TT], F32)) for i in range(2)]
        psub = [Buf(f"ffn_psu{i}") for i in range(2)]
        psy = [es.enter_context(C.psum(f"ffn_psy{i}", [128, TT], F32)) for i in range(2)]
        psyb = [Buf(f"ffn_psy{i}") for i in range(2)]

        stv = C.st.rearrange("(k p) t -> p k t", p=128)

        tasks = []
        for ti in range(len(tiles)):
            for jg in range(NJG):
                tasks.append(("in", jg))
            for m in range(KC):
                tasks.append(("out", m))
        issued = [0]
        cnts = {"in": 0, "out": 0}
        slot_of = {}

        def ensure(upto):
            while issued[0] <= min(upto, len(tasks) - 1):
                i = issued[0]
                kind, idx = tasks[i]
                sl = cnts[kind] % 2
                cnts[kind] += 1
                slot_of[i] = sl
                if kind == "in":
                    for gu in range(2):
                        c0 = gu * DFF + idx * JG * 128
                        S.dma("sp", wins[sl][:, gu, :, :], winv[:, :, c0:c0 + JG * 128], [C.winb_b[slot]], [winb[sl]],
                              partial=(gu == 1))
                else:
                    S.dma("sp", wouts[sl][:], C.woutb[slot][idx].rearrange("p (j c) -> p j c", c=128),
                          [C.woutb_b[slot]], [woutb[sl]])
                issued[0] += 1

        def load_x(ti, k):
            col0 = tiles[ti][0]
            S.dma("act", x[:, k, :], stv[:, k, col0:col0 + TT], [C.st_b], [xb[k]])

        for k in range(KC):
            load_x(0, k)
        ensure(1)
        tcount = 0
        nsq = 0
        ntmp = 0
        nps = 0
        npy = 0
        for ti, (col0, r) in enumerate(tiles):
            for k in range(KC):
                i = nsq % 2
                nsq += 1
                S.op("act", [xb[k]], [sqb[i]], lambda e, k=k, i=i: e.activation(out=sq[i][:], in_=x[:, k, :], func=AF.Square))
                _mm(S, psn[:], C.ones_bf[:], sq[i][:], k == 0, k == KC - 1, [sqb[i], C.const_b], [psnb])
            S.op("act", [psnb], [rstdb], lambda e: e.activation(out=rstd[:], in_=psn[:], func=AF.Ln, bias=C.eps_sb[:, 0:1], scale=1.0))
            S.op("act", [rstdb], [rstdb], lambda e: e.activation(out=rstd[:], in_=rstd[:], func=AF.Exp, scale=-0.5))
            for k in range(KC):
                i = ntmp % 2
                ntmp += 1
                S.op("dve", [xb[k], rstdb, C.gs_b], [tmpb[i]],
                     lambda e, k=k, i=i: e.scalar_tensor_tensor(tmp[i][:], x[:, k, :], C.gs[:, k_mod, k, r:r + 1], rstd[:],
                                                                ALU.mult, ALU.mult))
                S.op("act", [tmpb[i], C.modv_b], [hb],
                     lambda e, k=k, i=i: e.activation(out=h[:, k, :], in_=tmp[i][:], func=AF.Identity,
                                                      bias=C.modv[:, (3 * k_mod) * KC + k, r:r + 1], scale=1.0),
                     self_dep=False)
            if ti == 0:
                C.dump(f"rstd{l}{w}", rstd[:], [128, TT], F32, [rstdb])
                C.dump(f"h{l}{w}", h[:], [128, KC, TT], BF16, [hb])
            for jg in range(NJG):
                ensure(tcount + 1)
                sl = slot_of[tcount]
                tcount += 1
                if ti == 0 and jg == 0:
                    C.dump(f"wins{l}{w}", wins[sl][:], [128, 2, KC, JG * 128], BF16, [winb[sl]])
                for jj in range(JG):
                    j = jg * JG + jj
                    pi = nps % 2
                    nps += 1
                    for k in range(KC):
                        _mm(S, psg[pi][:], wins[sl][:, 0, k, jj * 128:(jj + 1) * 128], h[:, k, :], k == 0, k == KC - 1,
                            [winb[sl], hb], [psgb[pi]])
                    for k in range(KC):
                        _mm(S, psu[pi][:], wins[sl][:, 1, k, jj * 128:(jj + 1) * 128], h[:, k, :], k == 0, k == KC - 1,
                            [winb[sl], hb], [psub[pi]])
                    S.op("act", [psgb[pi]], [sgb[pi]], lambda e, pi=pi: e.activation(out=sg[pi][:], in_=psg[pi][:], func=AF.Silu))
                    if ti == 0 and j == 0:
                        C.dump(f"sg{l}{w}", sg[pi][:], [128, TT], F32, [sgb[pi]])
                    S.op("dve", [sgb[pi], psub[pi]], [ab],
                         lambda e, pi=pi, j=j: e.tensor_tensor(a[:, j, :], sg[pi][:], psu[pi][:], ALU.mult), self_dep=False)
            if ti == 0:
                C.dump(f"a{l}{w}", a[:], [128, FC, TT], BF16, [ab])
            for m in range(KC):
                ensure(tcount + 1)
                sl = slot_of[tcount]
                tcount += 1
                pi = npy % 2
                npy += 1
                for j in range(FC):
                    _mm(S, psy[pi][:], wouts[sl][:, j, :], a[:, j, :], j == 0, j == FC - 1, [woutb[sl], ab], [psyb[pi]])
                S.op("dve", [psyb[pi], xb[m], C.hg_b], [xb[m]],
                     lambda e, pi=pi, m=m: e.scalar_tensor_tensor(x[:, m, :], psy[pi][:], C.hg[:, k_mod, m, r:r + 1], x[:, m, :],
                                                                  ALU.mult, ALU.add))
                S.dma("act", stv[:, m, col0:col0 + TT], x[:, m, :], [xb[m]], [C.st_b], partial=True)
                if ti + 1 < len(tiles):
                    load_x(ti + 1, m)
        S.barrier()


class NormKit:
    def __init__(self, C, es, pfx):
        nc = C.nc
        sb = lambda n, sh, dt_: es.enter_context(C.sbuf(f"{pfx}_{n}", sh, dt_))
        self.xk = [sb(f"xk{i}", [128, TT], F32) for i in range(3)]
        self.xkb = [Buf(f"xk{i}") for i in range(3)]
        self.sq = [sb(f"sq{i}", [128, TT], BF16) for i in range(2)]
        self.sqb = [Buf(f"sq{i}") for i in range(2)]
        self.tmp = [sb(f"tmp{i}", [128, TT], F32) for i in range(2)]
        self.tmpb = [Buf(f"tmp{i}") for i in range(2)]
        self.rstd = sb("rstd", [128, TT], F32)
        self.rstdb = Buf("rstd")
        self.psn = es.enter_context(C.psum(f"{pfx}_psn", [128, TT], F32))
        self.psnb = Buf("psn")
        self.nx = 0
        self.ns = 0
        self.nt = 0

    def emit(self, C, col0, n, r, k_mod, h, hb, hoff=0):
        S = C.S
        stv = C.st.rearrange("(k p) t -> p k t", p=128)
        for k in range(KC):
            i = self.nx % 3
            self.nx += 1
            S.dma("pool", self.xk[i][:, :n], stv[:, k, col0:col0 + n], [C.st_b], [self.xkb[i]])
            j = self.ns % 2
            self.ns += 1
            S.op("act", [self.xkb[i]], [self.sqb[j]],
                 lambda e, i=i, j=j: e.activation(out=self.sq[j][:, :n], in_=self.xk[i][:, :n], func=AF.Square))
            _mm(S, self.psn[:, :n], C.ones_bf[:], self.sq[j][:, :n], k == 0, k == KC - 1, [self.sqb[j], C.const_b], [self.psnb])
        S.op("act", [self.psnb], [self.rstdb],
             lambda e: e.activation(out=self.rstd[:, :n], in_=self.psn[:, :n], func=AF.Ln, bias=C.eps_sb[:, 0:1], scale=1.0))
        S.op("act", [self.rstdb], [self.rstdb], lambda e: e.activation(out=self.rstd[:, :n], in_=self.rstd[:, :n], func=AF.Exp, scale=-0.5))
        for k in range(KC):
            i = self.nx % 3
            self.nx += 1
            S.dma("pool", self.xk[i][:, :n], stv[:, k, col0:col0 + n], [C.st_b], [self.xkb[i]])
            j = self.nt % 2
            self.nt += 1
            S.op("dve", [self.xkb[i], self.rstdb, C.gs_b], [self.tmpb[j]],
                 lambda e, i=i, j=j, k=k: e.scalar_tensor_tensor(self.tmp[j][:, :n], self.xk[i][:, :n], C.gs[:, k_mod, k, r:r + 1],
                                                                 self.rstd[:, :n], ALU.mult, ALU.mult))
            S.op("act", [self.tmpb[j], C.modv_b], [hb],
                 lambda e, j=j, k=k: e.activation(out=h[:, k, hoff:hoff + n], in_=self.tmp[j][:, :n], func=AF.Identity,
                                                  bias=C.modv[:, (3 * k_mod) * KC + k, r:r + 1], scale=1.0), self_dep=False)


class WStream:
    def __init__(self, S, slots, seq, src_buf, q="sp"):
        self.S = S
        self.slots = slots
        self.bufs = [Buf(f"ws{i}") for i in range(len(slots))]
        self.seq = seq
        self.src_buf = src_buf
        self.q = q
        self.issued = 0
        self.i = 0

    def _issue(self):
        k = self.issued
        item = self.seq[k]
        if not isinstance(item, list):
            item = [item]
        sl = k % 2
        for ii, (src, dst_fn) in enumerate(item):
            self.S.dma(self.q, dst_fn(self.slots[sl]), src, [self.src_buf], [self.bufs[sl]], partial=(ii > 0))
        self.issued += 1

    def get(self):
        while self.issued <= min(self.i + 1, len(self.seq) - 1):
            self._issue()
        sl = self.i % 2
        self.i += 1
        return sl


A_QW = 2048
A_KOFF = 2048
A_VOFF = 2560
NKT = (CTX + SEQ) // 128


def emit_attn(C, l):
    nc, S = C.nc, C.S
    i = l // 2
    ctx_out = (l == 0)
    w_in = C.win(f"w_attn_in{i}", [D, 3072])
    w_out = C.win(f"w_attn_out{i}", [D, D])
    gqk_d = C.win(f"gqk{i}", [128, 4])
    sink_d = C.win(f"sink{i}", [128, 8])
    rope_d = C.win("rope_cs", [2, 128, SEQ])
    rotm_d = C.win("rotm", [128, 128], BF16)
    mask_d = C.win("wmask", [128, 2, 128], BF16)
    for k in range(KC):
        S.dma("bg", C.wainb[k * 128:(k + 1) * 128, :], w_in[k * 128:(k + 1) * 128, :], [], [C.wainb_b], partial=True)
    wo_v = w_out.rearrange("(j p) (m c) -> m p j c", p=128, c=128)
    for m in range(KC):
        S.dma("bg", C.waob[m].rearrange("p (j c) -> p j c", c=128), wo_v[m], [], [C.waob_b], partial=True)
    wainv = C.wainb.rearrange("(k p) n -> p k n", p=128)
    stv = C.st.rearrange("(k p) t -> p k t", p=128)
    SCALE = float(HD) ** -0.5

    with ExitStack() as es:
        sbt = lambda n, sh, dt_: es.enter_context(C.sbuf("at_" + n, sh, dt_))
        pst = lambda n: es.enter_context(C.psum("at_" + n, [128, TT], F32))
        kit = NormKit(C, es, "at")
        h = sbt("h", [128, KC, TT], BF16)
        hb = Buf("h")
        KT = [sbt(f"kt{s_}", [128, 2, CTX + SEQ], BF16) for s_ in range(2)]
        KTb = [Buf(f"kt{s_}") for s_ in range(2)]
        V = sbt("v", [128, NKT, 512], BF16)
        Vb = Buf("v")
        OT = sbt("ot", [128, 16, TT], BF16)
        OTb = Buf("ot")
        wsl = [sbt(f"w{j}", [128, KC, 512], BF16) for j in range(2)]
        wos = [sbt(f"wo{j}", [128, 16, 128], BF16) for j in range(2)]
        ropeC = sbt("ropec", [128, SEQ], F32)
        ropeS = sbt("ropes", [128, SEQ], F32)
        rotm = sbt("rotm", [128, 128], BF16)
        masks = sbt("masks", [128, 2, 128], BF16)
        gqk = sbt("gqk", [128, 4], F32)
        esink = sbt("esink", [128, 8], F32)
        ones1 = sbt("ones1", [128, 128], BF16)
        onesh = sbt("onesh", [128, 128], BF16)
        cb = Buf("at_const")
        S.dma("sp", ropeC[:], rope_d[0], [], [cb])
        S.dma("sp", ropeS[:], rope_d[1], [], [cb], partial=True)
        S.dma("sp", rotm[:], rotm_d, [], [cb], partial=True)
        S.dma("sp", masks[:], mask_d, [], [cb], partial=True)
        S.dma("sp", gqk[:], gqk_d, [], [cb], partial=True)
        S.dma("sp", esink[:], sink_d, [], [cb], partial=True)
        S.op("act", [cb], [cb], lambda e: e.activation(out=esink[:], in_=esink[:], func=AF.Exp))
        S.op("dve", [cb], [cb], lambda e: e.memset(ones1[:], 1.0))
        S.op("dve", [cb], [cb], lambda e: e.memset(onesh[:], 1.0 / HD))
        sqh = [sbt(f"sqh{j}", [128, TT], BF16) for j in range(2)]
        sqhb = [Buf(f"sqh{j}") for j in range(2)]
        rs = sbt("rs", [128, TT], F32)
        rsb = Buf("rs")
        tgb = [sbt(f"tgb{j}", [128, TT], BF16) for j in range(2)]
        tgbb = [Buf(f"tgb{j}") for j in range(2)]
        u1 = sbt("u1", [128, TT], F32)
        u1b = Buf("u1")
        u2 = sbt("u2", [128, TT], F32)
        u2b = Buf("u2")
        qb_ = [sbt(f"q{j}", [128, TT], BF16) for j in range(2)]
        qbb = [Buf(f"q{j}") for j in range(2)]
        pT = [sbt(f"pT{j}", [128, TT], BF16) for j in range(2)]
        pTb = [Buf(f"pT{j}") for j in range(2)]
        rden = sbt("rden", [128, TT], F32)
        rdenb = Buf("rden")
        xres = [sbt(f"xres{j}", [128, TT], F32) for j in range(2)]
        xresb = [Buf(f"xres{j}") for j in range(2)]
        pt = [pst(f"pt{j}") for j in range(2)]
        ptb = [Buf(f"pt{j}") for j in range(2)]
        pr = pst("pr")
        prb = Buf("pr")
        ps = [pst(f"ps{j}") for j in range(2)]
        psb = [Buf(f"ps{j}") for j in range(2)]
        po = pst("po")
        pob = Buf("po")
        pd = pst("pd")
        pdb = Buf("pd")
        cnt = {"pt": 0, "sqh": 0, "tgb": 0, "q": 0, "ps": 0, "xres": 0}

        def rot(name, nbuf=2):
            v = cnt[name] % nbuf
            cnt[name] += 1
            return v

        def p1_tiles(be):
            t = [(NLAT + be * CTX, CTX, NB, 0, None)]
            for q in range(SEQ // TT):
                t.append((be * SEQ + q * TT, TT, be, CTX + q * TT, q * TT))
            return t

        def p2_tiles(be):
            t = []
            if ctx_out:
                t.append((NLAT + be * CTX, CTX, NB, None, None))
            for q in range(SEQ // TT):
                t.append((be * SEQ + q * TT, TT, be, q * TT, q))
            return t

        in_seq = []
        out_seq = []
        for be in C.be_list:
            for _ in p1_tiles(be):
                for c in range(4):
                    in_seq.append((wainv[:, :, A_KOFF + c * 128:A_KOFF + (c + 1) * 128], lambda sl: sl[:, :, 0:128]))
                in_seq.append((wainv[:, :, A_VOFF:A_VOFF + 512], lambda sl: sl[:, :, :]))
            for _ in p2_tiles(be):
                for hd in range(16):
                    in_seq.append((wainv[:, :, hd * 128:(hd + 1) * 128], lambda sl: sl[:, :, 0:128]))
                for m in range(KC):
                    out_seq.append((C.waob[m].rearrange("p (j c) -> p j c", c=128), lambda sl: sl[:, :, :]))
        win_s = WStream(S, wsl, in_seq, C.wainb_b)
        wo_s = WStream(S, wos, out_seq, C.waob_b)

        def qk_finish(pi, n, gcol, rope_off, dst_ap, dst_buf):
            j = rot("sqh")
            S.op("act", [ptb[pi]], [sqhb[j]], lambda e: e.activation(out=sqh[j][:, :n], in_=pt[pi][:, :n], func=AF.Square))
            _mm(S, kit.psn[:, :n], onesh[:], sqh[j][:, :n], True, True, [sqhb[j], cb], [kit.psnb])
            S.op("act", [kit.psnb], [rsb],
                 lambda e: e.activation(out=rs[:, :n], in_=kit.psn[:, :n], func=AF.Ln, bias=C.eps_sb[:, 0:1], scale=1.0))
            S.op("act", [rsb], [rsb], lambda e: e.activation(out=rs[:, :n], in_=rs[:, :n], func=AF.Exp, scale=-0.5))
            if rope_off is None:
                S.op("dve", [ptb[pi], rsb, cb], [dst_buf],
                     lambda e: e.scalar_tensor_tensor(dst_ap, pt[pi][:, :n], gqk[:, gcol:gcol + 1], rs[:, :n], ALU.mult, ALU.mult))
                return
            t = rot("tgb")
            S.op("dve", [ptb[pi], rsb, cb], [tgbb[t]],
                 lambda e: e.scalar_tensor_tensor(tgb[t][:, :n], pt[pi][:, :n], gqk[:, gcol:gcol + 1], rs[:, :n], ALU.mult, ALU.mult))
            _mm(S, pr[:, :n], rotm[:], tgb[t][:, :n], True, True, [tgbb[t], cb], [prb])
            S.op("dve", [tgbb[t], cb], [u1b], lambda e: e.tensor_tensor(u1[:, :n], tgb[t][:, :n], ropeC[:, rope_off:rope_off + n], ALU.mult))
            S.op("dve", [prb, cb], [u2b], lambda e: e.tensor_tensor(u2[:, :n], pr[:, :n], ropeS[:, rope_off:rope_off + n], ALU.mult))
            S.op("dve", [u1b, u2b], [dst_buf], lambda e: e.tensor_tensor(dst_ap, u1[:, :n], u2[:, :n], ALU.add))

        for be in C.be_list:
            for (col0, n, r, key_off, rope_off) in p1_tiles(be):
                kit.emit(C, col0, n, r, 1, h, hb)
                for c in range(4):
                    sl = win_s.get()
                    pi = rot("pt")
                    for k in range(KC):
                        _mm(S, pt[pi][:, :n], wsl[sl][:, k, 0:128], h[:, k, :n], k == 0, k == KC - 1, [win_s.bufs[sl], hb], [ptb[pi]])
                    sset = c // 2
                    qk_finish(pi, n, 2 + sset, rope_off, KT[sset][:, c % 2, key_off:key_off + n], KTb[sset])
                sl = win_s.get()
                for sub in range(n // 128):
                    pi = rot("pt")
                    for k in range(KC):
                        _mm(S, pt[pi][:, :], h[:, k, sub * 128:(sub + 1) * 128], wsl[sl][:, k, :], k == 0, k == KC - 1,
                            [win_s.bufs[sl], hb], [ptb[pi]])
                    kt = key_off // 128 + sub
                    S.op("act", [ptb[pi]], [Vb], lambda e, pi=pi, kt=kt: e.copy(V[:, kt, :], pt[pi][:, :]), self_dep=False)
            for (col0, n, r, rope_off, qt) in p2_tiles(be):
                is_ctx = (qt is None)
                kit.emit(C, col0, n, r, 1, h, hb)
                for hd in range(16):
                    sset = 0 if hd < 8 else 1
                    kv = (hd % 8) // 4
                    sl = win_s.get()
                    pi = rot("pt")
                    for k in range(KC):
                        _mm(S, pt[pi][:, :n], wsl[sl][:, k, 0:128], h[:, k, :n], k == 0, k == KC - 1, [win_s.bufs[sl], hb], [ptb[pi]])
                    qi = rot("q")
                    qk_finish(pi, n, sset, rope_off, qb_[qi][:, :n], qbb[qi])
                    keys = []
                    if is_ctx:
                        keys = [(0, 0, n, []), (1, 0, n, [])]
                    elif sset == 0:
                        keys = [(kt, 0, n, []) for kt in range(NKT)]
                    else:
                        keys = [(0, 0, n, []), (1, 0, n, [])]
                        qb0 = qt * 4
                        for kb in range(max(0, qb0 - 1), min(15, qb0 + 4) + 1):
                            lo = max(kb - 1, qb0)
                            hi = min(kb + 1, qb0 + 3)
                            if lo > hi:
                                continue
                            mk = []
                            for qbk in range(lo, hi + 1):
                                if qbk == kb + 1:
                                    mk.append(((qbk - lo) * 128, 0))
                                elif qbk == kb - 1:
                                    mk.append(((qbk - lo) * 128, 1))
                            keys.append((2 + kb, (lo - qb0) * 128, (hi - qb0 + 1) * 128, mk))
                    slot_a = {}

                    def emit_s(ki):
                        kt, c0, c1, mk = keys[ki]
                        w_ = c1 - c0
                        a = rot("ps")
                        slot_a[ki] = a
                        _mm(S, ps[a][:, :w_], KT[sset][:, kv, kt * 128:(kt + 1) * 128], qb_[qi][:, c0:c1], True, True,
                            [KTb[sset], qbb[qi]], [psb[a]])
                        S.op("act", [psb[a]], [pTb[a]],
                             lambda e: e.activation(out=pT[a][:, :w_], in_=ps[a][:, :w_], func=AF.Exp, scale=SCALE))
                        for (off, which) in mk:
                            S.op("pool", [pTb[a], cb], [pTb[a]],
                                 lambda e, off=off, which=which: e.tensor_tensor(pT[a][:, off:off + 128], pT[a][:, off:off + 128],
                                                                                 masks[:, which, :], ALU.mult))

                    def emit_pv(ki):
                        kt, c0, c1, mk = keys[ki]
                        w_ = c1 - c0
                        a = slot_a[ki]
                        vcol = sset * 256 + kv * 128
                        _mm(S, po[:, c0:c1], V[:, kt, vcol:vcol + 128], pT[a][:, :w_], ki == 0, ki == len(keys) - 1, [Vb, pTb[a]], [pob])
                        _mm(S, pd[:, c0:c1], ones1[:], pT[a][:, :w_], ki == 0, ki == len(keys) - 1, [cb, pTb[a]], [pdb])

                    emit_s(0)
                    for ki in range(len(keys)):
                        if ki + 1 < len(keys):
                            emit_s(ki + 1)
                        emit_pv(ki)
                    if sset == 1:
                        S.op("dve", [pdb, cb], [rdenb],
                             lambda e, hd=hd: e.tensor_scalar(rden[:, :n], pd[:, :n], esink[:, hd - 8:hd - 7], None, ALU.add))
                        S.op("dve", [rdenb], [rdenb], lambda e: e.reciprocal(rden[:, :n], rden[:, :n]))
                    else:
                        S.op("dve", [pdb], [rdenb], lambda e: e.reciprocal(rden[:, :n], pd[:, :n]))
                    S.op("dve", [pob, rdenb], [OTb], lambda e, hd=hd: e.tensor_tensor(OT[:, hd, :n], po[:, :n], rden[:, :n], ALU.mult))
                for m in range(KC):
                    sl = wo_s.get()
                    pi = rot("pt")
                    for hd in range(16):
                        _mm(S, pt[pi][:, :n], wos[sl][:, hd, :], OT[:, hd, :n], hd == 0, hd == 15, [wo_s.bufs[sl], OTb], [ptb[pi]])
                    xi = rot("xres")
                    S.dma("pool", xres[xi][:, :n], stv[:, m, col0:col0 + n], [C.st_b], [xresb[xi]])
                    S.op("dve", [ptb[pi], xresb[xi], C.hg_b], [xresb[xi]],
                         lambda e, pi=pi, xi=xi, m=m: e.scalar_tensor_tensor(xres[xi][:, :n], pt[pi][:, :n], C.hg[:, 1, m, r:r + 1],
                                                                            xres[xi][:, :n], ALU.mult, ALU.add))
                    S.dma("pool", stv[:, m, col0:col0 + n], xres[xi][:, :n], [xresb[xi]], [C.st_b], partial=True)
        S.barrier()


def _hy_sets(l):
    sets = [(0, SEQ)]
    if l == 1:
        sets.append((1, CTX))
    return sets


def _hy_seq_info(C, nset, be):
    if nset == 0:
        return be, be * SEQ, be
    return NB + be, NLAT + be * CTX, NB


def _hy_dft(C, n):
    nch = n // 128
    t = {}
    for nm in ("cf", "sf", "si"):
        t[nm] = C.win(f"dft_{nm}{n}", [nch, 128, nch * 128], BF16)
    for nm in ("cr", "sr"):
        t[nm] = C.win(f"dft_{nm}{n}", [n, n], BF16)
    return t


def emit_hyena(C, l):
    nc, S = C.nc, C.S
    i = l // 2
    sets = _hy_sets(l)
    w_in = C.win(f"w_hy_in{i}", [D, 3 * D])
    w_out = C.win(f"w_hy_out{i}", [D, D])
    for k in range(KC):
        S.dma("bg", C.whinb[k * 128:(k + 1) * 128, :], w_in[k * 128:(k + 1) * 128, :], [], [C.whinb_b], partial=True)
    wo_v = w_out.rearrange("(j p) (m c) -> m p j c", p=128, c=128)
    for m in range(KC):
        S.dma("bg", C.waob[m].rearrange("p (j c) -> p j c", c=128), wo_v[m], [], [C.waob_b], partial=True)
    emit_hy_filter(C, l)
    emit_hy_inproj(C, l)
    emit_hy_conv(C, l)
    emit_hy_outproj(C, l)


def _dft_forward(C, S, fw, u, ub, n, psA, psAb, psB, psBb, consumer, need_a=True):
    nch = n // 128
    for fc in range(nch):
        sl = fw.get()
        a = fc % 2
        if need_a:
            for sc in range(nch):
                _mm(S, psA[a][:, :], fw.slots[sl][:, 0, sc * 128:(sc + 1) * 128], u[:, sc, :], sc == 0, sc == nch - 1,
                    [fw.bufs[sl], ub], [psAb[a]])
        for sc in range(nch):
            _mm(S, psB[a][:, :], fw.slots[sl][:, 1, sc * 128:(sc + 1) * 128], u[:, sc, :], sc == 0, sc == nch - 1,
                [fw.bufs[sl], ub], [psBb[a]])
        consumer(fc, a)


def emit_hy_filter(C, l):
    nc, S = C.nc, C.S
    i = l // 2
    w1_d = C.win(f"hf_w1_{i}", [33, 64])
    w2_d = C.win(f"hf_w2_{i}", [64, 64])
    w3_d = C.win(f"hf_w3_{i}", [64, 4 * D])
    hfv_d = C.win(f"hf_vec_{i}", [64, 4])
    bias_d = C.win(f"hy_bias_{i}", [128, 2, D])
    absd_d = C.win("hy_absdelta", [128, D])
    for (nset, n) in _hy_sets(l):
        nch = n // 128
        NN = 2 * n
        dft = _hy_dft(C, n)
        feats_d = C.win(f"hy_feats{n}", [33, n])
        negt_d = C.win(f"hy_negt{n}", [128, nch])
        wf_d = C.win(f"hy_wf{n}", [128, 2, nch])
        W = min(512, n)
        with ExitStack() as es:
            sbt = lambda nm, sh, dt_: es.enter_context(C.sbuf("hf_" + nm, sh, dt_))
            pst = lambda nm: es.enter_context(C.psum("hf_" + nm, [128, 512], F32))
            cb = Buf("hf_const")
            w1 = sbt("w1", [33, 64], F32)
            w2 = sbt("w2", [64, 64], F32)
            w3 = sbt("w3", [64, 2, 512], F32)
            w3b = Buf("w3")
            hfv = sbt("hfv", [64, 4], F32)
            sc = sbt("sc", [64, 4], F32)
            feats = sbt("feats", [33, n], F32)
            bias = sbt("bias", [128, 512], F32)
            absd = sbt("absd", [128, 512], F32)
            negt = sbt("negt", [128, nch], F32)
            wf = sbt("wf", [128, 2, nch], F32)
            ones1 = sbt("ones1", [128, 128], BF16)
            for dst, src in ((w1, w1_d), (w2, w2_d), (hfv, hfv_d), (feats, feats_d), (negt, negt_d), (wf, wf_d)):
                S.dma("sp", dst[:], src, [], [cb], partial=True)
            S.op("dve", [cb], [cb], lambda e: e.memset(ones1[:], 1.0))
            for j in range(2):
                S.op("dve", [cb], [cb], lambda e, j=j: e.tensor_scalar(sc[:, 2 * j:2 * j + 1], hfv[:, 2 * j + 1:2 * j + 2], 1.0 / 3.0, None, ALU.mult))
                S.op("dve", [cb], [cb], lambda e, j=j: e.tensor_tensor(sc[:, 2 * j + 1:2 * j + 2], sc[:, 2 * j:2 * j + 1], hfv[:, 2 * j:2 * j + 1], ALU.mult))
            h1 = sbt("h1", [64, n], F32)
            h1b = Buf("h1")
            h2 = sbt("h2", [64, n], F32)
            h2b = Buf("h2")
            s3 = sbt("s3", [64, W], F32)
            s3b = Buf("s3")
            tq = sbt("tq", [64, W], F32)
            tqb = Buf("tq")
            pz = pst("pz")
            pzb = Buf("pz")
            for (lhs, src, srcb, dst, dstb, j) in ((w1, feats, cb, h1, h1b, 0), (w2, h1, h1b, h2, h2b, 1)):
                for c0 in range(0, n, W):
                    _mm(S, pz[0:64, :W], lhs[:], src[:, c0:c0 + W], True, True, [cb, srcb], [pzb])
                    S.op("act", [pzb, cb], [s3b], lambda e, j=j: e.activation(out=s3[:], in_=pz[0:64, :W], func=AF.Sin,
                                                                              bias=sc[:, 2 * j + 1:2 * j + 2], scale=sc[:, 2 * j:2 * j + 1]))
                    S.op("dve", [s3b], [tqb], lambda e: e.tensor_tensor(tq[:], s3[:], s3[:], ALU.mult))
                    S.op("dve", [tqb], [tqb], lambda e: e.tensor_scalar(tq[:], tq[:], -4.0, 3.0, ALU.mult, ALU.add))
                    S.op("dve", [tqb, s3b], [dstb], lambda e, dst=dst, c0=c0: e.tensor_tensor(dst[:, c0:c0 + W], s3[:], tq[:], ALU.mult))
            kf = sbt("kf", [128, nch, 512], F32)
            kfb = Buf("kf")
            kb = sbt("kb", [128, nch, 512], F32)
            kbb = Buf("kb")
            eT = sbt("e", [128, nch, 512], BF16)
            eb = Buf("e")
            gT = sbt("g", [128, nch, 512], BF16)
            gb = Buf("g")
            kre2 = [sbt(f"kre{j}", [128, 512], F32) for j in range(2)]
            kre2b = [Buf(f"kre{j}") for j in range(2)]
            kim2 = [sbt(f"kim{j}", [128, 512], F32) for j in range(2)]
            kim2b = [Buf(f"kim{j}") for j in range(2)]
            krs = sbt("krs", [128, 512], F32)
            krsb = Buf("krs")
            dec = sbt("dec", [128, 512], F32)
            decb = Buf("dec")
            sq = [sbt(f"sq{j}", [128, 512], BF16) for j in range(2)]
            sqb = [Buf(f"sq{j}") for j in range(2)]
            scl = sbt("scl", [128, 512], F32)
            sclb = Buf("scl")
            tmp = sbt("tmp", [128, 512], F32)
            tmpb = Buf("tmp")
            slots = [sbt(f"dsl{j}", [128, 2, nch * 128], BF16) for j in range(2)]
            pk = [pst(f"pk{j}") for j in range(2)]
            pkb = [Buf(f"pk{j}") for j in range(2)]
            pn = pst("pn")
            pnb = Buf("pn")
            psA = [pst(f"pa{j}") for j in range(2)]
            psAb = [Buf(f"pa{j}") for j in range(2)]
            psB = [pst(f"pb{j}") for j in range(2)]
            psBb = [Buf(f"pb{j}") for j in range(2)]
            seq = []
            for o in range(2):
                for dt in range(4):
                    for rep in range(2):
                        for fc in range(nch):
                            seq.append([(dft["cf"][fc], lambda sl: sl[:, 0, :]), (dft["sf"][fc], lambda sl: sl[:, 1, :])])
            fw = WStream(S, slots, seq, Buf("dftc"))
            nsq = [0]
            for o in range(2):
                for dt in range(4):
                    dsl = slice(dt * 512, (dt + 1) * 512)
                    ob = Buf("hf_odt")
                    for dr in range(2):
                        c0 = o * 2 * D + dr * D + dt * 512
                        S.dma("sp", w3[:, dr, :], w3_d[:, c0:c0 + 512], [], [w3b], partial=(dr == 1))
                    S.dma("sp", bias[:], bias_d[:, o, dsl], [], [cb], partial=True)
                    S.dma("sp", absd[:], absd_d[:, dsl], [], [cb], partial=True)
                    for lc in range(nch):
                        S.op("act", [cb], [decb], lambda e, lc=lc: e.activation(out=dec[:], in_=absd[:], func=AF.Exp, scale=negt[:, lc:lc + 1]))
                        for dr, (kt_, ktb_) in enumerate(((kf, kfb), (kb, kbb))):
                            a = (2 * lc + dr) % 2
                            _mm(S, pk[a][:, :], h2[:, lc * 128:(lc + 1) * 128], w3[:, dr, :], True, True, [h2b, w3b], [pkb[a]])
                            S.op("dve", [pkb[a], decb], [ktb_], lambda e, kt_=kt_, a=a, lc=lc: e.tensor_tensor(kt_[:, lc, :], pk[a][:, :], dec[:], ALU.mult))
                    S.op("dve", [kbb], [kbb], lambda e: e.memset(kb[0:1, 0, :], 0.0))
                    tot = 2 * nch
                    cnt_ = 0
                    for (kt_, ktb_) in ((kf, kfb), (kb, kbb)):
                        for lc in range(nch):
                            j = nsq[0] % 2
                            nsq[0] += 1
                            S.op("act", [ktb_], [sqb[j]], lambda e, kt_=kt_, lc=lc, j=j: e.activation(out=sq[j][:], in_=kt_[:, lc, :], func=AF.Square))
                            _mm(S, pn[:, :], ones1[:], sq[j][:], cnt_ == 0, cnt_ == tot - 1, [sqb[j], cb], [pnb])
                            cnt_ += 1
                    S.op("act", [pnb], [sclb], lambda e: e.activation(out=scl[:], in_=pn[:, :], func=AF.Sqrt, bias=C.eps_sb[:, 0:1], scale=1.0))
                    S.op("dve", [sclb], [sclb], lambda e: e.reciprocal(scl[:], scl[:]))
                    for lc in range(nch):
                        S.op("dve", [kfb, kbb], [tmpb], lambda e, lc=lc: e.tensor_tensor(tmp[:], kf[:, lc, :], kb[:, lc, :], ALU.add))
                        S.op("dve", [tmpb, sclb], [eb], lambda e, lc=lc: e.tensor_tensor(eT[:, lc, :], tmp[:], scl[:], ALU.mult))
                        S.op("dve", [kfb, kbb], [tmpb], lambda e, lc=lc: e.tensor_tensor(tmp[:], kf[:, lc, :], kb[:, lc, :], ALU.subtract))
                        S.op("dve", [tmpb, sclb], [gb], lambda e, lc=lc: e.tensor_tensor(gT[:, lc, :], tmp[:], scl[:], ALU.mult))

                    ktb = C.ktab_b[nset]

                    def cons_e(fc, a, o=o, dt=dt):
                        j = fc % 2
                        S.op("dve", [psAb[a], cb], [tmpb], lambda e: e.tensor_tensor(tmp[:], psA[a][:, :], bias[:], ALU.add))
                        S.op("dve", [tmpb, cb], [kre2b[j]], lambda e: e.tensor_scalar(kre2[j][:], tmp[:], wf[:, 0, fc:fc + 1], None, ALU.mult))
                        if fc == 0:
                            S.op("dve", [kre2b[j]], [krsb], lambda e: e.tensor_copy(krs[:], kre2[j][:]))
                            S.op("dve", [psBb[a], cb], [tmpb], lambda e: e.tensor_tensor(tmp[0:1, :], psB[a][0:1, :], bias[0:1, :], ALU.add))
                            S.op("dve", [tmpb], [krsb], lambda e: e.tensor_scalar(krs[0:1, :], tmp[0:1, :], 1.0 / NN, None, ALU.mult))
                            S.dma("sp", C.krs0[nset, o, dt], krs[:], [krsb], [ktb], partial=True)
                        S.dma("sp", C.ktab[nset, o, 0, dt][:, fc * 512:(fc + 1) * 512], kre2[j][:], [kre2b[j]], [ktb], partial=True)

                    def cons_g(fc, a, o=o, dt=dt):
                        j = fc % 2
                        S.op("dve", [psBb[a], cb], [kim2b[j]], lambda e: e.tensor_scalar(kim2[j][:], psB[a][:, :], wf[:, 1, fc:fc + 1], None, ALU.mult))
                        if fc == 0:
                            S.op("dve", [kim2b[j]], [kim2b[j]], lambda e: e.memset(kim2[j][0:1, :], 0.0))
                        S.dma("sp", C.ktab[nset, o, 1, dt][:, fc * 512:(fc + 1) * 512], kim2[j][:], [kim2b[j]], [ktb], partial=True)

                    _dft_forward(C, S, fw, eT, eb, n, psA, psAb, psB, psBb, cons_e, need_a=True)
                    _dft_forward(C, S, fw, gT, gb, n, psA, psAb, psB, psBb, cons_g, need_a=False)
            S.barrier()


def emit_hy_inproj(C, l):
    nc, S = C.nc, C.S
    i = l // 2
    bin_d = C.win(f"hy_bin_{i}", [128, 48])
    wc_d = C.win(f"hy_wc_{i}", [128, 3, 48])
    bc_d = C.win(f"hy_bc_{i}", [128, 48])
    ident_d = C.win("ident_bf", [128, 128], BF16)
    whv = C.whinb.rearrange("(k p) n -> p k n", p=128)
    for (nset, n) in _hy_sets(l):
        nch = n // 128
        W = min(512, n)
        with ExitStack() as es:
            sbt = lambda nm, sh, dt_: es.enter_context(C.sbuf("hi_" + nm, sh, dt_))
            pst = lambda nm: es.enter_context(C.psum("hi_" + nm, [128, 512], F32))
            kit = NormKit(C, es, "hi")
            cb = Buf("hi_const")
            binv = sbt("bin", [128, 48], F32)
            wcv = sbt("wc", [128, 3, 48], F32)
            bcv = sbt("bc", [128, 48], F32)
            ident = sbt("ident", [128, 128], BF16)
            for dst, src in ((binv, bin_d), (wcv, wc_d), (bcv, bc_d), (ident, ident_d)):
                S.dma("sp", dst[:], src, [], [cb], partial=True)
            h = sbt("h", [128, KC, n], BF16)
            hb = Buf("h")
            pf = sbt("pf", [128, n + 2], F32)
            pfb = Buf("pf")
            pc = sbt("pc", [128, n], F32)
            pcb = Buf("pc")
            pcb16 = sbt("pc16", [128, n], BF16)
            pc16b = Buf("pc16")
            stg = sbt("stg", [128, nch, 512], BF16)
            stgb = Buf("stg")
            wsl = [sbt(f"w{j}", [128, KC, 128], BF16) for j in range(2)]
            pp = [pst(f"pp{j}") for j in range(2)]
            ppb = [Buf(f"pp{j}") for j in range(2)]
            ptr = [es.enter_context(C.psum(f"hi_ptr{j}", [128, 512], BF16)) for j in range(2)]
            ptrb = [Buf(f"ptr{j}") for j in range(2)]
            S.op("dve", [], [pfb], lambda e: e.memset(pf[:], 0.0))
            seq = []
            for be in C.be_list:
                for c in range(48):
                    seq.append((whv[:, :, c * 128:(c + 1) * 128], lambda sl: sl[:, :, :]))
            ws = WStream(S, wsl, seq, C.whinb_b)
            npp = [0]
            ntr = [0]
            for be in C.be_list:
                q, col0, r = _hy_seq_info(C, nset, be)
                for t0 in range(0, n, W):
                    kit.emit(C, col0 + t0, W, r, 1, h, hb, hoff=t0)
                for c in range(48):
                    sl = ws.get()
                    for t0 in range(0, n, W):
                        a = npp[0] % 2
                        npp[0] += 1
                        for k in range(KC):
                            _mm(S, pp[a][:, :W], wsl[sl][:, k, :], h[:, k, t0:t0 + W], k == 0, k == KC - 1, [ws.bufs[sl], hb], [ppb[a]])
                        S.op("act", [ppb[a], cb], [pfb], lambda e, a=a, t0=t0, c=c: e.activation(out=pf[:, 1 + t0:1 + t0 + W], in_=pp[a][:, :W],
                                                                                              func=AF.Identity, bias=binv[:, c:c + 1], scale=1.0),
                             self_dep=False)
                    S.op("act", [pfb, cb], [pcb], lambda e, c=c: e.activation(out=pc[:], in_=pf[:, 1:n + 1], func=AF.Identity,
                                                                             bias=bcv[:, c:c + 1], scale=wcv[:, 1, c:c + 1]))
                    S.op("dve", [pfb, pcb, cb], [pcb], lambda e, c=c: e.scalar_tensor_tensor(pc[:], pf[:, 0:n], wcv[:, 0, c:c + 1], pc[:], ALU.mult, ALU.add))
                    S.op("dve", [pfb, pcb, cb], [pc16b], lambda e, c=c: e.scalar_tensor_tensor(pcb16[:], pf[:, 2:n + 2], wcv[:, 2, c:c + 1], pc[:], ALU.mult, ALU.add))
                    if c >= 32:
                        S.dma("pool", C.hy_x2T[q, c - 32][:, :n], pcb16[:], [pc16b], [C.hy_x2T_b], partial=True)
                        continue
                    dcol = (c % 4) * 128
                    for g0 in range(0, nch, 4):
                        a = ntr[0] % 2
                        ntr[0] += 1
                        ng = min(4, nch - g0)
                        for gi in range(ng):
                            tcn = g0 + gi
                            S.op("pe", [pc16b, cb], [ptrb[a]], lambda e, a=a, gi=gi, tcn=tcn: e.transpose(ptr[a][:, gi * 128:(gi + 1) * 128],
                                                                                                      pcb16[:, tcn * 128:(tcn + 1) * 128], ident[:]))
                        S.op("act", [ptrb[a]], [stgb], lambda e, a=a, g0=g0, ng=ng, dcol=dcol: e.copy(
                            stg[:, g0:g0 + ng, dcol:dcol + 128], ptr[a][:, :ng * 128].rearrange("p (g c) -> p g c", c=128)), self_dep=False)
                    if c % 4 == 3:
                        dt = (c % 16) // 4
                        dst = C.hy_u if c < 16 else C.hy_x1
                        dstb = C.hy_u_b if c < 16 else C.hy_x1_b
                        S.dma("pool", dst[q, dt][:, :nch * 512].rearrange("p (c d) -> p c d", d=512), stg[:], [stgb], [dstb], partial=True)
            S.barrier()


def emit_hy_conv(C, l):
    nc, S = C.nc, C.S
    for (nset, n) in _hy_sets(l):
        nch = n // 128
        W = min(256, n)
        dft = _hy_dft(C, n)
        crv = dft["cr"].rearrange("(fk p) t -> p fk t", p=128)
        srv = dft["sr"].rearrange("(fk p) t -> p fk t", p=128)
        with ExitStack() as es:
            sbt = lambda nm, sh, dt_: es.enter_context(C.sbuf("hc_" + nm, sh, dt_))
            pst = lambda nm: es.enter_context(C.psum("hc_" + nm, [128, 512], F32))
            kre = sbt("kre", [128, nch, 512], F32)
            kim = sbt("kim", [128, nch, 512], F32)
            krs = sbt("krs", [128, 512], F32)
            ktb = Buf("hc_ktab")
            u = sbt("u", [128, nch, 512], BF16)
            ub = Buf("u")
            P = sbt("P", [128, nch, 512], BF16)
            Pb = Buf("P")
            Q = sbt("Q", [128, nch, 512], BF16)
            Qb = Buf("Q")
            t1 = sbt("t1", [128, 512], F32)
            t1b = Buf("t1")
            t2 = sbt("t2", [128, 512], F32)
            t2b = Buf("t2")
            t3 = sbt("t3", [128, 512], F32)
            t3b = Buf("t3")
            t4 = sbt("t4", [128, 512], F32)
            t4b = Buf("t4")
            xg = [sbt(f"xg{j}", [128, 512], BF16) for j in range(2)]
            xgb = [Buf(f"xg{j}") for j in range(2)]
            zo = [sbt(f"zo{j}", [128, 512], BF16) for j in range(2)]
            zob = [Buf(f"zo{j}") for j in range(2)]
            slots = [sbt(f"dsl{j}", [128, 2, nch * 128], BF16) for j in range(2)]
            wslots = [sbt(f"wsl{j}", [128, 2, nch, W], BF16) for j in range(2)]
            psA = [pst(f"pa{j}") for j in range(2)]
            psAb = [Buf(f"pa{j}") for j in range(2)]
            psB = [pst(f"pb{j}") for j in range(2)]
            psBb = [Buf(f"pb{j}") for j in range(2)]
            psY = [pst(f"py{j}") for j in range(2)]
            psYb = [Buf(f"py{j}") for j in range(2)]
            seqs = [_hy_seq_info(C, nset, be)[0] for be in C.be_list]
            seq = []
            wseq = []
            for dt in range(4):
                for o in range(2):
                    for q in seqs:
                        for fc in range(nch):
                            seq.append([(dft["cf"][fc], lambda sl: sl[:, 0, :]), (dft["sf"][fc], lambda sl: sl[:, 1, :])])
                        if o == 0:
                            for tc in range(nch):
                                seq.append([(dft["cf"][tc], lambda sl: sl[:, 0, :]), (dft["si"][tc], lambda sl: sl[:, 1, :])])
                        else:
                            for t0 in range(0, n, W):
                                wseq.append([(crv[:, :, t0:t0 + W], lambda sl: sl[:, 0, :, :]), (srv[:, :, t0:t0 + W], lambda sl: sl[:, 1, :, :])])
            fw = WStream(S, slots, seq, Buf("dftc"))
            ww = WStream(S, wslots, wseq, Buf("dftw"))
            cnt = {"xg": 0, "zo": 0, "py": 0}

            def rot(nm):
                v = cnt[nm] % 2
                cnt[nm] += 1
                return v

            for dt in range(4):
                for o in range(2):
                    S.dma("sp", kre[:], C.ktab[nset, o, 0, dt][:, :nch * 512].rearrange("p (c d) -> p c d", d=512), [C.ktab_b[nset]], [ktb])
                    S.dma("sp", kim[:], C.ktab[nset, o, 1, dt][:, :nch * 512].rearrange("p (c d) -> p c d", d=512), [C.ktab_b[nset]], [ktb], partial=True)
                    S.dma("sp", krs[:], C.krs0[nset, o, dt], [C.ktab_b[nset]], [ktb], partial=True)
                    for q in seqs:
                        usrc = C.hy_u[q, dt][:, :nch * 512].rearrange("p (c d) -> p c d", d=512)
                        S.dma("sp", u[:], usrc, [C.hy_u_b], [ub])

                        def cons(fc, a):
                            krx = krs[:] if fc == 0 else kre[:, fc, :]
                            S.op("dve", [psAb[a], ktb], [t1b], lambda e: e.tensor_tensor(t1[:], psA[a][:, :], kre[:, fc, :], ALU.mult))
                            S.op("dve", [psBb[a], ktb], [t2b], lambda e: e.tensor_tensor(t2[:], psB[a][:, :], kim[:, fc, :], ALU.mult))
                            S.op("pool", [t1b, t2b], [Pb], lambda e: e.tensor_tensor(P[:, fc, :], t1[:], t2[:], ALU.add))
                            S.op("dve", [psBb[a], ktb], [t3b], lambda e: e.tensor_tensor(t3[:], psB[a][:, :], krx, ALU.mult))
                            S.op("dve", [psAb[a], ktb], [t4b], lambda e: e.tensor_tensor(t4[:], psA[a][:, :], kim[:, fc, :], ALU.mult))
                            S.op("pool", [t3b, t4b], [Qb], lambda e: e.tensor_tensor(Q[:, fc, :], t3[:], t4[:], ALU.subtract))

                        _dft_forward(C, S, fw, u, ub, n, psA, psAb, psB, psBb, cons, need_a=True)
                        if o == 0:
                            for tc in range(nch):
                                sl = fw.get()
                                a = rot("py")
                                for fk in range(nch):
                                    _mm(S, psY[a][:, :], slots[sl][:, 0, fk * 128:(fk + 1) * 128], P[:, fk, :], fk == 0, False,
                                        [fw.bufs[sl], Pb], [psYb[a]])
                                for fk in range(nch):
                                    _mm(S, psY[a][:, :], slots[sl][:, 1, fk * 128:(fk + 1) * 128], Q[:, fk, :], False, fk == nch - 1,
                                        [fw.bufs[sl], Qb], [psYb[a]])
                                xi = rot("xg")
                                S.dma("pool", xg[xi][:], C.hy_x1[q, dt][:, tc * 512:(tc + 1) * 512], [C.hy_x1_b], [xgb[xi]])
                                zi = rot("zo")
                                S.op("dve", [psYb[a], xgb[xi]], [zob[zi]], lambda e: e.tensor_tensor(zo[zi][:], psY[a][:, :], xg[xi][:], ALU.mult))
                                S.dma("pool", C.hy_u[q, dt][:, tc * 512:(tc + 1) * 512], zo[zi][:], [zob[zi]], [C.hy_u_b], partial=True)
                        else:
                            for t0 in range(0, n, W):
                                sl = ww.get()
                                for dc in range(4):
                                    a = rot("py")
                                    for fk in range(nch):
                                        _mm(S, psY[a][:, :W], P[:, fk, dc * 128:(dc + 1) * 128], wslots[sl][:, 0, fk, :], fk == 0, False,
                                            [ww.bufs[sl], Pb], [psYb[a]])
                                    for fk in range(nch):
                                        _mm(S, psY[a][:, :W], Q[:, fk, dc * 128:(dc + 1) * 128], wslots[sl][:, 1, fk, :], False, fk == nch - 1,
                                            [ww.bufs[sl], Qb], [psYb[a]])
                                    xi = rot("xg")
                                    S.dma("pool", xg[xi][:, :W], C.hy_x2T[q, dt * 4 + dc][:, t0:t0 + W], [C.hy_x2T_b], [xgb[xi]])
                                    zi = rot("zo")
                                    S.op("dve", [psYb[a], xgb[xi]], [zob[zi]], lambda e: e.tensor_tensor(zo[zi][:, :W], psY[a][:, :W], xg[xi][:, :W], ALU.mult))
                                    S.dma("pool", C.hy_z2T[q, dt * 4 + dc][:, t0:t0 + W], zo[zi][:, :W], [zob[zi]], [C.hy_z2T_b], partial=True)
            S.barrier()


def emit_hy_outproj(C, l):
    nc, S = C.nc, C.S
    i = l // 2
    bo_d = C.win(f"hy_bout_{i}", [128, KC])
    stv = C.st.rearrange("(k p) t -> p k t", p=128)
    with ExitStack() as es:
        sbt = lambda nm, sh, dt_: es.enter_context(C.sbuf("ho_" + nm, sh, dt_))
        pst = lambda nm: es.enter_context(C.psum("ho_" + nm, [128, 512], F32))
        bo = sbt("bo", [128, KC], F32)
        cb = Buf("ho_const")
        S.dma("sp", bo[:], bo_d, [], [cb])
        z = sbt("z", [128, KC, TT], BF16)
        zb = Buf("z")
        wos = [sbt(f"wo{j}", [128, KC, 128], BF16) for j in range(2)]
        xres = [sbt(f"xres{j}", [128, TT], F32) for j in range(2)]
        xresb = [Buf(f"xres{j}") for j in range(2)]
        yb_ = sbt("y", [128, TT], F32)
        ybb = Buf("y")
        pt = [pst(f"pt{j}") for j in range(2)]
        ptb = [Buf(f"pt{j}") for j in range(2)]
        tiles = []
        for (nset, n) in _hy_sets(l):
            W = min(TT, n)
            for be in C.be_list:
                q, col0, r = _hy_seq_info(C, nset, be)
                for t0 in range(0, n, W):
                    tiles.append((q, col0, r, t0, W))
        seq = []
        for _ in tiles:
            for m in range(KC):
                seq.append((C.waob[m].rearrange("p (j c) -> p j c", c=128), lambda sl: sl[:, :, :]))
        ws = WStream(S, wos, seq, C.waob_b)
        n_ = [0]
        for (q, col0, r, t0, W) in tiles:
            S.dma("sp", z[:, :, :W], C.hy_z2T[q][:, :, t0:t0 + W].rearrange("k p t -> p k t"), [C.hy_z2T_b], [zb])
            for m in range(KC):
                sl = ws.get()
                a = n_[0] % 2
                n_[0] += 1
                for k in range(KC):
                    _mm(S, pt[a][:, :W], wos[sl][:, k, :], z[:, k, :W], k == 0, k == KC - 1, [ws.bufs[sl], zb], [ptb[a]])
                S.dma("pool", xres[a][:, :W], stv[:, m, col0 + t0:col0 + t0 + W], [C.st_b], [xresb[a]])
                S.op("act", [ptb[a], cb], [ybb], lambda e: e.activation(out=yb_[:, :W], in_=pt[a][:, :W], func=AF.Identity, bias=bo[:, m:m + 1], scale=1.0))
                S.op("dve", [ybb, xresb[a], C.hg_b], [xresb[a]],
                     lambda e: e.scalar_tensor_tensor(xres[a][:, :W], yb_[:, :W], C.hg[:, 1, m, r:r + 1], xres[a][:, :W], ALU.mult, ALU.add))
                S.dma("pool", stv[:, m, col0 + t0:col0 + t0 + W], xres[a][:, :W], [xresb[a]], [C.st_b], partial=True)
        S.barrier()


def build_program(phases, dbg=False, be_list=None):
    nc = bass.Bass("TRN2", target_bir_lowering=False)
    C = Ctx()
    C.nc = nc
    C.debug = dbg
    C.be_list = list(range(NB)) if be_list is None else be_list
    dt = nc.dram_tensor
    C.xin = dt("xin", [D, NTOK], F32, kind="ExternalInput").ap()
    C.cT = dt("cT", [128, KC, NR], F32, kind="ExternalInput").ap()
    C.st = dt("st", [D, NTOK], F32, kind="ExternalOutput").ap()
    C.winb = [dt(f"winb{i}", [D, 2 * DFF], BF16, kind="Internal").ap() for i in range(2)]
    C.woutb = [dt(f"woutb{i}", [KC, 128, DFF], BF16, kind="Internal").ap() for i in range(2)]
    C.winb_b = [Buf(f"winb{i}") for i in range(2)]
    C.woutb_b = [Buf(f"woutb{i}") for i in range(2)]
    C.st_b = Buf("st")
    C.wainb = dt("wainb", [D, 3072], BF16, kind="Internal").ap()
    C.waob = dt("waob", [KC, 128, D], BF16, kind="Internal").ap()
    C.whinb = dt("whinb", [D, 3 * D], BF16, kind="Internal").ap()
    C.whinb_b = Buf("whinb")
    NQ = 2 * NB
    C.hy_u = dt("hy_u", [NQ, 4, 128, KC * 512], BF16, kind="Internal").ap()
    C.hy_x1 = dt("hy_x1", [NQ, 4, 128, KC * 512], BF16, kind="Internal").ap()
    C.hy_x2T = dt("hy_x2T", [NQ, KC, 128, SEQ], BF16, kind="Internal").ap()
    C.hy_z2T = dt("hy_z2T", [NQ, KC, 128, SEQ], BF16, kind="Internal").ap()
    C.ktab = dt("hy_ktab", [2, 2, 2, 4, 128, KC * 512], F32, kind="Internal").ap()
    C.krs0 = dt("hy_krs0", [2, 2, 4, 128, 512], F32, kind="Internal").ap()
    C.hy_u_b = Buf("hy_u")
    C.hy_x1_b = Buf("hy_x1")
    C.hy_x2T_b = Buf("hy_x2T")
    C.hy_z2T_b = Buf("hy_z2T")
    C.ktab_b = [Buf("ktab0"), Buf("ktab1")]
    C.wainb_b = Buf("wainb")
    C.waob_b = Buf("waob")

    with ExitStack() as es:
        nc.allow_low_precision("bf16 matmuls with fp32 accumulation")
        S = Sched(nc, es)
        C.S = S
        sb = lambda name, shape, dtype: es.enter_context(C.sbuf(name, shape, dtype))
        C.ones_bf = sb("ones_bf", [128, 128], BF16)
        C.const_b = Buf("const")
        C.s_sb = sb("s_sb", [128, KC, NR], F32)
        C.s_b = Buf("s")
        C.modv = sb("modv", [128, NMOD * KC, NR], F32)
        C.modv_b = Buf("modv")
        C.gs = sb("gs", [128, 3, KC, NR], F32)
        C.gs_b = Buf("gs")
        C.hg = sb("hg", [128, 3, KC, NR], F32)
        C.hg_b = Buf("hg")

        S.op("dve", [], [C.const_b], lambda e: e.memset(C.ones_bf[:], 1.0 / D))
        C.eps_sb = sb("eps_sb", [128, 2], F32)
        S.op("dve", [], [C.const_b], lambda e: e.memset(C.eps_sb[:], EPS))
        S.dma("sp", C.s_sb[:], C.cT, [], [C.s_b])
        S.op("act", [C.s_b], [C.s_b], lambda e: e.activation(out=C.s_sb[:], in_=C.s_sb[:], func=AF.Silu))
        for k in range(KC):
            S.dma("pool", C.st[k * 128:(k + 1) * 128, :], C.xin[k * 128:(k + 1) * 128, :], [], [C.st_b], partial=True)

        lat_tiles = [(t * TT, t // (SEQ // TT)) for t in range(NLAT // TT)]
        ctx_tile = [(NLAT + i * TT, NB) for i in range(NB * CTX // TT)]
        for ph in phases:
            kind = ph[0]
            if kind == "conv":
                emit_convert_ffn(C, ph[1], ph[2], ph[3])
            elif kind == "mod":
                emit_mod(C, ph[1])
            elif kind == "hyena":
                emit_hyena(C, ph[1])
            elif kind == "attn":
                emit_attn(C, ph[1])
            elif kind == "ffn":
                _, l, w, slot, with_ctx, ntl = ph
                tiles = lat_tiles[:ntl] + (ctx_tile if with_ctx else [])
                emit_ffn(C, l, w, slot, tiles)
        S.barrier(include_bg=True)
        C.n_ops = S.nops
    return nc, C


def _const_tables():
    import ml_dtypes
    f32 = np.float32
    t = {}
    rows = SEQ // 64
    r, col = np.meshgrid(np.arange(rows), np.arange(64), indexing="ij")
    r = r.reshape(-1).astype(f32)
    col = col.reshape(-1).astype(f32)
    half = HD // 2
    inv = (f32(10000.0) ** (-np.arange(0, half, 2, dtype=f32) / f32(half))).astype(f32)
    ang = np.concatenate([r[:, None] * inv, col[:, None] * inv], axis=-1).astype(f32)
    cs = np.stack([np.cos(ang), np.sin(ang)]).astype(f32)
    cs = np.concatenate([cs, cs], axis=-1)
    t["rope_cs"] = np.ascontiguousarray(cs.transpose(0, 2, 1))
    rm = np.zeros((128, 128), f32)
    for m in range(64):
        rm[m + 64, m] = -1.0
        rm[m, m + 64] = 1.0
    t["rotm"] = rm.astype(ml_dtypes.bfloat16)
    kj = np.arange(128)[:, None]
    qi = np.arange(128)[None, :]
    t["wmask"] = np.ascontiguousarray(np.stack([(kj >= qi), (kj <= qi)], axis=1).astype(f32)).astype(ml_dtypes.bfloat16)
    return t


_HY_CACHE = {}


def _hyena_tables(n):
    if n in _HY_CACHE:
        return _HY_CACHE[n]
    import ml_dtypes
    f32 = np.float32
    bf = ml_dtypes.bfloat16
    t = {}
    nch = n // 128
    N = 2 * n
    tt = np.linspace(0.0, 1.0, n, dtype=f32)
    w = (f32(2.0 * math.pi / n) * np.arange(n, dtype=f32))[:, None]
    bands = np.linspace(1e-4, 15, 16, dtype=f32)
    ang = (w * bands[None, :]).astype(f32)
    feats = np.concatenate([tt[:, None], np.cos(ang), -np.sin(ang)], axis=-1).astype(f32)
    t[f"hy_feats{n}"] = np.ascontiguousarray(feats.T)
    t[f"hy_negt{n}"] = np.ascontiguousarray((-tt).reshape(nch, 128).T)
    wfv = np.full(n, 2.0 / N, f32)
    wfv[0] = 1.0 / N
    wf = wfv.reshape(nch, 128).T
    t[f"hy_wf{n}"] = np.ascontiguousarray(np.stack([wf, -wf], axis=1))
    max_decay = math.log(1e-2) / 0.3
    min_decay = math.log(1e-2) / 1.5
    deltas = np.abs(np.linspace(min_decay, max_decay, D, dtype=f32))
    t["hy_absdelta"] = np.ascontiguousarray(np.broadcast_to(deltas[None, :], (128, D)))
    idx = np.arange(n, dtype=np.int64)
    prod = (idx[:, None] * idx[None, :]) % N
    angm = prod.astype(np.float64) * (2.0 * math.pi / N)
    Cm = np.cos(angm)
    Sm = np.sin(angm)
    Sm[0, :] = np.where(idx % 2 == 0, 1.0, -1.0)
    Cb = Cm.astype(f32).astype(bf)
    Sb = Sm.astype(f32).astype(bf)
    C4 = Cb.reshape(nch, 128, nch, 128)
    S4 = Sb.reshape(nch, 128, nch, 128)
    t[f"dft_cf{n}"] = np.ascontiguousarray(C4.transpose(2, 1, 0, 3)).reshape(nch, 128, nch * 128)
    t[f"dft_sf{n}"] = np.ascontiguousarray(S4.transpose(0, 3, 2, 1)).reshape(nch, 128, nch * 128)
    t[f"dft_si{n}"] = np.ascontiguousarray(S4.transpose(2, 1, 0, 3)).reshape(nch, 128, nch * 128)
    t[f"dft_cr{n}"] = Cb
    t[f"dft_sr{n}"] = Sb
    t["ident_bf"] = np.eye(128, dtype=f32).astype(bf)
    _HY_CACHE[n] = t
    return t


def _prep_inputs(inputs, layers=range(DEPTH)):
    x = inputs["x"]
    ctx = inputs["ctx"]
    c = inputs["c"]
    c_ctx = inputs["c_ctx"]
    shared = {}
    for l in layers:
        shared[f"w_mod{l}"] = inputs["w_mod"][l]
        shared[f"b_mod{l}"] = np.ascontiguousarray(inputs["b_mod"][l].reshape(NMOD * KC, 128).T)
        shared[f"g_norm{l}"] = np.ascontiguousarray(inputs["g_norm"][l].reshape(3, KC, 128).transpose(2, 0, 1))
        for w in range(2):
            shared[f"w_ffn_in{l}_{w}"] = inputs["w_ffn_in"][l, w]
            shared[f"w_ffn_out{l}_{w}"] = inputs["w_ffn_out"][l, w]
    perm = np.concatenate([np.arange(0, HD, 2), np.arange(1, HD, 2)])
    for l in layers:
        if l % 2 == 0:
            i = l // 2
            wi = inputs["w_attn_in"][i]
            qcols = wi[:, :2048].reshape(D, 16, HD)[:, :, perm].reshape(D, 2048)
            kA = wi[:, 2048:2304].reshape(D, 2, HD)[:, :, perm].reshape(D, 256)
            vA = wi[:, 2304:2560]
            kB = wi[:, 2560:2816].reshape(D, 2, HD)[:, :, perm].reshape(D, 256)
            vB = wi[:, 2816:3072]
            shared[f"w_attn_in{i}"] = np.ascontiguousarray(np.concatenate([qcols, kA, kB, vA, vB], axis=1))
            shared[f"w_attn_out{i}"] = inputs["w_attn_out"][i]
            gq = inputs["g_q"][i][:, perm]
            gk = inputs["g_k"][i][:, perm]
            shared[f"gqk{i}"] = np.ascontiguousarray(np.stack([gq[0], gq[1], gk[0], gk[1]], axis=1))
            shared[f"sink{i}"] = np.ascontiguousarray(np.broadcast_to(inputs["sink"][i][None, :], (128, 8)))
    for l in layers:
        if l % 2 == 1:
            i = l // 2
            shared[f"w_hy_in{i}"] = inputs["w_hy_in"][i]
            shared[f"w_hy_out{i}"] = inputs["w_hy_out"][i]
            shared[f"hf_w1_{i}"] = inputs["hf_w1"][i]
            shared[f"hf_w2_{i}"] = inputs["hf_w2"][i]
            shared[f"hf_w3_{i}"] = inputs["hf_w3"][i]
            shared[f"hf_vec_{i}"] = np.ascontiguousarray(np.stack(
                [inputs["hf_b1"][i], inputs["hf_freq1"][i], inputs["hf_b2"][i], inputs["hf_freq2"][i]], axis=1))
            shared[f"hy_bias_{i}"] = np.ascontiguousarray(np.broadcast_to(inputs["hy_bias"][i][None], (128, 2, D)))
            shared[f"hy_bin_{i}"] = np.ascontiguousarray(inputs["b_hy_in"][i].reshape(48, 128).T)
            shared[f"hy_wc_{i}"] = np.ascontiguousarray(inputs["w_hy_conv"][i].reshape(3, 48, 128).transpose(2, 0, 1))
            shared[f"hy_bc_{i}"] = np.ascontiguousarray(inputs["b_hy_conv"][i].reshape(48, 128).T)
            shared[f"hy_bout_{i}"] = np.ascontiguousarray(inputs["b_hy_out"][i].reshape(KC, 128).T)
    shared.update(_const_tables())
    shared.update(_hyena_tables(SEQ))
    if 1 in layers:
        shared.update(_hyena_tables(CTX))
    maps = []
    for core in range(NCORES):
        b0 = core * NB
        xin = np.empty((D, NTOK), np.float32)
        for be in range(NB):
            xin[:, be * SEQ:(be + 1) * SEQ] = x[b0 + be].T
            xin[:, NLAT + be * CTX:NLAT + (be + 1) * CTX] = ctx[b0 + be].T
        cT = np.zeros((128, KC, NR), np.float32)
        for be in range(NB):
            cT[:, :, be] = c[b0 + be].reshape(KC, 128).T
        cT[:, :, NB] = c_ctx.reshape(KC, 128).T
        m = dict(shared)
        m["xin"] = xin
        m["cT"] = cT
        maps.append(m)
    return maps


def full_phases(ntl=NB * 4):
    ph = []
    ph.append(("conv", 0, 0, 0))
    for l in range(DEPTH):
        ph.append(("mod", l))
        ph.append(("conv", l, 1, 1))
        ph.append(("ffn", l, 0, 0, l <= 2, ntl))
        if l + 1 < DEPTH:
            ph.append(("conv", l + 1, 0, 0))
        ph.append(("attn" if l % 2 == 0 else "hyena", l))
        ph.append(("ffn", l, 1, 1, l < 2, ntl))
    return ph


def kernel(**inputs):
    maps = _prep_inputs(inputs)
    nc, C = build_program(full_phases())
    names = list(C.decl.keys()) + ["xin", "cT"]
    res = run_bass_kernel_spmd(nc, [{k: m[k] for k in names} for m in maps], core_ids=list(range(NCORES)))
    out = np.empty((16, SEQ, D), np.float32)
    for core in range(NCORES):
        st = res.results[core]["st"]
        for be in range(NB):
            out[core * NB + be] = st[:, be * SEQ:(be + 1) * SEQ].T
    return out
```

```python
import math
from contextlib import ExitStack
import numpy as np
import concourse.bass as bass
import concourse.mybir as mybir
from concourse.bass_utils import run_bass_kernel_spmd

F32 = mybir.dt.float32
BF16 = mybir.dt.bfloat16
ALU = mybir.AluOpType
AF = mybir.ActivationFunctionType
AX = mybir.AxisListType

D = 2048
KC = D // 128
SEQ = 2048
CTX = 256
NCORES = 8
NB = 16 // NCORES
NR = 8
NLAT = NB * SEQ
NTOK = NLAT + NB * CTX
DFF = 5632
FC = DFF // 128
NMOD = 9
DEPTH = 4
HD = 128
EPS = 1e-6
TT = 512


class Buf:
    __slots__ = ("name", "w", "r")

    def __init__(self, name):
        self.name = name
        self.w = {}
        self.r = {}


class Sched:
    def __init__(self, nc, es, ndma=12):
        self.nc = nc
        self.E = {"pe": nc.tensor, "act": nc.scalar, "dve": nc.vector, "pool": nc.gpsimd, "sp": nc.sync}
        self.sems = {}
        self.cnt = {}
        self.waited = {e: {} for e in self.E}
        self.es = es
        for e in ("pe", "act", "dve", "pool"):
            self._mk("c_" + e)
        self.dq = {}
        for q in ("sp", "pool", "bg", "act"):
            self.dq[q] = [0, [self._mk(f"d_{q}_{i}") for i in range(ndma)]]
        self.nops = 0

    def _mk(self, name):
        self.sems[name] = self.es.enter_context(self.nc.semaphore(name))
        self.cnt[name] = 0
        return name

    def _deps(self, reads, writes):
        deps = {}
        for b in reads:
            for s, v in b.w.items():
                if deps.get(s, 0) < v:
                    deps[s] = v
        for b in writes:
            for s, v in b.w.items():
                if deps.get(s, 0) < v:
                    deps[s] = v
            for s, v in b.r.items():
                if deps.get(s, 0) < v:
                    deps[s] = v
        return deps

    def _wait(self, e, deps, skip=None):
        eng = self.E[e]
        wd = self.waited[e]
        for s, v in deps.items():
            if s == skip:
                continue
            if wd.get(s, 0) < v:
                eng.wait_ge(self.sems[s], v)
                wd[s] = v

    def op(self, e, reads, writes, fn, self_dep=True):
        own = "c_" + e
        deps = self._deps(reads, writes)
        self._wait(e, deps, skip=own if (e == "pe" or not self_dep) else None)
        inst = fn(self.E[e])
        self.cnt[own] += 1
        v = self.cnt[own]
        inst.then_inc(self.sems[own], 1)
        for b in reads:
            b.r[own] = v
        for b in writes:
            b.w = {own: v}
            b.r = {}
        self.nops += 1
        return inst

    def dma(self, q, out, in_, reads, writes, partial=False):
        e = q if q in ("sp", "act") else "pool"
        st = self.dq[q]
        name = st[1][st[0] % len(st[1])]
        st[0] += 1
        deps = self._deps(reads, writes)
        prev = self.cnt[name]
        if prev and deps.get(name, 0) < prev:
            deps[name] = prev
        self._wait(e, deps)
        inst = self.E[e].dma_start(out=out, in_=in_)
        self.cnt[name] += 16
        v = self.cnt[name]
        inst.then_inc(self.sems[name], 16)
        for b in reads:
            b.r[name] = v
        for b in writes:
            if partial:
                b.w[name] = v
            else:
                b.w = {name: v}
                b.r = {}
        self.nops += 1
        return inst

    def barrier(self, include_bg=False):
        deps = {}
        for s, v in self.cnt.items():
            if v and (include_bg or not s.startswith("d_bg")):
                deps[s] = v
        for e in self.E:
            self._wait(e, deps)


class Ctx:
    def __init__(self):
        self.decl = {}
        self.dbg = {}
        self.uid = 0

    def sbuf(self, name, shape, dtype):
        self.uid += 1
        return self.nc.sbuf_tensor(f"{name}_u{self.uid}", shape, dtype)

    def psum(self, name, shape, dtype):
        self.uid += 1
        return self.nc.psum_tensor(f"{name}_u{self.uid}", shape, dtype)

    def win(self, name, shape, dtype=F32):
        if name not in self.decl:
            self.decl[name] = self.nc.dram_tensor(name, list(shape), dtype, kind="ExternalInput").ap()
        return self.decl[name]

    def dump(self, name, src_ap, shape, dtype, reads):
        if not self.debug:
            return
        t = self.nc.dram_tensor("dbg_" + name, list(shape), dtype, kind="ExternalOutput").ap()
        self.dbg[name] = t
        self.S.dma("sp", t, src_ap, reads, [Buf("dbg_" + name)])


def _mm(S, out_ap, lhsT, rhs, start, stop, reads, writes):
    return S.op("pe", reads, writes, lambda e: e.matmul(out_ap, lhsT, rhs, start=start, stop=stop))


def emit_mod(C, l):
    nc, S = C.nc, C.S
    NCH = NMOD * KC
    GRP = 4
    NG = NCH // GRP
    with ExitStack() as es:
        wsl = [es.enter_context(C.sbuf(f"modw{i}", [128, KC, GRP * 128], BF16)) for i in range(2)]
        wb = [Buf(f"modw{i}") for i in range(2)]
        ps = es.enter_context(C.psum("modps", [128, 3, 512], F32))
        psb = Buf("modps")
        bm = es.enter_context(C.sbuf("modb", [128, NCH], F32))
        bmb = Buf("modb")
        gn = es.enter_context(C.sbuf("modg", [128, 3, KC], F32))
        gnb = Buf("modg")
        S.dma("sp", bm[:], C.win(f"b_mod{l}", [128, NMOD * KC]), [], [bmb])
        S.dma("sp", gn[:], C.win(f"g_norm{l}", [128, 3, KC]), [], [gnb])
        wsrc = C.win(f"w_mod{l}", [D, NMOD * D]).rearrange("(k p) n -> p k n", p=128)

        def load(g):
            S.dma("pool", wsl[g % 2][:], wsrc[:, :, g * 512:(g + 1) * 512], [], [wb[g % 2]])

        load(0)
        for g in range(NG):
            if g + 1 < NG:
                load(g + 1)
            for jj in range(GRP):
                n = g * GRP + jj
                o = ps[:, n // 64, (n % 64) * NR:(n % 64) * NR + NR]
                for k in range(KC):
                    _mm(S, o, wsl[g % 2][:, k, jj * 128:(jj + 1) * 128], C.s_bf[:, k, :], k == 0, k == KC - 1,
                        [wb[g % 2], C.s_b], [psb])
        for r in range(NB + 1):
            for hb in range(3):
                n0 = hb * 64
                cn = min(64, NCH - n0)
                S.op("dve", [psb, bmb], [C.modv_b],
                     lambda e, r=r, hb=hb, n0=n0, cn=cn: e.tensor_tensor(
                         C.modv[:, n0:n0 + cn, r],
                         ps[:, hb, 0:cn * NR].rearrange("p (n r) -> p n r", r=NR)[:, :, r],
                         bm[:, n0:n0 + cn], ALU.add), self_dep=False)
        for k in range(3):
            for r in range(NB + 1):
                S.op("dve", [C.modv_b, gnb], [C.gs_b],
                     lambda e, k=k, r=r: e.scalar_tensor_tensor(
                         C.gs[:, k, :, r], C.modv[:, (3 * k + 1) * KC:(3 * k + 2) * KC, r], 1.0, gn[:, k, :],
                         ALU.add, ALU.mult), self_dep=(k == 0 and r == 0))
            S.op("dve", [C.modv_b], [C.hg_b],
                 lambda e, k=k: e.tensor_scalar(
                     C.hg[:, k, :, :], C.modv[:, (3 * k + 2) * KC:(3 * k + 3) * KC, :],
                     0.5 if k != 1 else 1.0, None, ALU.mult), self_dep=(k == 0))
        C.dump(f"modv{l}", C.modv[:], [128, NMOD * KC, NR], F32, [C.modv_b])
        C.dump(f"gs{l}", C.gs[:], [128, 3, KC, NR], F32, [C.gs_b])
        C.dump(f"hg{l}", C.hg[:], [128, 3, KC, NR], F32, [C.hg_b])
        S.barrier()


def emit_convert_ffn(C, l, w, slot):
    S = C.S
    wi = C.win(f"w_ffn_in{l}_{w}", [D, 2 * DFF])
    wo = C.win(f"w_ffn_out{l}_{w}", [DFF, D])
    for k in range(KC):
        S.dma("bg", C.winb[slot][k * 128:(k + 1) * 128, :], wi[k * 128:(k + 1) * 128, :], [], [C.winb_b[slot]],
              partial=True)
    wo_v = wo.rearrange("(j p) (m c) -> m p j c", p=128, c=128)
    for m in range(KC):
        S.dma("bg", C.woutb[slot][m].rearrange("p (j c) -> p j c", c=128), wo_v[m], [], [C.woutb_b[slot]], partial=True)


def emit_ffn(C, l, w, slot, tiles):
    nc, S = C.nc, C.S
    k_mod = 0 if w == 0 else 2
    JG = 4
    NJG = FC // JG
    winv = C.winb[slot].rearrange("(k p) n -> p k n", p=128)
    with ExitStack() as es:
        x = es.enter_context(C.sbuf("ffn_x", [128, KC, TT], F32))
        xb = [Buf(f"ffn_x{k}") for k in range(KC)]
        h = es.enter_context(C.sbuf("ffn_h", [128, KC, TT], BF16))
        hb = Buf("ffn_h")
        a = es.enter_context(C.sbuf("ffn_a", [128, FC, TT], BF16))
        ab = Buf("ffn_a")
        wins = [es.enter_context(C.sbuf(f"ffn_wi{i}", [128, 2, KC, JG * 128], BF16)) for i in range(2)]
        winb = [Buf(f"ffn_wi{i}") for i in range(2)]
        wouts = [es.enter_context(C.sbuf(f"ffn_wo{i}", [128, FC, 128], BF16)) for i in range(2)]
        woutb = [Buf(f"ffn_wo{i}") for i in range(2)]
        sq = [es.enter_context(C.sbuf(f"ffn_sq{i}", [128, TT], BF16)) for i in range(2)]
        sqb = [Buf(f"ffn_sq{i}") for i in range(2)]
        tmp = [es.enter_context(C.sbuf(f"ffn_tmp{i}", [128, TT], F32)) for i in range(2)]
        tmpb = [Buf(f"ffn_tmp{i}") for i in range(2)]
        sg = [es.enter_context(C.sbuf(f"ffn_sg{i}", [128, TT], F32)) for i in range(2)]
        sgb = [Buf(f"ffn_sg{i}") for i in range(2)]
        rstd = es.enter_context(C.sbuf("ffn_rstd", [128, TT], F32))
        rstdb = Buf("ffn_rstd")
        psn = es.enter_context(C.psum("ffn_psn", [128, TT], F32))
        psnb = Buf("ffn_psn")
        psg = [es.enter_context(C.psum(f"ffn_psg{i}", [128, TT], F32)) for i in range(2)]
        psgb = [Buf(f"ffn_psg{i}") for i in range(2)]
        psu = [es.enter_context(C.psum(f"ffn_psu{i}", [128, TT], F32)) for i in range(2)]
        psub = [Buf(f"ffn_psu{i}") for i in range(2)]
        psy = [es.enter_context(C.psum(f"ffn_psy{i}", [128, TT], F32)) for i in range(2)]
        psyb = [Buf(f"ffn_psy{i}") for i in range(2)]

        stv = C.st.rearrange("(k p) t -> p k t", p=128)

        tasks = []
        for ti in range(len(tiles)):
            for jg in range(NJG):
                tasks.append(("in", jg))
            for m in range(KC):
                tasks.append(("out", m))
        issued = [0]
        cnts = {"in": 0, "out": 0}
        slot_of = {}

        def ensure(upto):
            while issued[0] <= min(upto, len(tasks) - 1):
                i = issued[0]
                kind, idx = tasks[i]
                sl = cnts[kind] % 2
                cnts[kind] += 1
                slot_of[i] = sl
                if kind == "in":
                    for gu in range(2):
                        c0 = gu * DFF + idx * JG * 128
                        S.dma("sp", wins[sl][:, gu, :, :], winv[:, :, c0:c0 + JG * 128], [C.winb_b[slot]], [winb[sl]],
                              partial=(gu == 1))
                else:
                    S.dma("sp", wouts[sl][:], C.woutb[slot][idx].rearrange("p (j c) -> p j c", c=128),
                          [C.woutb_b[slot]], [woutb[sl]])
                issued[0] += 1

        def load_x(ti, k):
            col0 = tiles[ti][0]
            S.dma("act", x[:, k, :], stv[:, k, col0:col0 + TT], [C.st_b], [xb[k]])

        for k in range(KC):
            load_x(0, k)
        ensure(1)
        tcount = 0
        nsq = 0
        ntmp = 0
        nps = 0
        npy = 0
        for ti, (col0, r) in enumerate(tiles):
            for k in range(KC):
                i = nsq % 2
                nsq += 1
                S.op("act", [xb[k]], [sqb[i]], lambda e, k=k, i=i: e.activation(out=sq[i][:], in_=x[:, k, :], func=AF.Square))
                _mm(S, psn[:], C.ones_bf[:], sq[i][:], k == 0, k == KC - 1, [sqb[i], C.const_b], [psnb])
            S.op("act", [psnb], [rstdb], lambda e: e.activation(out=rstd[:], in_=psn[:], func=AF.Ln, bias=C.eps_sb[:, 0:1], scale=1.0))
            S.op("act", [rstdb], [rstdb], lambda e: e.activation(out=rstd[:], in_=rstd[:], func=AF.Exp, scale=-0.5))
            for k in range(KC):
                i = ntmp % 2
                ntmp += 1
                S.op("dve", [xb[k], rstdb, C.gs_b], [tmpb[i]],
                     lambda e, k=k, i=i: e.scalar_tensor_tensor(tmp[i][:], x[:, k, :], C.gs[:, k_mod, k, r:r + 1], rstd[:],
                                                                ALU.mult, ALU.mult))
                S.op("act", [tmpb[i], C.modv_b], [hb],
                     lambda e, k=k, i=i: e.activation(out=h[:, k, :], in_=tmp[i][:], func=AF.Identity,
                                                      bias=C.modv[:, (3 * k_mod) * KC + k, r:r + 1], scale=1.0),
                     self_dep=False)
            if ti == 0:
                C.dump(f"rstd{l}{w}", rstd[:], [128, TT], F32, [rstdb])
                C.dump(f"h{l}{w}", h[:], [128, KC, TT], BF16, [hb])
            for jg in range(NJG):
                ensure(tcount + 1)
                sl = slot_of[tcount]
                tcount += 1
                if ti == 0 and jg == 0:
                    C.dump(f"wins{l}{w}", wins[sl][:], [128, 2, KC, JG * 128], BF16, [winb[sl]])
                for jj in range(JG):
                    j = jg * JG + jj
                    pi = nps % 2
                    nps += 1
                    for k in range(KC):
                        _mm(S, psg[pi][:], wins[sl][:, 0, k, jj * 128:(jj + 1) * 128], h[:, k, :], k == 0, k == KC - 1,
                            [winb[sl], hb], [psgb[pi]])
                    for k in range(KC):
                        _mm(S, psu[pi][:], wins[sl][:, 1, k, jj * 128:(jj + 1) * 128], h[:, k, :], k == 0, k == KC - 1,
                            [winb[sl], hb], [psub[pi]])
                    S.op("act", [psgb[pi]], [sgb[pi]], lambda e, pi=pi: e.activation(out=sg[pi][:], in_=psg[pi][:], func=AF.Silu))
                    if ti == 0 and j == 0:
                        C.dump(f"sg{l}{w}", sg[pi][:], [128, TT], F32, [sgb[pi]])
                    S.op("dve", [sgb[pi], psub[pi]], [ab],
                         lambda e, pi=pi, j=j: e.tensor_tensor(a[:, j, :], sg[pi][:], psu[pi][:], ALU.mult), self_dep=False)
            if ti == 0:
                C.dump(f"a{l}{w}", a[:], [128, FC, TT], BF16, [ab])
            for m in range(KC):
                ensure(tcount + 1)
                sl = slot_of[tcount]
                tcount += 1
                pi = npy % 2
                npy += 1
                for j in range(FC):
                    _mm(S, psy[pi][:], wouts[sl][:, j, :], a[:, j, :], j == 0, j == FC - 1, [woutb[sl], ab], [psyb[pi]])
                S.op("dve", [psyb[pi], xb[m], C.hg_b], [xb[m]],
                     lambda e, pi=pi, m=m: e.scalar_tensor_tensor(x[:, m, :], psy[pi][:], C.hg[:, k_mod, m, r:r + 1], x[:, m, :],
                                                                  ALU.mult, ALU.add))
                S.dma("act", stv[:, m, col0:col0 + TT], x[:, m, :], [xb[m]], [C.st_b], partial=True)
                if ti + 1 < len(tiles):
                    load_x(ti + 1, m)
        S.barrier()


class NormKit:
    def __init__(self, C, es, pfx):
        nc = C.nc
        sb = lambda n, sh, dt_: es.enter_context(C.sbuf(f"{pfx}_{n}", sh, dt_))
        self.xk = [sb(f"xk{i}", [128, TT], F32) for i in range(3)]
        self.xkb = [Buf(f"xk{i}") for i in range(3)]
        self.sq = [sb(f"sq{i}", [128, TT], BF16) for i in range(2)]
        self.sqb = [Buf(f"sq{i}") for i in range(2)]
        self.tmp = [sb(f"tmp{i}", [128, TT], F32) for i in range(2)]
        self.tmpb = [Buf(f"tmp{i}") for i in range(2)]
        self.rstd = sb("rstd", [128, TT], F32)
        self.rstdb = Buf("rstd")
        self.psn = es.enter_context(C.psum(f"{pfx}_psn", [128, TT], F32))
        self.psnb = Buf("psn")
        self.nx = 0
        self.ns = 0
        self.nt = 0

    def emit(self, C, col0, n, r, k_mod, h, hb, hoff=0):
        S = C.S
        stv = C.st.rearrange("(k p) t -> p k t", p=128)
        for k in range(KC):
            i = self.nx % 3
            self.nx += 1
            S.dma("pool", self.xk[i][:, :n], stv[:, k, col0:col0 + n], [C.st_b], [self.xkb[i]])
            j = self.ns % 2
            self.ns += 1
            S.op("act", [self.xkb[i]], [self.sqb[j]],
                 lambda e, i=i, j=j: e.activation(out=self.sq[j][:, :n], in_=self.xk[i][:, :n], func=AF.Square))
            _mm(S, self.psn[:, :n], C.ones_bf[:], self.sq[j][:, :n], k == 0, k == KC - 1, [self.sqb[j], C.const_b], [self.psnb])
        S.op("act", [self.psnb], [self.rstdb],
             lambda e: e.activation(out=self.rstd[:, :n], in_=self.psn[:, :n], func=AF.Ln, bias=C.eps_sb[:, 0:1], scale=1.0))
        S.op("act", [self.rstdb], [self.rstdb], lambda e: e.activation(out=self.rstd[:, :n], in_=self.rstd[:, :n], func=AF.Exp, scale=-0.5))
        for k in range(KC):
            i = self.nx % 3
            self.nx += 1
            S.dma("pool", self.xk[i][:, :n], stv[:, k, col0:col0 + n], [C.st_b], [self.xkb[i]])
            j = self.nt % 2
            self.nt += 1
            S.op("dve", [self.xkb[i], self.rstdb, C.gs_b], [self.tmpb[j]],
                 lambda e, i=i, j=j, k=k: e.scalar_tensor_tensor(self.tmp[j][:, :n], self.xk[i][:, :n], C.gs[:, k_mod, k, r:r + 1],
                                                                 self.rstd[:, :n], ALU.mult, ALU.mult))
            S.op("act", [self.tmpb[j], C.modv_b], [hb],
                 lambda e, j=j, k=k: e.activation(out=h[:, k, hoff:hoff + n], in_=self.tmp[j][:, :n], func=AF.Identity,
                                                  bias=C.modv[:, (3 * k_mod) * KC + k, r:r + 1], scale=1.0), self_dep=False)


class WStream:
    def __init__(self, S, slots, seq, src_buf, q="sp"):
        self.S = S
        self.slots = slots
        self.bufs = [Buf(f"ws{i}") for i in range(len(slots))]
        self.seq = seq
        self.src_buf = src_buf
        self.q = q
        self.issued = 0
        self.i = 0

    def _issue(self):
        k = self.issued
        item = self.seq[k]
        if not isinstance(item, list):
            item = [item]
        sl = k % 2
        for ii, (src, dst_fn) in enumerate(item):
            self.S.dma(self.q, dst_fn(self.slots[sl]), src, [self.src_buf], [self.bufs[sl]], partial=(ii > 0))
        self.issued += 1

    def get(self):
        while self.issued <= min(self.i + 1, len(self.seq) - 1):
            self._issue()
        sl = self.i % 2
        self.i += 1
        return sl


A_QW = 2048
A_KOFF = 2048
A_VOFF = 2560
NKT = (CTX + SEQ) // 128


def emit_attn(C, l):
    nc, S = C.nc, C.S
    i = l // 2
    ctx_out = (l == 0)
    w_in = C.win(f"w_attn_in{i}", [D, 3072])
    w_out = C.win(f"w_attn_out{i}", [D, D])
    gqk_d = C.win(f"gqk{i}", [128, 4])
    sink_d = C.win(f"sink{i}", [128, 8])
    rope_d = C.win("rope_cs", [2, 128, SEQ])
    rotm_d = C.win("rotm", [128, 128], BF16)
    mask_d = C.win("wmask", [128, 2, 128], BF16)
    for k in range(KC):
        S.dma("bg", C.wainb[k * 128:(k + 1) * 128, :], w_in[k * 128:(k + 1) * 128, :], [], [C.wainb_b], partial=True)
    wo_v = w_out.rearrange("(j p) (m c) -> m p j c", p=128, c=128)
    for m in range(KC):
        S.dma("bg", C.waob[m].rearrange("p (j c) -> p j c", c=128), wo_v[m], [], [C.waob_b], partial=True)
    wainv = C.wainb.rearrange("(k p) n -> p k n", p=128)
    stv = C.st.rearrange("(k p) t -> p k t", p=128)
    SCALE = float(HD) ** -0.5

    with ExitStack() as es:
        sbt = lambda n, sh, dt_: es.enter_context(C.sbuf("at_" + n, sh, dt_))
        pst = lambda n: es.enter_context(C.psum("at_" + n, [128, TT], F32))
        kit = NormKit(C, es, "at")
        h = sbt("h", [128, KC, TT], BF16)
        hb = Buf("h")
        KT = [sbt(f"kt{s_}", [128, 2, CTX + SEQ], BF16) for s_ in range(2)]
        KTb = [Buf(f"kt{s_}") for s_ in range(2)]
        V = sbt("v", [128, NKT, 512], BF16)
        Vb = Buf("v")
        OT = sbt("ot", [128, 16, TT], BF16)
        OTb = Buf("ot")
        wsl = [sbt(f"w{j}", [128, KC, 512], BF16) for j in range(2)]
        wos = [sbt(f"wo{j}", [128, 16, 128], BF16) for j in range(2)]
        ropeC = sbt("ropec", [128, SEQ], F32)
        ropeS = sbt("ropes", [128, SEQ], F32)
        rotm = sbt("rotm", [128, 128], BF16)
        masks = sbt("masks", [128, 2, 128], BF16)
        gqk = sbt("gqk", [128, 4], F32)
        esink = sbt("esink", [128, 8], F32)
        ones1 = sbt("ones1", [128, 128], BF16)
        onesh = sbt("onesh", [128, 128], BF16)
        cb = Buf("at_const")
        S.dma("sp", ropeC[:], rope_d[0], [], [cb])
        S.dma("sp", ropeS[:], rope_d[1], [], [cb], partial=True)
        S.dma("sp", rotm[:], rotm_d, [], [cb], partial=True)
        S.dma("sp", masks[:], mask_d, [], [cb], partial=True)
        S.dma("sp", gqk[:], gqk_d, [], [cb], partial=True)
        S.dma("sp", esink[:], sink_d, [], [cb], partial=True)
        S.op("act", [cb], [cb], lambda e: e.activation(out=esink[:], in_=esink[:], func=AF.Exp))
        S.op("dve", [cb], [cb], lambda e: e.memset(ones1[:], 1.0))
        S.op("dve", [cb], [cb], lambda e: e.memset(onesh[:], 1.0 / HD))
        sqh = [sbt(f"sqh{j}", [128, TT], BF16) for j in range(2)]
        sqhb = [Buf(f"sqh{j}") for j in range(2)]
        rs = sbt("rs", [128, TT], F32)
        rsb = Buf("rs")
        tgb = [sbt(f"tgb{j}", [128, TT], BF16) for j in range(2)]
        tgbb = [Buf(f"tgb{j}") for j in range(2)]
        u1 = sbt("u1", [128, TT], F32)
        u1b = Buf("u1")
        u2 = sbt("u2", [128, TT], F32)
        u2b = Buf("u2")
        qb_ = [sbt(f"q{j}", [128, TT], BF16) for j in range(2)]
        qbb = [Buf(f"q{j}") for j in range(2)]
        pT = [sbt(f"pT{j}", [128, TT], BF16) for j in range(2)]
        pTb = [Buf(f"pT{j}") for j in range(2)]
        rden = sbt("rden", [128, TT], F32)
        rdenb = Buf("rden")
        xres = [sbt(f"xres{j}", [128, TT], F32) for j in range(2)]
        xresb = [Buf(f"xres{j}") for j in range(2)]
        pt = [pst(f"pt{j}") for j in range(2)]
        ptb = [Buf(f"pt{j}") for j in range(2)]
        pr = pst("pr")
        prb = Buf("pr")
        ps = [pst(f"ps{j}") for j in range(2)]
        psb = [Buf(f"ps{j}") for j in range(2)]
        po = pst("po")
        pob = Buf("po")
        pd = pst("pd")
        pdb = Buf("pd")
        cnt = {"pt": 0, "sqh": 0, "tgb": 0, "q": 0, "ps": 0, "xres": 0}

        def rot(name, nbuf=2):
            v = cnt[name] % nbuf
            cnt[name] += 1
            return v

        def p1_tiles(be):
            t = [(NLAT + be * CTX, CTX, NB, 0, None)]
            for q in range(SEQ // TT):
                t.append((be * SEQ + q * TT, TT, be, CTX + q * TT, q * TT))
            return t

        def p2_tiles(be):
            t = []
            if ctx_out:
                t.append((NLAT + be * CTX, CTX, NB, None, None))
            for q in range(SEQ // TT):
                t.append((be * SEQ + q * TT, TT, be, q * TT, q))
            return t

        in_seq = []
        out_seq = []
        for be in C.be_list:
            for _ in p1_tiles(be):
                for c in range(4):
                    in_seq.append((wainv[:, :, A_KOFF + c * 128:A_KOFF + (c + 1) * 128], lambda sl: sl[:, :, 0:128]))
                in_seq.append((wainv[:, :, A_VOFF:A_VOFF + 512], lambda sl: sl[:, :, :]))
            for _ in p2_tiles(be):
                for hd in range(16):
                    in_seq.append((wainv[:, :, hd * 128:(hd + 1) * 128], lambda sl: sl[:, :, 0:128]))
                for m in range(KC):
                    out_seq.append((C.waob[m].rearrange("p (j c) -> p j c", c=128), lambda sl: sl[:, :, :]))
        win_s = WStream(S, wsl, in_seq, C.wainb_b)
        wo_s = WStream(S, wos, out_seq, C.waob_b)

        def qk_finish(pi, n, gcol, rope_off, dst_ap, dst_buf):
            j = rot("sqh")
            S.op("act", [ptb[pi]], [sqhb[j]], lambda e: e.activation(out=sqh[j][:, :n], in_=pt[pi][:, :n], func=AF.Square))
            _mm(S, kit.psn[:, :n], onesh[:], sqh[j][:, :n], True, True, [sqhb[j], cb], [kit.psnb])
            S.op("act", [kit.psnb], [rsb],
                 lambda e: e.activation(out=rs[:, :n], in_=kit.psn[:, :n], func=AF.Ln, bias=C.eps_sb[:, 0:1], scale=1.0))
            S.op("act", [rsb], [rsb], lambda e: e.activation(out=rs[:, :n], in_=rs[:, :n], func=AF.Exp, scale=-0.5))
            if rope_off is None:
                S.op("dve", [ptb[pi], rsb, cb], [dst_buf],
                     lambda e: e.scalar_tensor_tensor(dst_ap, pt[pi][:, :n], gqk[:, gcol:gcol + 1], rs[:, :n], ALU.mult, ALU.mult))
                return
            t = rot("tgb")
            S.op("dve", [ptb[pi], rsb, cb], [tgbb[t]],
                 lambda e: e.scalar_tensor_tensor(tgb[t][:, :n], pt[pi][:, :n], gqk[:, gcol:gcol + 1], rs[:, :n], ALU.mult, ALU.mult))
            _mm(S, pr[:, :n], rotm[:], tgb[t][:, :n], True, True, [tgbb[t], cb], [prb])
            S.op("dve", [tgbb[t], cb], [u1b], lambda e: e.tensor_tensor(u1[:, :n], tgb[t][:, :n], ropeC[:, rope_off:rope_off + n], ALU.mult))
            S.op("dve", [prb, cb], [u2b], lambda e: e.tensor_tensor(u2[:, :n], pr[:, :n], ropeS[:, rope_off:rope_off + n], ALU.mult))
            S.op("dve", [u1b, u2b], [dst_buf], lambda e: e.tensor_tensor(dst_ap, u1[:, :n], u2[:, :n], ALU.add))

        for be in C.be_list:
            for (col0, n, r, key_off, rope_off) in p1_tiles(be):
                kit.emit(C, col0, n, r, 1, h, hb)
                for c in range(4):
                    sl = win_s.get()
                    pi = rot("pt")
                    for k in range(KC):
                        _mm(S, pt[pi][:, :n], wsl[sl][:, k, 0:128], h[:, k, :n], k == 0, k == KC - 1, [win_s.bufs[sl], hb], [ptb[pi]])
                    sset = c // 2
                    qk_finish(pi, n, 2 + sset, rope_off, KT[sset][:, c % 2, key_off:key_off + n], KTb[sset])
                sl = win_s.get()
                for sub in range(n // 128):
                    pi = rot("pt")
                    for k in range(KC):
                        _mm(S, pt[pi][:, :], h[:, k, sub * 128:(sub + 1) * 128], wsl[sl][:, k, :], k == 0, k == KC - 1,
                            [win_s.bufs[sl], hb], [ptb[pi]])
                    kt = key_off // 128 + sub
                    S.op("act", [ptb[pi]], [Vb], lambda e, pi=pi, kt=kt: e.copy(V[:, kt, :], pt[pi][:, :]), self_dep=False)
            for (col0, n, r, rope_off, qt) in p2_tiles(be):
                is_ctx = (qt is None)
                kit.emit(C, col0, n, r, 1, h, hb)
                for hd in range(16):
                    sset = 0 if hd < 8 else 1
                    kv = (hd % 8) // 4
                    sl = win_s.get()
                    pi = rot("pt")
                    for k in range(KC):
                        _mm(S, pt[pi][:, :n], wsl[sl][:, k, 0:128], h[:, k, :n], k == 0, k == KC - 1, [win_s.bufs[sl], hb], [ptb[pi]])
                    qi = rot("q")
                    qk_finish(pi, n, sset, rope_off, qb_[qi][:, :n], qbb[qi])
                    keys = []
                    if is_ctx:
                        keys = [(0, 0, n, []), (1, 0, n, [])]
                    elif sset == 0:
                        keys = [(kt, 0, n, []) for kt in range(NKT)]
                    else:
                        keys = [(0, 0, n, []), (1, 0, n, [])]
                        qb0 = qt * 4
                        for kb in range(max(0, qb0 - 1), min(15, qb0 + 4) + 1):
                            lo = max(kb - 1, qb0)
                            hi = min(kb + 1, qb0 + 3)
                            if lo > hi:
                                continue
                            mk = []
                            for qbk in range(lo, hi + 1):
                                if qbk == kb + 1:
                                    mk.append(((qbk - lo) * 128, 0))
                                elif qbk == kb - 1:
                                    mk.append(((qbk - lo) * 128, 1))
                            keys.append((2 + kb, (lo - qb0) * 128, (hi - qb0 + 1) * 128, mk))
                    slot_a = {}

                    def emit_s(ki):
                        kt, c0, c1, mk = keys[ki]
                        w_ = c1 - c0
                        a = rot("ps")
                        slot_a[ki] = a
                        _mm(S, ps[a][:, :w_], KT[sset][:, kv, kt * 128:(kt + 1) * 128], qb_[qi][:, c0:c1], True, True,
                            [KTb[sset], qbb[qi]], [psb[a]])
                        S.op("act", [psb[a]], [pTb[a]],
                             lambda e: e.activation(out=pT[a][:, :w_], in_=ps[a][:, :w_], func=AF.Exp, scale=SCALE))
                        for (off, which) in mk:
                            S.op("pool", [pTb[a], cb], [pTb[a]],
                                 lambda e, off=off, which=which: e.tensor_tensor(pT[a][:, off:off + 128], pT[a][:, off:off + 128],
                                                                                 masks[:, which, :], ALU.mult))

                    def emit_pv(ki):
                        kt, c0, c1, mk = keys[ki]
                        w_ = c1 - c0
                        a = slot_a[ki]
                        vcol = sset * 256 + kv * 128
                        _mm(S, po[:, c0:c1], V[:, kt, vcol:vcol + 128], pT[a][:, :w_], ki == 0, ki == len(keys) - 1, [Vb, pTb[a]], [pob])
                        _mm(S, pd[:, c0:c1], ones1[:], pT[a][:, :w_], ki == 0, ki == len(keys) - 1, [cb, pTb[a]], [pdb])

                    emit_s(0)
                    for ki in range(len(keys)):
                        if ki + 1 < len(keys):
                            emit_s(ki + 1)
                        emit_pv(ki)
                    if sset == 1:
                        S.op("dve", [pdb, cb], [rdenb],
                             lambda e, hd=hd: e.tensor_scalar(rden[:, :n], pd[:, :n], esink[:, hd - 8:hd - 7], None, ALU.add))
                        S.op("dve", [rdenb], [rdenb], lambda e: e.reciprocal(rden[:, :n], rden[:, :n]))
                    else:
                        S.op("dve", [pdb], [rdenb], lambda e: e.reciprocal(rden[:, :n], pd[:, :n]))
                    S.op("dve", [pob, rdenb], [OTb], lambda e, hd=hd: e.tensor_tensor(OT[:, hd, :n], po[:, :n], rden[:, :n], ALU.mult))
                for m in range(KC):
                    sl = wo_s.get()
                    pi = rot("pt")
                    for hd in range(16):
                        _mm(S, pt[pi][:, :n], wos[sl][:, hd, :], OT[:, hd, :n], hd == 0, hd == 15, [wo_s.bufs[sl], OTb], [ptb[pi]])
                    xi = rot("xres")
                    S.dma("pool", xres[xi][:, :n], stv[:, m, col0:col0 + n], [C.st_b], [xresb[xi]])
                    S.op("dve", [ptb[pi], xresb[xi], C.hg_b], [xresb[xi]],
                         lambda e, pi=pi, xi=xi, m=m: e.scalar_tensor_tensor(xres[xi][:, :n], pt[pi][:, :n], C.hg[:, 1, m, r:r + 1],
                                                                            xres[xi][:, :n], ALU.mult, ALU.add))
                    S.dma("pool", stv[:, m, col0:col0 + n], xres[xi][:, :n], [xresb[xi]], [C.st_b], partial=True)
        S.barrier()


def _hy_sets(l):
    sets = [(0, SEQ)]
    if l == 1:
        sets.append((1, CTX))
    return sets


def _hy_seq_info(C, nset, be):
    if nset == 0:
        return be, be * SEQ, be
    return NB + be, NLAT + be * CTX, NB


def _hy_dft(C, n):
    nch = n // 128
    t = {}
    for nm in ("cf", "sf", "si"):
        t[nm] = C.win(f"dft_{nm}{n}", [nch, 128, nch * 128], BF16)
    for nm in ("cr", "sr"):
        t[nm] = C.win(f"dft_{nm}{n}", [n, n], BF16)
    return t


def emit_hyena(C, l):
    nc, S = C.nc, C.S
    i = l // 2
    sets = _hy_sets(l)
    w_in = C.win(f"w_hy_in{i}", [D, 3 * D])
    w_out = C.win(f"w_hy_out{i}", [D, D])
    for k in range(KC):
        S.dma("bg", C.whinb[k * 128:(k + 1) * 128, :], w_in[k * 128:(k + 1) * 128, :], [], [C.whinb_b], partial=True)
    wo_v = w_out.rearrange("(j p) (m c) -> m p j c", p=128, c=128)
    for m in range(KC):
        S.dma("bg", C.waob[m].rearrange("p (j c) -> p j c", c=128), wo_v[m], [], [C.waob_b], partial=True)
    emit_hy_filter(C, l)
    emit_hy_inproj(C, l)
    emit_hy_conv(C, l)
    emit_hy_outproj(C, l)


def _dft_forward(C, S, fw, u, ub, n, psA, psAb, psB, psBb, consumer, need_a=True):
    nch = n // 128
    for fc in range(nch):
        sl = fw.get()
        a = fc % 2
        if need_a:
            for sc in range(nch):
                _mm(S, psA[a][:, :], fw.slots[sl][:, 0, sc * 128:(sc + 1) * 128], u[:, sc, :], sc == 0, sc == nch - 1,
                    [fw.bufs[sl], ub], [psAb[a]])
        for sc in range(nch):
            _mm(S, psB[a][:, :], fw.slots[sl][:, 1, sc * 128:(sc + 1) * 128], u[:, sc, :], sc == 0, sc == nch - 1,
                [fw.bufs[sl], ub], [psBb[a]])
        consumer(fc, a)


def emit_hy_filter(C, l):
    nc, S = C.nc, C.S
    i = l // 2
    w1_d = C.win(f"hf_w1_{i}", [33, 64])
    w2_d = C.win(f"hf_w2_{i}", [64, 64])
    w3_d = C.win(f"hf_w3_{i}", [64, 4 * D])
    hfv_d = C.win(f"hf_vec_{i}", [64, 4])
    bias_d = C.win(f"hy_bias_{i}", [128, 2, D])
    absd_d = C.win("hy_absdelta", [128, D])
    for (nset, n) in _hy_sets(l):
        nch = n // 128
        NN = 2 * n
        dft = _hy_dft(C, n)
        feats_d = C.win(f"hy_feats{n}", [33, n])
        negt_d = C.win(f"hy_negt{n}", [128, nch])
        wf_d = C.win(f"hy_wf{n}", [128, 2, nch])
        W = min(512, n)
        with ExitStack() as es:
            sbt = lambda nm, sh, dt_: es.enter_context(C.sbuf("hf_" + nm, sh, dt_))
            pst = lambda nm: es.enter_context(C.psum("hf_" + nm, [128, 512], F32))
            cb = Buf("hf_const")
            w1 = sbt("w1", [33, 64], F32)
            w2 = sbt("w2", [64, 64], F32)
            w3 = sbt("w3", [64, 2, 512], F32)
            w3b = Buf("w3")
            hfv = sbt("hfv", [64, 4], F32)
            sc = sbt("sc", [64, 4], F32)
            feats = sbt("feats", [33, n], F32)
            bias = sbt("bias", [128, 512], F32)
            absd = sbt("absd", [128, 512], F32)
            negt = sbt("negt", [128, nch], F32)
            wf = sbt("wf", [128, 2, nch], F32)
            ones1 = sbt("ones1", [128, 128], BF16)
            for dst, src in ((w1, w1_d), (w2, w2_d), (hfv, hfv_d), (feats, feats_d), (negt, negt_d), (wf, wf_d)):
                S.dma("sp", dst[:], src, [], [cb], partial=True)
            S.op("dve", [cb], [cb], lambda e: e.memset(ones1[:], 1.0))
            for j in range(2):
                S.op("dve", [cb], [cb], lambda e, j=j: e.tensor_scalar(sc[:, 2 * j:2 * j + 1], hfv[:, 2 * j + 1:2 * j + 2], 1.0 / 3.0, None, ALU.mult))
                S.op("dve", [cb], [cb], lambda e, j=j: e.tensor_tensor(sc[:, 2 * j + 1:2 * j + 2], sc[:, 2 * j:2 * j + 1], hfv[:, 2 * j:2 * j + 1], ALU.mult))
            h1 = sbt("h1", [64, n], F32)
            h1b = Buf("h1")
            h2 = sbt("h2", [64, n], F32)
            h2b = Buf("h2")
            s3 = sbt("s3", [64, W], F32)
            s3b = Buf("s3")
            tq = sbt("tq", [64, W], F32)
            tqb = Buf("tq")
            pz = pst("pz")
            pzb = Buf("pz")
            for (lhs, src, srcb, dst, dstb, j) in ((w1, feats, cb, h1, h1b, 0), (w2, h1, h1b, h2, h2b, 1)):
                for c0 in range(0, n, W):
                    _mm(S, pz[0:64, :W], lhs[:], src[:, c0:c0 + W], True, True, [cb, srcb], [pzb])
                    S.op("act", [pzb, cb], [s3b], lambda e, j=j: e.activation(out=s3[:], in_=pz[0:64, :W], func=AF.Sin,
                                                                              bias=sc[:, 2 * j + 1:2 * j + 2], scale=sc[:, 2 * j:2 * j + 1]))
                    S.op("dve", [s3b], [tqb], lambda e: e.tensor_tensor(tq[:], s3[:], s3[:], ALU.mult))
                    S.op("dve", [tqb], [tqb], lambda e: e.tensor_scalar(tq[:], tq[:], -4.0, 3.0, ALU.mult, ALU.add))
                    S.op("dve", [tqb, s3b], [dstb], lambda e, dst=dst, c0=c0: e.tensor_tensor(dst[:, c0:c0 + W], s3[:], tq[:], ALU.mult))
            kf = sbt("kf", [128, nch, 512], F32)
            kfb = Buf("kf")
            kb = sbt("kb", [128, nch, 512], F32)
            kbb = Buf("kb")
            eT = sbt("e", [128, nch, 512], BF16)
            eb = Buf("e")
            gT = sbt("g", [128, nch, 512], BF16)
            gb = Buf("g")
            kre2 = [sbt(f"kre{j}", [128, 512], F32) for j in range(2)]
            kre2b = [Buf(f"kre{j}") for j in range(2)]
            kim2 = [sbt(f"kim{j}", [128, 512], F32) for j in range(2)]
            kim2b = [Buf(f"kim{j}") for j in range(2)]
            krs = sbt("krs", [128, 512], F32)
            krsb = Buf("krs")
            dec = sbt("dec", [128, 512], F32)
            decb = Buf("dec")
            sq = [sbt(f"sq{j}", [128, 512], BF16) for j in range(2)]
            sqb = [Buf(f"sq{j}") for j in range(2)]
            scl = sbt("scl", [128, 512], F32)
            sclb = Buf("scl")
            tmp = sbt("tmp", [128, 512], F32)
            tmpb = Buf("tmp")
            slots = [sbt(f"dsl{j}", [128, 2, nch * 128], BF16) for j in range(2)]
            pk = [pst(f"pk{j}") for j in range(2)]
            pkb = [Buf(f"pk{j}") for j in range(2)]
            pn = pst("pn")
            pnb = Buf("pn")
            psA = [pst(f"pa{j}") for j in range(2)]
            psAb = [Buf(f"pa{j}") for j in range(2)]
            psB = [pst(f"pb{j}") for j in range(2)]
            psBb = [Buf(f"pb{j}") for j in range(2)]
            seq = []
            for o in range(2):
                for dt in range(4):
                    for rep in range(2):
                        for fc in range(nch):
                            seq.append([(dft["cf"][fc], lambda sl: sl[:, 0, :]), (dft["sf"][fc], lambda sl: sl[:, 1, :])])
            fw = WStream(S, slots, seq, Buf("dftc"))
            nsq = [0]
            for o in range(2):
                for dt in range(4):
                    dsl = slice(dt * 512, (dt + 1) * 512)
                    ob = Buf("hf_odt")
                    for dr in range(2):
                        c0 = o * 2 * D + dr * D + dt * 512
                        S.dma("sp", w3[:, dr, :], w3_d[:, c0:c0 + 512], [], [w3b], partial=(dr == 1))
                    S.dma("sp", bias[:], bias_d[:, o, dsl], [], [cb], partial=True)
                    S.dma("sp", absd[:], absd_d[:, dsl], [], [cb], partial=True)
                    for lc in range(nch):
                        S.op("act", [cb], [decb], lambda e, lc=lc: e.activation(out=dec[:], in_=absd[:], func=AF.Exp, scale=negt[:, lc:lc + 1]))
                        for dr, (kt_, ktb_) in enumerate(((kf, kfb), (kb, kbb))):
                            a = (2 * lc + dr) % 2
                            _mm(S, pk[a][:, :], h2[:, lc * 128:(lc + 1) * 128], w3[:, dr, :], True, True, [h2b, w3b], [pkb[a]])
                            S.op("dve", [pkb[a], decb], [ktb_], lambda e, kt_=kt_, a=a, lc=lc: e.tensor_tensor(kt_[:, lc, :], pk[a][:, :], dec[:], ALU.mult))
                    S.op("dve", [kbb], [kbb], lambda e: e.memset(kb[0:1, 0, :], 0.0))
                    tot = 2 * nch
                    cnt_ = 0
                    for (kt_, ktb_) in ((kf, kfb), (kb, kbb)):
                        for lc in range(nch):
                            j = nsq[0] % 2
                            nsq[0] += 1
                            S.op("act", [ktb_], [sqb[j]], lambda e, kt_=kt_, lc=lc, j=j: e.activation(out=sq[j][:], in_=kt_[:, lc, :], func=AF.Square))
                            _mm(S, pn[:, :], ones1[:], sq[j][:], cnt_ == 0, cnt_ == tot - 1, [sqb[j], cb], [pnb])
                            cnt_ += 1
                    S.op("act", [pnb], [sclb], lambda e: e.activation(out=scl[:], in_=pn[:, :], func=AF.Sqrt, bias=C.eps_sb[:, 0:1], scale=1.0))
                    S.op("dve", [sclb], [sclb], lambda e: e.reciprocal(scl[:], scl[:]))
                    for lc in range(nch):
                        S.op("dve", [kfb, kbb], [tmpb], lambda e, lc=lc: e.tensor_tensor(tmp[:], kf[:, lc, :], kb[:, lc, :], ALU.add))
                        S.op("dve", [tmpb, sclb], [eb], lambda e, lc=lc: e.tensor_tensor(eT[:, lc, :], tmp[:], scl[:], ALU.mult))
                        S.op("dve", [kfb, kbb], [tmpb], lambda e, lc=lc: e.tensor_tensor(tmp[:], kf[:, lc, :], kb[:, lc, :], ALU.subtract))
                        S.op("dve", [tmpb, sclb], [gb], lambda e, lc=lc: e.tensor_tensor(gT[:, lc, :], tmp[:], scl[:], ALU.mult))

                    ktb = C.ktab_b[nset]

                    def cons_e(fc, a, o=o, dt=dt):
                        j = fc % 2
                        S.op("dve", [psAb[a], cb], [tmpb], lambda e: e.tensor_tensor(tmp[:], psA[a][:, :], bias[:], ALU.add))
                        S.op("dve", [tmpb, cb], [kre2b[j]], lambda e: e.tensor_scalar(kre2[j][:], tmp[:], wf[:, 0, fc:fc + 1], None, ALU.mult))
                        if fc == 0:
                            S.op("dve", [kre2b[j]], [krsb], lambda e: e.tensor_copy(krs[:], kre2[j][:]))
                            S.op("dve", [psBb[a], cb], [tmpb], lambda e: e.tensor_tensor(tmp[0:1, :], psB[a][0:1, :], bias[0:1, :], ALU.add))
                            S.op("dve", [tmpb], [krsb], lambda e: e.tensor_scalar(krs[0:1, :], tmp[0:1, :], 1.0 / NN, None, ALU.mult))
                            S.dma("sp", C.krs0[nset, o, dt], krs[:], [krsb], [ktb], partial=True)
                        S.dma("sp", C.ktab[nset, o, 0, dt][:, fc * 512:(fc + 1) * 512], kre2[j][:], [kre2b[j]], [ktb], partial=True)

                    def cons_g(fc, a, o=o, dt=dt):
                        j = fc % 2
                        S.op("dve", [psBb[a], cb], [kim2b[j]], lambda e: e.tensor_scalar(kim2[j][:], psB[a][:, :], wf[:, 1, fc:fc + 1], None, ALU.mult))
                        if fc == 0:
                            S.op("dve", [kim2b[j]], [kim2b[j]], lambda e: e.memset(kim2[j][0:1, :], 0.0))
                        S.dma("sp", C.ktab[nset, o, 1, dt][:, fc * 512:(fc + 1) * 512], kim2[j][:], [kim2b[j]], [ktb], partial=True)

                    _dft_forward(C, S, fw, eT, eb, n, psA, psAb, psB, psBb, cons_e, need_a=True)
                    _dft_forward(C, S, fw, gT, gb, n, psA, psAb, psB, psBb, cons_g, need_a=False)
            S.barrier()


def emit_hy_inproj(C, l):
    nc, S = C.nc, C.S
    i = l // 2
    bin_d = C.win(f"hy_bin_{i}", [128, 48])
    wc_d = C.win(f"hy_wc_{i}", [128, 3, 48])
    bc_d = C.win(f"hy_bc_{i}", [128, 48])
    ident_d = C.win("ident_bf", [128, 128], BF16)
    whv = C.whinb.rearrange("(k p) n -> p k n", p=128)
    for (nset, n) in _hy_sets(l):
        nch = n // 128
        W = min(512, n)
        with ExitStack() as es:
            sbt = lambda nm, sh, dt_: es.enter_context(C.sbuf("hi_" + nm, sh, dt_))
            pst = lambda nm: es.enter_context(C.psum("hi_" + nm, [128, 512], F32))
            kit = NormKit(C, es, "hi")
            cb = Buf("hi_const")
            binv = sbt("bin", [128, 48], F32)
            wcv = sbt("wc", [128, 3, 48], F32)
            bcv = sbt("bc", [128, 48], F32)
            ident = sbt("ident", [128, 128], BF16)
            for dst, src in ((binv, bin_d), (wcv, wc_d), (bcv, bc_d), (ident, ident_d)):
                S.dma("sp", dst[:], src, [], [cb], partial=True)
            h = sbt("h", [128, KC, n], BF16)
            hb = Buf("h")
            pf2 = [sbt(f"pf{j}", [128, n + 2], F32) for j in range(2)]
            pf2b = [Buf(f"pf{j}") for j in range(2)]
            pc = sbt("pc", [128, n], F32)
            pcb = Buf("pc")
            pcb16 = sbt("pc16", [128, n], BF16)
            pc16b = Buf("pc16")
            stg = sbt("stg", [128, nch, 512], BF16)
            stgb = Buf("stg")
            wsl = [sbt(f"w{j}", [128, KC, 128], BF16) for j in range(2)]
            pp = [pst(f"pp{j}") for j in range(2)]
            ppb = [Buf(f"pp{j}") for j in range(2)]
            ptr = [es.enter_context(C.psum(f"hi_ptr{j}", [128, 512], BF16)) for j in range(2)]
            ptrb = [Buf(f"ptr{j}") for j in range(2)]
            for j in range(2):
                S.op("dve", [], [pf2b[j]], lambda e, j=j: e.memset(pf2[j][:], 0.0))
            seq = []
            for be in C.be_list:
                for c in range(48):
                    seq.append((whv[:, :, c * 128:(c + 1) * 128], lambda sl: sl[:, :, :]))
            ws = WStream(S, wsl, seq, C.whinb_b)
            npp = [0]
            ntr = [0]
            for be in C.be_list:
                q, col0, r = _hy_seq_info(C, nset, be)
                for t0 in range(0, n, W):
                    kit.emit(C, col0 + t0, W, r, 1, h, hb, hoff=t0)
                for c in range(48):
                    sl = ws.get()
                    pf = pf2[c % 2]
                    pfb = pf2b[c % 2]
                    for t0 in range(0, n, W):
                        a = npp[0] % 2
                        npp[0] += 1
                        for k in range(KC):
                            _mm(S, pp[a][:, :W], wsl[sl][:, k, :], h[:, k, t0:t0 + W], k == 0, k == KC - 1, [ws.bufs[sl], hb], [ppb[a]])
                        S.op("act", [ppb[a], cb], [pfb], lambda e, a=a, t0=t0, c=c: e.activation(out=pf[:, 1 + t0:1 + t0 + W], in_=pp[a][:, :W],
                                                                                              func=AF.Identity, bias=binv[:, c:c + 1], scale=1.0),
                             self_dep=False)
                    S.op("act", [pfb, cb], [pcb], lambda e, c=c: e.activation(out=pc[:], in_=pf[:, 1:n + 1], func=AF.Identity,
                                                                             bias=bcv[:, c:c + 1], scale=wcv[:, 1, c:c + 1]))
                    S.op("dve", [pfb, pcb, cb], [pcb], lambda e, c=c: e.scalar_tensor_tensor(pc[:], pf[:, 0:n], wcv[:, 0, c:c + 1], pc[:], ALU.mult, ALU.add))
                    S.op("dve", [pfb, pcb, cb], [pc16b], lambda e, c=c: e.scalar_tensor_tensor(pcb16[:], pf[:, 2:n + 2], wcv[:, 2, c:c + 1], pc[:], ALU.mult, ALU.add))
                    if c >= 32:
                        S.dma("pool", C.hy_x2T[q, c - 32][:, :n], pcb16[:], [pc16b], [C.hy_x2T_b], partial=True)
                        continue
                    dcol = (c % 4) * 128
                    for g0 in range(0, nch, 4):
                        a = ntr[0] % 2
                        ntr[0] += 1
                        ng = min(4, nch - g0)
                        for gi in range(ng):
                            tcn = g0 + gi
                            S.op("pe", [pc16b, cb], [ptrb[a]], lambda e, a=a, gi=gi, tcn=tcn: e.transpose(ptr[a][:, gi * 128:(gi + 1) * 128],
                                                                                                      pcb16[:, tcn * 128:(tcn + 1) * 128], ident[:]))
                        S.op("act", [ptrb[a]], [stgb], lambda e, a=a, g0=g0, ng=ng, dcol=dcol: e.copy(
                            stg[:, g0:g0 + ng, dcol:dcol + 128], ptr[a][:, :ng * 128].rearrange("p (g c) -> p g c", c=128)), self_dep=False)
                    if c % 4 == 3:
                        dt = (c % 16) // 4
                        dst = C.hy_u if c < 16 else C.hy_x1
                        dstb = C.hy_u_b if c < 16 else C.hy_x1_b
                        S.dma("pool", dst[q, dt][:, :nch * 512].rearrange("p (c d) -> p c d", d=512), stg[:], [stgb], [dstb], partial=True)
            S.barrier()


def emit_hy_conv(C, l):
    nc, S = C.nc, C.S
    for (nset, n) in _hy_sets(l):
        nch = n // 128
        W = min(256, n)
        dft = _hy_dft(C, n)
        crv = dft["cr"].rearrange("(fk p) t -> p fk t", p=128)
        srv = dft["sr"].rearrange("(fk p) t -> p fk t", p=128)
        with ExitStack() as es:
            sbt = lambda nm, sh, dt_: es.enter_context(C.sbuf("hc_" + nm, sh, dt_))
            pst = lambda nm: es.enter_context(C.psum("hc_" + nm, [128, 512], F32))
            kre = sbt("kre", [128, nch, 512], F32)
            kim = sbt("kim", [128, nch, 512], F32)
            krs = sbt("krs", [128, 512], F32)
            ktb = Buf("hc_ktab")
            u = sbt("u", [128, nch, 512], BF16)
            ub = Buf("u")
            P = sbt("P", [128, nch, 512], BF16)
            Pb = Buf("P")
            Q = sbt("Q", [128, nch, 512], BF16)
            Qb = Buf("Q")
            t1 = sbt("t1", [128, 512], F32)
            t1b = Buf("t1")
            t2 = sbt("t2", [128, 512], F32)
            t2b = Buf("t2")
            t3 = sbt("t3", [128, 512], F32)
            t3b = Buf("t3")
            t4 = sbt("t4", [128, 512], F32)
            t4b = Buf("t4")
            xg = [sbt(f"xg{j}", [128, 512], BF16) for j in range(2)]
            xgb = [Buf(f"xg{j}") for j in range(2)]
            zo = [sbt(f"zo{j}", [128, 512], BF16) for j in range(2)]
            zob = [Buf(f"zo{j}") for j in range(2)]
            slots = [sbt(f"dsl{j}", [128, 2, nch * 128], BF16) for j in range(2)]
            wslots = [sbt(f"wsl{j}", [128, 2, nch, W], BF16) for j in range(2)]
            psA = [pst(f"pa{j}") for j in range(2)]
            psAb = [Buf(f"pa{j}") for j in range(2)]
            psB = [pst(f"pb{j}") for j in range(2)]
            psBb = [Buf(f"pb{j}") for j in range(2)]
            psY = [pst(f"py{j}") for j in range(2)]
            psYb = [Buf(f"py{j}") for j in range(2)]
            seqs = [_hy_seq_info(C, nset, be)[0] for be in C.be_list]
            seq = []
            wseq = []
            for dt in range(4):
                for o in range(2):
                    for q in seqs:
                        for fc in range(nch):
                            seq.append([(dft["cf"][fc], lambda sl: sl[:, 0, :]), (dft["sf"][fc], lambda sl: sl[:, 1, :])])
                        if o == 0:
                            for tc in range(nch):
                                seq.append([(dft["cf"][tc], lambda sl: sl[:, 0, :]), (dft["si"][tc], lambda sl: sl[:, 1, :])])
                        else:
                            for t0 in range(0, n, W):
                                wseq.append([(crv[:, :, t0:t0 + W], lambda sl: sl[:, 0, :, :]), (srv[:, :, t0:t0 + W], lambda sl: sl[:, 1, :, :])])
            fw = WStream(S, slots, seq, Buf("dftc"))
            ww = WStream(S, wslots, wseq, Buf("dftw"))
            cnt = {"xg": 0, "zo": 0, "py": 0}

            def rot(nm):
                v = cnt[nm] % 2
                cnt[nm] += 1
                return v

            for dt in range(4):
                for o in range(2):
                    S.dma("sp", kre[:], C.ktab[nset, o, 0, dt][:, :nch * 512].rearrange("p (c d) -> p c d", d=512), [C.ktab_b[nset]], [ktb])
                    S.dma("sp", kim[:], C.ktab[nset, o, 1, dt][:, :nch * 512].rearrange("p (c d) -> p c d", d=512), [C.ktab_b[nset]], [ktb], partial=True)
                    S.dma("sp", krs[:], C.krs0[nset, o, dt], [C.ktab_b[nset]], [ktb], partial=True)
                    for q in seqs:
                        usrc = C.hy_u[q, dt][:, :nch * 512].rearrange("p (c d) -> p c d", d=512)
                        S.dma("sp", u[:], usrc, [C.hy_u_b], [ub])

                        def cons(fc, a):
                            krx = krs[:] if fc == 0 else kre[:, fc, :]
                            S.op("dve", [psAb[a], ktb], [t1b], lambda e: e.tensor_tensor(t1[:], psA[a][:, :], kre[:, fc, :], ALU.mult))
                            S.op("dve", [psBb[a], ktb], [t2b], lambda e: e.tensor_tensor(t2[:], psB[a][:, :], kim[:, fc, :], ALU.mult))
                            S.op("pool", [t1b, t2b], [Pb], lambda e: e.tensor_tensor(P[:, fc, :], t1[:], t2[:], ALU.add))
                            S.op("dve", [psBb[a], ktb], [t3b], lambda e: e.tensor_tensor(t3[:], psB[a][:, :], krx, ALU.mult))
                            S.op("dve", [psAb[a], ktb], [t4b], lambda e: e.tensor_tensor(t4[:], psA[a][:, :], kim[:, fc, :], ALU.mult))
                            S.op("pool", [t3b, t4b], [Qb], lambda e: e.tensor_tensor(Q[:, fc, :], t3[:], t4[:], ALU.subtract))

                        _dft_forward(C, S, fw, u, ub, n, psA, psAb, psB, psBb, cons, need_a=True)
                        if o == 0:
                            for tc in range(nch):
                                sl = fw.get()
                                a = rot("py")
                                for fk in range(nch):
                                    _mm(S, psY[a][:, :], slots[sl][:, 0, fk * 128:(fk + 1) * 128], P[:, fk, :], fk == 0, False,
                                        [fw.bufs[sl], Pb], [psYb[a]])
                                for fk in range(nch):
                                    _mm(S, psY[a][:, :], slots[sl][:, 1, fk * 128:(fk + 1) * 128], Q[:, fk, :], False, fk == nch - 1,
                                        [fw.bufs[sl], Qb], [psYb[a]])
                                xi = rot("xg")
                                S.dma("pool", xg[xi][:], C.hy_x1[q, dt][:, tc * 512:(tc + 1) * 512], [C.hy_x1_b], [xgb[xi]])
                                zi = rot("zo")
                                S.op("dve", [psYb[a], xgb[xi]], [zob[zi]], lambda e: e.tensor_tensor(zo[zi][:], psY[a][:, :], xg[xi][:], ALU.mult))
                                S.dma("pool", C.hy_u[q, dt][:, tc * 512:(tc + 1) * 512], zo[zi][:], [zob[zi]], [C.hy_u_b], partial=True)
                        else:
                            for t0 in range(0, n, W):
                                sl = ww.get()
                                for dc in range(4):
                                    a = rot("py")
                                    for fk in range(nch):
                                        _mm(S, psY[a][:, :W], P[:, fk, dc * 128:(dc + 1) * 128], wslots[sl][:, 0, fk, :], fk == 0, False,
                                            [ww.bufs[sl], Pb], [psYb[a]])
                                    for fk in range(nch):
                                        _mm(S, psY[a][:, :W], Q[:, fk, dc * 128:(dc + 1) * 128], wslots[sl][:, 1, fk, :], False, fk == nch - 1,
                                            [ww.bufs[sl], Qb], [psYb[a]])
                                    xi = rot("xg")
                                    S.dma("pool", xg[xi][:, :W], C.hy_x2T[q, dt * 4 + dc][:, t0:t0 + W], [C.hy_x2T_b], [xgb[xi]])
                                    zi = rot("zo")
                                    S.op("dve", [psYb[a], xgb[xi]], [zob[zi]], lambda e: e.tensor_tensor(zo[zi][:, :W], psY[a][:, :W], xg[xi][:, :W], ALU.mult))
                                    S.dma("pool", C.hy_z2T[q, dt * 4 + dc][:, t0:t0 + W], zo[zi][:, :W], [zob[zi]], [C.hy_z2T_b], partial=True)
            S.barrier()


def emit_hy_outproj(C, l):
    nc, S = C.nc, C.S
    i = l // 2
    bo_d = C.win(f"hy_bout_{i}", [128, KC])
    stv = C.st.rearrange("(k p) t -> p k t", p=128)
    with ExitStack() as es:
        sbt = lambda nm, sh, dt_: es.enter_context(C.sbuf("ho_" + nm, sh, dt_))
        pst = lambda nm: es.enter_context(C.psum("ho_" + nm, [128, 512], F32))
        bo = sbt("bo", [128, KC], F32)
        cb = Buf("ho_const")
        S.dma("sp", bo[:], bo_d, [], [cb])
        z = sbt("z", [128, KC, TT], BF16)
        zb = Buf("z")
        wos = [sbt(f"wo{j}", [128, KC, 128], BF16) for j in range(2)]
        xres = [sbt(f"xres{j}", [128, TT], F32) for j in range(2)]
        xresb = [Buf(f"xres{j}") for j in range(2)]
        yb_ = sbt("y", [128, TT], F32)
        ybb = Buf("y")
        pt = [pst(f"pt{j}") for j in range(2)]
        ptb = [Buf(f"pt{j}") for j in range(2)]
        tiles = []
        for (nset, n) in _hy_sets(l):
            W = min(TT, n)
            for be in C.be_list:
                q, col0, r = _hy_seq_info(C, nset, be)
                for t0 in range(0, n, W):
                    tiles.append((q, col0, r, t0, W))
        seq = []
        for _ in tiles:
            for m in range(KC):
                seq.append((C.waob[m].rearrange("p (j c) -> p j c", c=128), lambda sl: sl[:, :, :]))
        ws = WStream(S, wos, seq, C.waob_b)
        n_ = [0]
        for (q, col0, r, t0, W) in tiles:
            S.dma("sp", z[:, :, :W], C.hy_z2T[q][:, :, t0:t0 + W].rearrange("k p t -> p k t"), [C.hy_z2T_b], [zb])
            for m in range(KC):
                sl = ws.get()
                a = n_[0] % 2
                n_[0] += 1
                for k in range(KC):
                    _mm(S, pt[a][:, :W], wos[sl][:, k, :], z[:, k, :W], k == 0, k == KC - 1, [ws.bufs[sl], zb], [ptb[a]])
                S.dma("pool", xres[a][:, :W], stv[:, m, col0 + t0:col0 + t0 + W], [C.st_b], [xresb[a]])
                S.op("act", [ptb[a], cb], [ybb], lambda e: e.activation(out=yb_[:, :W], in_=pt[a][:, :W], func=AF.Identity, bias=bo[:, m:m + 1], scale=1.0))
                S.op("dve", [ybb, xresb[a], C.hg_b], [xresb[a]],
                     lambda e: e.scalar_tensor_tensor(xres[a][:, :W], yb_[:, :W], C.hg[:, 1, m, r:r + 1], xres[a][:, :W], ALU.mult, ALU.add))
                S.dma("pool", stv[:, m, col0 + t0:col0 + t0 + W], xres[a][:, :W], [xresb[a]], [C.st_b], partial=True)
        S.barrier()


def build_program(phases, dbg=False, be_list=None):
    nc = bass.Bass("TRN2", target_bir_lowering=False)
    C = Ctx()
    C.nc = nc
    C.debug = dbg
    C.be_list = list(range(NB)) if be_list is None else be_list
    dt = nc.dram_tensor
    C.xin = dt("xin", [D, NTOK], F32, kind="ExternalInput").ap()
    C.cT = dt("cT", [128, KC, NR], F32, kind="ExternalInput").ap()
    C.st = dt("st", [D, NTOK], F32, kind="ExternalOutput").ap()
    C.winb = [dt(f"winb{i}", [D, 2 * DFF], BF16, kind="Internal").ap() for i in range(2)]
    C.woutb = [dt(f"woutb{i}", [KC, 128, DFF], BF16, kind="Internal").ap() for i in range(2)]
    C.winb_b = [Buf(f"winb{i}") for i in range(2)]
    C.woutb_b = [Buf(f"woutb{i}") for i in range(2)]
    C.st_b = Buf("st")
    C.wainb = dt("wainb", [D, 3072], BF16, kind="Internal").ap()
    C.waob = dt("waob", [KC, 128, D], BF16, kind="Internal").ap()
    C.whinb = dt("whinb", [D, 3 * D], BF16, kind="Internal").ap()
    C.whinb_b = Buf("whinb")
    NQ = 2 * NB
    C.hy_u = dt("hy_u", [NQ, 4, 128, KC * 512], BF16, kind="Internal").ap()
    C.hy_x1 = dt("hy_x1", [NQ, 4, 128, KC * 512], BF16, kind="Internal").ap()
    C.hy_x2T = dt("hy_x2T", [NQ, KC, 128, SEQ], BF16, kind="Internal").ap()
    C.hy_z2T = dt("hy_z2T", [NQ, KC, 128, SEQ], BF16, kind="Internal").ap()
    C.ktab = dt("hy_ktab", [2, 2, 2, 4, 128, KC * 512], F32, kind="Internal").ap()
    C.krs0 = dt("hy_krs0", [2, 2, 4, 128, 512], F32, kind="Internal").ap()
    C.hy_u_b = Buf("hy_u")
    C.hy_x1_b = Buf("hy_x1")
    C.hy_x2T_b = Buf("hy_x2T")
    C.hy_z2T_b = Buf("hy_z2T")
    C.ktab_b = [Buf("ktab0"), Buf("ktab1")]
    C.wainb_b = Buf("wainb")
    C.waob_b = Buf("waob")

    with ExitStack() as es:
        nc.allow_low_precision("bf16 matmuls with fp32 accumulation")
        S = Sched(nc, es)
        C.S = S
        sb = lambda name, shape, dtype: es.enter_context(C.sbuf(name, shape, dtype))
        C.ones_bf = sb("ones_bf", [128, 128], BF16)
        C.const_b = Buf("const")
        C.s_sb = sb("s_sb", [128, KC, NR], F32)
        C.s_b = Buf("s")
        C.modv = sb("modv", [128, NMOD * KC, NR], F32)
        C.modv_b = Buf("modv")
        C.gs = sb("gs", [128, 3, KC, NR], F32)
        C.gs_b = Buf("gs")
        C.hg = sb("hg", [128, 3, KC, NR], F32)
        C.hg_b = Buf("hg")

        S.op("dve", [], [C.const_b], lambda e: e.memset(C.ones_bf[:], 1.0 / D))
        C.eps_sb = sb("eps_sb", [128, 2], F32)
        S.op("dve", [], [C.const_b], lambda e: e.memset(C.eps_sb[:], EPS))
        S.dma("sp", C.s_sb[:], C.cT, [], [C.s_b])
        S.op("act", [C.s_b], [C.s_b], lambda e: e.activation(out=C.s_sb[:], in_=C.s_sb[:], func=AF.Silu))
        C.s_bf = sb("s_bf", [128, KC, NR], BF16)
        S.op("dve", [C.s_b], [C.s_b], lambda e: e.tensor_copy(C.s_bf[:], C.s_sb[:]))
        for k in range(KC):
            S.dma("pool", C.st[k * 128:(k + 1) * 128, :], C.xin[k * 128:(k + 1) * 128, :], [], [C.st_b], partial=True)

        lat_tiles = [(t * TT, t // (SEQ // TT)) for t in range(NLAT // TT)]
        ctx_tile = [(NLAT + i * TT, NB) for i in range(NB * CTX // TT)]
        for ph in phases:
            kind = ph[0]
            if kind == "conv":
                emit_convert_ffn(C, ph[1], ph[2], ph[3])
            elif kind == "mod":
                emit_mod(C, ph[1])
            elif kind == "hyena":
                emit_hyena(C, ph[1])
            elif kind == "attn":
                emit_attn(C, ph[1])
            elif kind == "ffn":
                _, l, w, slot, with_ctx, ntl = ph
                tiles = lat_tiles[:ntl] + (ctx_tile if with_ctx else [])
                emit_ffn(C, l, w, slot, tiles)
        S.barrier(include_bg=True)
        C.n_ops = S.nops
    return nc, C


def _const_tables():
    import ml_dtypes
    f32 = np.float32
    t = {}
    rows = SEQ // 64
    r, col = np.meshgrid(np.arange(rows), np.arange(64), indexing="ij")
    r = r.reshape(-1).astype(f32)
    col = col.reshape(-1).astype(f32)
    half = HD // 2
    inv = (f32(10000.0) ** (-np.arange(0, half, 2, dtype=f32) / f32(half))).astype(f32)
    ang = np.concatenate([r[:, None] * inv, col[:, None] * inv], axis=-1).astype(f32)
    cs = np.stack([np.cos(ang), np.sin(ang)]).astype(f32)
    cs = np.concatenate([cs, cs], axis=-1)
    t["rope_cs"] = np.ascontiguousarray(cs.transpose(0, 2, 1))
    rm = np.zeros((128, 128), f32)
    for m in range(64):
        rm[m + 64, m] = -1.0
        rm[m, m + 64] = 1.0
    t["rotm"] = rm.astype(ml_dtypes.bfloat16)
    kj = np.arange(128)[:, None]
    qi = np.arange(128)[None, :]
    t["wmask"] = np.ascontiguousarray(np.stack([(kj >= qi), (kj <= qi)], axis=1).astype(f32)).astype(ml_dtypes.bfloat16)
    return t


_HY_CACHE = {}


def _hyena_tables(n):
    if n in _HY_CACHE:
        return _HY_CACHE[n]
    import ml_dtypes
    f32 = np.float32
    bf = ml_dtypes.bfloat16
    t = {}
    nch = n // 128
    N = 2 * n
    tt = np.linspace(0.0, 1.0, n, dtype=f32)
    w = (f32(2.0 * math.pi / n) * np.arange(n, dtype=f32))[:, None]
    bands = np.linspace(1e-4, 15, 16, dtype=f32)
    ang = (w * bands[None, :]).astype(f32)
    feats = np.concatenate([tt[:, None], np.cos(ang), -np.sin(ang)], axis=-1).astype(f32)
    t[f"hy_feats{n}"] = np.ascontiguousarray(feats.T)
    t[f"hy_negt{n}"] = np.ascontiguousarray((-tt).reshape(nch, 128).T)
    wfv = np.full(n, 2.0 / N, f32)
    wfv[0] = 1.0 / N
    wf = wfv.reshape(nch, 128).T
    t[f"hy_wf{n}"] = np.ascontiguousarray(np.stack([wf, -wf], axis=1))
    max_decay = math.log(1e-2) / 0.3
    min_decay = math.log(1e-2) / 1.5
    deltas = np.abs(np.linspace(min_decay, max_decay, D, dtype=f32))
    t["hy_absdelta"] = np.ascontiguousarray(np.broadcast_to(deltas[None, :], (128, D)))
    idx = np.arange(n, dtype=np.int64)
    prod = (idx[:, None] * idx[None, :]) % N
    angm = prod.astype(np.float64) * (2.0 * math.pi / N)
    Cm = np.cos(angm)
    Sm = np.sin(angm)
    Sm[0, :] = np.where(idx % 2 == 0, 1.0, -1.0)
    Cb = Cm.astype(f32).astype(bf)
    Sb = Sm.astype(f32).astype(bf)
    C4 = Cb.reshape(nch, 128, nch, 128)
    S4 = Sb.reshape(nch, 128, nch, 128)
    t[f"dft_cf{n}"] = np.ascontiguousarray(C4.transpose(2, 1, 0, 3)).reshape(nch, 128, nch * 128)
    t[f"dft_sf{n}"] = np.ascontiguousarray(S4.transpose(0, 3, 2, 1)).reshape(nch, 128, nch * 128)
    t[f"dft_si{n}"] = np.ascontiguousarray(S4.transpose(2, 1, 0, 3)).reshape(nch, 128, nch * 128)
    t[f"dft_cr{n}"] = Cb
    t[f"dft_sr{n}"] = Sb
    t["ident_bf"] = np.eye(128, dtype=f32).astype(bf)
    _HY_CACHE[n] = t
    return t


def _prep_inputs(inputs, layers=range(DEPTH)):
    x = inputs["x"]
    ctx = inputs["ctx"]
    c = inputs["c"]
    c_ctx = inputs["c_ctx"]
    shared = {}
    for l in layers:
        shared[f"w_mod{l}"] = inputs["w_mod"][l]
        shared[f"b_mod{l}"] = np.ascontiguousarray(inputs["b_mod"][l].reshape(NMOD * KC, 128).T)
        shared[f"g_norm{l}"] = np.ascontiguousarray(inputs["g_norm"][l].reshape(3, KC, 128).transpose(2, 0, 1))
        for w in range(2):
            shared[f"w_ffn_in{l}_{w}"] = inputs["w_ffn_in"][l, w]
            shared[f"w_ffn_out{l}_{w}"] = inputs["w_ffn_out"][l, w]
    perm = np.concatenate([np.arange(0, HD, 2), np.arange(1, HD, 2)])
    for l in layers:
        if l % 2 == 0:
            i = l // 2
            wi = inputs["w_attn_in"][i]
            qcols = wi[:, :2048].reshape(D, 16, HD)[:, :, perm].reshape(D, 2048)
            kA = wi[:, 2048:2304].reshape(D, 2, HD)[:, :, perm].reshape(D, 256)
            vA = wi[:, 2304:2560]
            kB = wi[:, 2560:2816].reshape(D, 2, HD)[:, :, perm].reshape(D, 256)
            vB = wi[:, 2816:3072]
            shared[f"w_attn_in{i}"] = np.ascontiguousarray(np.concatenate([qcols, kA, kB, vA, vB], axis=1))
            shared[f"w_attn_out{i}"] = inputs["w_attn_out"][i]
            gq = inputs["g_q"][i][:, perm]
            gk = inputs["g_k"][i][:, perm]
            shared[f"gqk{i}"] = np.ascontiguousarray(np.stack([gq[0], gq[1], gk[0], gk[1]], axis=1))
            shared[f"sink{i}"] = np.ascontiguousarray(np.broadcast_to(inputs["sink"][i][None, :], (128, 8)))
    for l in layers:
        if l % 2 == 1:
            i = l // 2
            shared[f"w_hy_in{i}"] = inputs["w_hy_in"][i]
            shared[f"w_hy_out{i}"] = inputs["w_hy_out"][i]
            shared[f"hf_w1_{i}"] = inputs["hf_w1"][i]
            shared[f"hf_w2_{i}"] = inputs["hf_w2"][i]
            shared[f"hf_w3_{i}"] = inputs["hf_w3"][i]
            shared[f"hf_vec_{i}"] = np.ascontiguousarray(np.stack(
                [inputs["hf_b1"][i], inputs["hf_freq1"][i], inputs["hf_b2"][i], inputs["hf_freq2"][i]], axis=1))
            shared[f"hy_bias_{i}"] = np.ascontiguousarray(np.broadcast_to(inputs["hy_bias"][i][None], (128, 2, D)))
            shared[f"hy_bin_{i}"] = np.ascontiguousarray(inputs["b_hy_in"][i].reshape(48, 128).T)
            shared[f"hy_wc_{i}"] = np.ascontiguousarray(inputs["w_hy_conv"][i].reshape(3, 48, 128).transpose(2, 0, 1))
            shared[f"hy_bc_{i}"] = np.ascontiguousarray(inputs["b_hy_conv"][i].reshape(48, 128).T)
            shared[f"hy_bout_{i}"] = np.ascontiguousarray(inputs["b_hy_out"][i].reshape(KC, 128).T)
    shared.update(_const_tables())
    shared.update(_hyena_tables(SEQ))
    if 1 in layers:
        shared.update(_hyena_tables(CTX))
    maps = []
    for core in range(NCORES):
        b0 = core * NB
        xin = np.empty((D, NTOK), np.float32)
        for be in range(NB):
            xin[:, be * SEQ:(be + 1) * SEQ] = x[b0 + be].T
            xin[:, NLAT + be * CTX:NLAT + (be + 1) * CTX] = ctx[b0 + be].T
        cT = np.zeros((128, KC, NR), np.float32)
        for be in range(NB):
            cT[:, :, be] = c[b0 + be].reshape(KC, 128).T
        cT[:, :, NB] = c_ctx.reshape(KC, 128).T
        m = dict(shared)
        m["xin"] = xin
        m["cT"] = cT
        maps.append(m)
    return maps


def full_phases(ntl=NB * 4):
    ph = []
    ph.append(("conv", 0, 0, 0))
    for l in range(DEPTH):
        ph.append(("mod", l))
        ph.append(("conv", l, 1, 1))
        ph.append(("ffn", l, 0, 0, l <= 2, ntl))
        if l + 1 < DEPTH:
            ph.append(("conv", l + 1, 0, 0))
        ph.append(("attn" if l % 2 == 0 else "hyena", l))
        ph.append(("ffn", l, 1, 1, l < 2, ntl))
    return ph


def kernel(**inputs):
    maps = _prep_inputs(inputs)
    nc, C = build_program(full_phases())
    names = list(C.decl.keys()) + ["xin", "cT"]
    res = run_bass_kernel_spmd(nc, [{k: m[k] for k in names} for m in maps], core_ids=list(range(NCORES)))
    out = np.empty((16, SEQ, D), np.float32)
    for core in range(NCORES):
        st = res.results[core]["st"]
        for be in range(NB):
            out[core * NB + be] = st[:, be * SEQ:(be + 1) * SEQ].T
    return out
```

```python
import math
from contextlib import ExitStack
import numpy as np
import concourse.bass as bass
import concourse.mybir as mybir
from concourse.bass_utils import run_bass_kernel_spmd

F32 = mybir.dt.float32
BF16 = mybir.dt.bfloat16
ALU = mybir.AluOpType
AF = mybir.ActivationFunctionType
AX = mybir.AxisListType

D = 2048
KC = D // 128
SEQ = 2048
CTX = 256
NCORES = 8
NB = 16 // NCORES
NR = 8
NLAT = NB * SEQ
NTOK = NLAT + NB * CTX
DFF = 5632
FC = DFF // 128
NMOD = 9
DEPTH = 4
HD = 128
EPS = 1e-6
TT = 512


class Buf:
    __slots__ = ("name", "w", "r")

    def __init__(self, name):
        self.name = name
        self.w = {}
        self.r = {}


class Sched:
    def __init__(self, nc, es, ndma=12):
        self.nc = nc
        self.E = {"pe": nc.tensor, "act": nc.scalar, "dve": nc.vector, "pool": nc.gpsimd, "sp": nc.sync}
        self.sems = {}
        self.cnt = {}
        self.waited = {e: {} for e in self.E}
        self.es = es
        for e in ("pe", "act", "dve", "pool"):
            self._mk("c_" + e)
        self.dq = {}
        for q in ("sp", "pool", "bg", "act"):
            self.dq[q] = [0, [self._mk(f"d_{q}_{i}") for i in range(ndma)]]
        self.nops = 0

    def _mk(self, name):
        self.sems[name] = self.es.enter_context(self.nc.semaphore(name))
        self.cnt[name] = 0
        return name

    def _deps(self, reads, writes):
        deps = {}
        for b in reads:
            for s, v in b.w.items():
                if deps.get(s, 0) < v:
                    deps[s] = v
        for b in writes:
            for s, v in b.w.items():
                if deps.get(s, 0) < v:
                    deps[s] = v
            for s, v in b.r.items():
                if deps.get(s, 0) < v:
                    deps[s] = v
        return deps

    def _wait(self, e, deps, skip=None):
        eng = self.E[e]
        wd = self.waited[e]
        for s, v in deps.items():
            if s == skip:
                continue
            if wd.get(s, 0) < v:
                eng.wait_ge(self.sems[s], v)
                wd[s] = v

    def op(self, e, reads, writes, fn, self_dep=True):
        own = "c_" + e
        deps = self._deps(reads, writes)
        self._wait(e, deps, skip=own if (e == "pe" or not self_dep) else None)
        inst = fn(self.E[e])
        self.cnt[own] += 1
        v = self.cnt[own]
        inst.then_inc(self.sems[own], 1)
        for b in reads:
            b.r[own] = v
        for b in writes:
            b.w = {own: v}
            b.r = {}
        self.nops += 1
        return inst

    def dma(self, q, out, in_, reads, writes, partial=False):
        e = q if q in ("sp", "act") else "pool"
        st = self.dq[q]
        name = st[1][st[0] % len(st[1])]
        st[0] += 1
        deps = self._deps(reads, writes)
        prev = self.cnt[name]
        if prev and deps.get(name, 0) < prev:
            deps[name] = prev
        self._wait(e, deps)
        inst = self.E[e].dma_start(out=out, in_=in_)
        self.cnt[name] += 16
        v = self.cnt[name]
        inst.then_inc(self.sems[name], 16)
        for b in reads:
            b.r[name] = v
        for b in writes:
            if partial:
                b.w[name] = v
            else:
                b.w = {name: v}
                b.r = {}
        self.nops += 1
        return inst

    def barrier(self, include_bg=False):
        deps = {}
        for s, v in self.cnt.items():
            if v and (include_bg or not s.startswith("d_bg")):
                deps[s] = v
        for e in self.E:
            self._wait(e, deps)


class _BufMap(dict):
    def __init__(self, name):
        super().__init__()
        self.name = name

    def __missing__(self, key):
        b = Buf(f"{self.name}{key}")
        self[key] = b
        return b


class Ctx:
    def __init__(self):
        self.decl = {}
        self.dbg = {}
        self.uid = 0

    def sbuf(self, name, shape, dtype):
        self.uid += 1
        return self.nc.sbuf_tensor(f"{name}_u{self.uid}", shape, dtype)

    def psum(self, name, shape, dtype):
        self.uid += 1
        return self.nc.psum_tensor(f"{name}_u{self.uid}", shape, dtype)

    def win(self, name, shape, dtype=F32):
        if name not in self.decl:
            self.decl[name] = self.nc.dram_tensor(name, list(shape), dtype, kind="ExternalInput").ap()
        return self.decl[name]

    def dump(self, name, src_ap, shape, dtype, reads):
        if not self.debug:
            return
        t = self.nc.dram_tensor("dbg_" + name, list(shape), dtype, kind="ExternalOutput").ap()
        self.dbg[name] = t
        self.S.dma("sp", t, src_ap, reads, [Buf("dbg_" + name)])


def _mm(S, out_ap, lhsT, rhs, start, stop, reads, writes):
    return S.op("pe", reads, writes, lambda e: e.matmul(out_ap, lhsT, rhs, start=start, stop=stop))


def emit_mod(C, l):
    nc, S = C.nc, C.S
    NCH = NMOD * KC
    GRP = 4
    NG = NCH // GRP
    with ExitStack() as es:
        wsl = [es.enter_context(C.sbuf(f"modw{i}", [128, KC, GRP * 128], BF16)) for i in range(2)]
        wb = [Buf(f"modw{i}") for i in range(2)]
        ps = es.enter_context(C.psum("modps", [128, 3, 512], F32))
        psb = Buf("modps")
        bm = es.enter_context(C.sbuf("modb", [128, NCH], F32))
        bmb = Buf("modb")
        gn = es.enter_context(C.sbuf("modg", [128, 3, KC], F32))
        gnb = Buf("modg")
        S.dma("sp", bm[:], C.win(f"b_mod{l}", [128, NMOD * KC]), [], [bmb])
        S.dma("sp", gn[:], C.win(f"g_norm{l}", [128, 3, KC]), [], [gnb])
        wsrc = C.win(f"w_mod{l}", [D, NMOD * D]).rearrange("(k p) n -> p k n", p=128)

        def load(g):
            S.dma("pool", wsl[g % 2][:], wsrc[:, :, g * 512:(g + 1) * 512], [], [wb[g % 2]])

        load(0)
        for g in range(NG):
            if g + 1 < NG:
                load(g + 1)
            for jj in range(GRP):
                n = g * GRP + jj
                o = ps[:, n // 64, (n % 64) * NR:(n % 64) * NR + NR]
                for k in range(KC):
                    _mm(S, o, wsl[g % 2][:, k, jj * 128:(jj + 1) * 128], C.s_bf[:, k, :], k == 0, k == KC - 1,
                        [wb[g % 2], C.s_b], [psb])
        for r in range(NB + 1):
            for hb in range(3):
                n0 = hb * 64
                cn = min(64, NCH - n0)
                S.op("dve", [psb, bmb], [C.modv_b],
                     lambda e, r=r, hb=hb, n0=n0, cn=cn: e.tensor_tensor(
                         C.modv[:, n0:n0 + cn, r],
                         ps[:, hb, 0:cn * NR].rearrange("p (n r) -> p n r", r=NR)[:, :, r],
                         bm[:, n0:n0 + cn], ALU.add), self_dep=False)
        for k in range(3):
            for r in range(NB + 1):
                S.op("dve", [C.modv_b, gnb], [C.gs_b],
                     lambda e, k=k, r=r: e.scalar_tensor_tensor(
                         C.gs[:, k, :, r], C.modv[:, (3 * k + 1) * KC:(3 * k + 2) * KC, r], 1.0, gn[:, k, :],
                         ALU.add, ALU.mult), self_dep=(k == 0 and r == 0))
            S.op("dve", [C.modv_b], [C.hg_b],
                 lambda e, k=k: e.tensor_scalar(
                     C.hg[:, k, :, :], C.modv[:, (3 * k + 2) * KC:(3 * k + 3) * KC, :],
                     0.5 if k != 1 else 1.0, None, ALU.mult), self_dep=(k == 0))
        C.dump(f"modv{l}", C.modv[:], [128, NMOD * KC, NR], F32, [C.modv_b])
        C.dump(f"gs{l}", C.gs[:], [128, 3, KC, NR], F32, [C.gs_b])
        C.dump(f"hg{l}", C.hg[:], [128, 3, KC, NR], F32, [C.hg_b])
        S.barrier()


def emit_convert_ffn(C, l, w, slot):
    S = C.S
    wi = C.win(f"w_ffn_in{l}_{w}", [D, 2 * DFF])
    wo = C.win(f"w_ffn_out{l}_{w}", [DFF, D])
    for k in range(KC):
        S.dma("bg", C.winb[slot][k * 128:(k + 1) * 128, :], wi[k * 128:(k + 1) * 128, :], [], [C.winb_b[slot]],
              partial=True)
    wo_v = wo.rearrange("(j p) (m c) -> m p j c", p=128, c=128)
    for m in range(KC):
        S.dma("bg", C.woutb[slot][m].rearrange("p (j c) -> p j c", c=128), wo_v[m], [], [C.woutb_b[slot]], partial=True)


def emit_ffn(C, l, w, slot, tiles):
    nc, S = C.nc, C.S
    k_mod = 0 if w == 0 else 2
    JG = 4
    NJG = FC // JG
    winv = C.winb[slot].rearrange("(k p) n -> p k n", p=128)
    with ExitStack() as es:
        x = es.enter_context(C.sbuf("ffn_x", [128, KC, TT], F32))
        xb = [Buf(f"ffn_x{k}") for k in range(KC)]
        h = es.enter_context(C.sbuf("ffn_h", [128, KC, TT], BF16))
        hb = Buf("ffn_h")
        a = es.enter_context(C.sbuf("ffn_a", [128, FC, TT], BF16))
        ab = Buf("ffn_a")
        wins = [es.enter_context(C.sbuf(f"ffn_wi{i}", [128, 2, KC, JG * 128], BF16)) for i in range(2)]
        winb = [Buf(f"ffn_wi{i}") for i in range(2)]
        wouts = [es.enter_context(C.sbuf(f"ffn_wo{i}", [128, FC, 128], BF16)) for i in range(2)]
        woutb = [Buf(f"ffn_wo{i}") for i in range(2)]
        sq = [es.enter_context(C.sbuf(f"ffn_sq{i}", [128, TT], BF16)) for i in range(2)]
        sqb = [Buf(f"ffn_sq{i}") for i in range(2)]
        tmp = [es.enter_context(C.sbuf(f"ffn_tmp{i}", [128, TT], F32)) for i in range(2)]
        tmpb = [Buf(f"ffn_tmp{i}") for i in range(2)]
        sg = [es.enter_context(C.sbuf(f"ffn_sg{i}", [128, TT], F32)) for i in range(2)]
        sgb = [Buf(f"ffn_sg{i}") for i in range(2)]
        rstd = es.enter_context(C.sbuf("ffn_rstd", [128, TT], F32))
        rstdb = Buf("ffn_rstd")
        psn = es.enter_context(C.psum("ffn_psn", [128, TT], F32))
        psnb = Buf("ffn_psn")
        psg = [es.enter_context(C.psum(f"ffn_psg{i}", [128, TT], F32)) for i in range(2)]
        psgb = [Buf(f"ffn_psg{i}") for i in range(2)]
        psu = [es.enter_context(C.psum(f"ffn_psu{i}", [128, TT], F32)) for i in range(2)]
        psub = [Buf(f"ffn_psu{i}") for i in range(2)]
        psy = [es.enter_context(C.psum(f"ffn_psy{i}", [128, TT], F32)) for i in range(2)]
        psyb = [Buf(f"ffn_psy{i}") for i in range(2)]

        stv = C.st.rearrange("(k p) t -> p k t", p=128)

        tasks = []
        for ti in range(len(tiles)):
            for jg in range(NJG):
                tasks.append(("in", jg))
            for m in range(KC):
                tasks.append(("out", m))
        issued = [0]
        cnts = {"in": 0, "out": 0}
        slot_of = {}

        def ensure(upto):
            while issued[0] <= min(upto, len(tasks) - 1):
                i = issued[0]
                kind, idx = tasks[i]
                sl = cnts[kind] % 2
                cnts[kind] += 1
                slot_of[i] = sl
                if kind == "in":
                    for gu in range(2):
                        c0 = gu * DFF + idx * JG * 128
                        S.dma("sp", wins[sl][:, gu, :, :], winv[:, :, c0:c0 + JG * 128], [C.winb_b[slot]], [winb[sl]],
                              partial=(gu == 1))
                else:
                    S.dma("sp", wouts[sl][:], C.woutb[slot][idx].rearrange("p (j c) -> p j c", c=128),
                          [C.woutb_b[slot]], [woutb[sl]])
                issued[0] += 1

        def load_x(ti, k):
            col0 = tiles[ti][0]
            S.dma("act", x[:, k, :], stv[:, k, col0:col0 + TT], [C.st_b], [xb[k]])

        for k in range(KC):
            load_x(0, k)
        ensure(1)
        tcount = 0
        nsq = 0
        ntmp = 0
        nps = 0
        npy = 0
        for ti, (col0, r) in enumerate(tiles):
            for k in range(KC):
                i = nsq % 2
                nsq += 1
                S.op("act", [xb[k]], [sqb[i]], lambda e, k=k, i=i: e.activation(out=sq[i][:], in_=x[:, k, :], func=AF.Square))
                _mm(S, psn[:], C.ones_bf[:], sq[i][:], k == 0, k == KC - 1, [sqb[i], C.const_b], [psnb])
            S.op("act", [psnb], [rstdb], lambda e: e.activation(out=rstd[:], in_=psn[:], func=AF.Ln, bias=C.eps_sb[:, 0:1], scale=1.0))
            S.op("act", [rstdb], [rstdb], lambda e: e.activation(out=rstd[:], in_=rstd[:], func=AF.Exp, scale=-0.5))
            for k in range(KC):
                i = ntmp % 2
                ntmp += 1
                S.op("dve", [xb[k], rstdb, C.gs_b], [tmpb[i]],
                     lambda e, k=k, i=i: e.scalar_tensor_tensor(tmp[i][:], x[:, k, :], C.gs[:, k_mod, k, r:r + 1], rstd[:],
                                                                ALU.mult, ALU.mult))
                S.op("act", [tmpb[i], C.modv_b], [hb],
                     lambda e, k=k, i=i: e.activation(out=h[:, k, :], in_=tmp[i][:], func=AF.Identity,
                                                      bias=C.modv[:, (3 * k_mod) * KC + k, r:r + 1], scale=1.0),
                     self_dep=False)
            if ti == 0:
                C.dump(f"rstd{l}{w}", rstd[:], [128, TT], F32, [rstdb])
                C.dump(f"h{l}{w}", h[:], [128, KC, TT], BF16, [hb])
            for jg in range(NJG):
                ensure(tcount + 1)
                sl = slot_of[tcount]
                tcount += 1
                if ti == 0 and jg == 0:
                    C.dump(f"wins{l}{w}", wins[sl][:], [128, 2, KC, JG * 128], BF16, [winb[sl]])
                for jj in range(JG):
                    j = jg * JG + jj
                    pi = nps % 2
                    nps += 1
                    for k in range(KC):
                        _mm(S, psg[pi][:], wins[sl][:, 0, k, jj * 128:(jj + 1) * 128], h[:, k, :], k == 0, k == KC - 1,
                            [winb[sl], hb], [psgb[pi]])
                    for k in range(KC):
                        _mm(S, psu[pi][:], wins[sl][:, 1, k, jj * 128:(jj + 1) * 128], h[:, k, :], k == 0, k == KC - 1,
                            [winb[sl], hb], [psub[pi]])
                    S.op("act", [psgb[pi]], [sgb[pi]], lambda e, pi=pi: e.activation(out=sg[pi][:], in_=psg[pi][:], func=AF.Silu))
                    if ti == 0 and j == 0:
                        C.dump(f"sg{l}{w}", sg[pi][:], [128, TT], F32, [sgb[pi]])
                    S.op("dve", [sgb[pi], psub[pi]], [ab],
                         lambda e, pi=pi, j=j: e.tensor_tensor(a[:, j, :], sg[pi][:], psu[pi][:], ALU.mult), self_dep=False)
            if ti == 0:
                C.dump(f"a{l}{w}", a[:], [128, FC, TT], BF16, [ab])
            for m in range(KC):
                ensure(tcount + 1)
                sl = slot_of[tcount]
                tcount += 1
                pi = npy % 2
                npy += 1
                for j in range(FC):
                    _mm(S, psy[pi][:], wouts[sl][:, j, :], a[:, j, :], j == 0, j == FC - 1, [woutb[sl], ab], [psyb[pi]])
                S.op("dve", [psyb[pi], xb[m], C.hg_b], [xb[m]],
                     lambda e, pi=pi, m=m: e.scalar_tensor_tensor(x[:, m, :], psy[pi][:], C.hg[:, k_mod, m, r:r + 1], x[:, m, :],
                                                                  ALU.mult, ALU.add))
                S.dma("act", stv[:, m, col0:col0 + TT], x[:, m, :], [xb[m]], [C.st_b], partial=True)
                if ti + 1 < len(tiles):
                    load_x(ti + 1, m)
        S.barrier()


class NormKit:
    def __init__(self, C, es, pfx):
        nc = C.nc
        sb = lambda n, sh, dt_: es.enter_context(C.sbuf(f"{pfx}_{n}", sh, dt_))
        self.xk = [sb(f"xk{i}", [128, TT], F32) for i in range(3)]
        self.xkb = [Buf(f"xk{i}") for i in range(3)]
        self.sq = [sb(f"sq{i}", [128, TT], BF16) for i in range(2)]
        self.sqb = [Buf(f"sq{i}") for i in range(2)]
        self.tmp = [sb(f"tmp{i}", [128, TT], F32) for i in range(2)]
        self.tmpb = [Buf(f"tmp{i}") for i in range(2)]
        self.rstd = sb("rstd", [128, TT], F32)
        self.rstdb = Buf("rstd")
        self.psn = es.enter_context(C.psum(f"{pfx}_psn", [128, TT], F32))
        self.psnb = Buf("psn")
        self.nx = 0
        self.ns = 0
        self.nt = 0

    def emit(self, C, col0, n, r, k_mod, h, hb, hoff=0):
        S = C.S
        stv = C.st.rearrange("(k p) t -> p k t", p=128)
        for k in range(KC):
            i = self.nx % 3
            self.nx += 1
            S.dma("pool", self.xk[i][:, :n], stv[:, k, col0:col0 + n], [C.st_b], [self.xkb[i]])
            j = self.ns % 2
            self.ns += 1
            S.op("act", [self.xkb[i]], [self.sqb[j]],
                 lambda e, i=i, j=j: e.activation(out=self.sq[j][:, :n], in_=self.xk[i][:, :n], func=AF.Square))
            _mm(S, self.psn[:, :n], C.ones_bf[:], self.sq[j][:, :n], k == 0, k == KC - 1, [self.sqb[j], C.const_b], [self.psnb])
        S.op("act", [self.psnb], [self.rstdb],
             lambda e: e.activation(out=self.rstd[:, :n], in_=self.psn[:, :n], func=AF.Ln, bias=C.eps_sb[:, 0:1], scale=1.0))
        S.op("act", [self.rstdb], [self.rstdb], lambda e: e.activation(out=self.rstd[:, :n], in_=self.rstd[:, :n], func=AF.Exp, scale=-0.5))
        for k in range(KC):
            i = self.nx % 3
            self.nx += 1
            S.dma("pool", self.xk[i][:, :n], stv[:, k, col0:col0 + n], [C.st_b], [self.xkb[i]])
            j = self.nt % 2
            self.nt += 1
            S.op("dve", [self.xkb[i], self.rstdb, C.gs_b], [self.tmpb[j]],
                 lambda e, i=i, j=j, k=k: e.scalar_tensor_tensor(self.tmp[j][:, :n], self.xk[i][:, :n], C.gs[:, k_mod, k, r:r + 1],
                                                                 self.rstd[:, :n], ALU.mult, ALU.mult))
            S.op("act", [self.tmpb[j], C.modv_b], [hb],
                 lambda e, j=j, k=k: e.activation(out=h[:, k, hoff:hoff + n], in_=self.tmp[j][:, :n], func=AF.Identity,
                                                  bias=C.modv[:, (3 * k_mod) * KC + k, r:r + 1], scale=1.0), self_dep=False)


class WStream:
    def __init__(self, S, slots, seq, src_buf, q="sp"):
        self.S = S
        self.slots = slots
        self.bufs = [Buf(f"ws{i}") for i in range(len(slots))]
        self.seq = seq
        self.src_buf = src_buf
        self.q = q
        self.issued = 0
        self.i = 0

    def _issue(self):
        k = self.issued
        item = self.seq[k]
        if not isinstance(item, list):
            item = [item]
        sl = k % 2
        for ii, (src, dst_fn) in enumerate(item):
            self.S.dma(self.q, dst_fn(self.slots[sl]), src, [self.src_buf], [self.bufs[sl]], partial=(ii > 0))
        self.issued += 1

    def get(self):
        while self.issued <= min(self.i + 1, len(self.seq) - 1):
            self._issue()
        sl = self.i % 2
        self.i += 1
        return sl


A_QW = 2048
A_KOFF = 2048
A_VOFF = 2560
NKT = (CTX + SEQ) // 128


def emit_attn(C, l):
    nc, S = C.nc, C.S
    i = l // 2
    ctx_out = (l == 0)
    w_in = C.win(f"w_attn_in{i}", [D, 3072])
    w_out = C.win(f"w_attn_out{i}", [D, D])
    gqk_d = C.win(f"gqk{i}", [128, 4])
    sink_d = C.win(f"sink{i}", [128, 8])
    rope_d = C.win("rope_cs", [2, 128, SEQ])
    rotm_d = C.win("rotm", [128, 128], BF16)
    mask_d = C.win("wmask", [128, 2, 128], BF16)
    for k in range(KC):
        S.dma("bg", C.wainb[k * 128:(k + 1) * 128, :], w_in[k * 128:(k + 1) * 128, :], [], [C.wainb_b], partial=True)
    wo_v = w_out.rearrange("(j p) (m c) -> m p j c", p=128, c=128)
    for m in range(KC):
        S.dma("bg", C.waob[m].rearrange("p (j c) -> p j c", c=128), wo_v[m], [], [C.waob_b], partial=True)
    wainv = C.wainb.rearrange("(k p) n -> p k n", p=128)
    stv = C.st.rearrange("(k p) t -> p k t", p=128)
    SCALE = float(HD) ** -0.5

    with ExitStack() as es:
        sbt = lambda n, sh, dt_: es.enter_context(C.sbuf("at_" + n, sh, dt_))
        pst = lambda n: es.enter_context(C.psum("at_" + n, [128, TT], F32))
        kit = NormKit(C, es, "at")
        h = sbt("h", [128, KC, TT], BF16)
        hb = Buf("h")
        KT = [sbt(f"kt{s_}", [128, 2, CTX + SEQ], BF16) for s_ in range(2)]
        KTb = [Buf(f"kt{s_}") for s_ in range(2)]
        V = sbt("v", [128, NKT, 512], BF16)
        Vb = Buf("v")
        OT = sbt("ot", [128, 16, TT], BF16)
        OTb = Buf("ot")
        wsl = [sbt(f"w{j}", [128, KC, 512], BF16) for j in range(2)]
        wos = [sbt(f"wo{j}", [128, 16, 128], BF16) for j in range(2)]
        ropeC = sbt("ropec", [128, SEQ], F32)
        ropeS = sbt("ropes", [128, SEQ], F32)
        rotm = sbt("rotm", [128, 128], BF16)
        masks = sbt("masks", [128, 2, 128], BF16)
        gqk = sbt("gqk", [128, 4], F32)
        esink = sbt("esink", [128, 8], F32)
        ones1 = sbt("ones1", [128, 128], BF16)
        onesh = sbt("onesh", [128, 128], BF16)
        cb = Buf("at_const")
        S.dma("sp", ropeC[:], rope_d[0], [], [cb])
        S.dma("sp", ropeS[:], rope_d[1], [], [cb], partial=True)
        S.dma("sp", rotm[:], rotm_d, [], [cb], partial=True)
        S.dma("sp", masks[:], mask_d, [], [cb], partial=True)
        S.dma("sp", gqk[:], gqk_d, [], [cb], partial=True)
        S.dma("sp", esink[:], sink_d, [], [cb], partial=True)
        S.op("act", [cb], [cb], lambda e: e.activation(out=esink[:], in_=esink[:], func=AF.Exp))
        S.op("dve", [cb], [cb], lambda e: e.memset(ones1[:], 1.0))
        S.op("dve", [cb], [cb], lambda e: e.memset(onesh[:], 1.0 / HD))
        sqh = [sbt(f"sqh{j}", [128, TT], BF16) for j in range(2)]
        sqhb = [Buf(f"sqh{j}") for j in range(2)]
        rs = sbt("rs", [128, TT], F32)
        rsb = Buf("rs")
        tgb = [sbt(f"tgb{j}", [128, TT], BF16) for j in range(2)]
        tgbb = [Buf(f"tgb{j}") for j in range(2)]
        u1 = sbt("u1", [128, TT], F32)
        u1b = Buf("u1")
        u2 = sbt("u2", [128, TT], F32)
        u2b = Buf("u2")
        qb_ = [sbt(f"q{j}", [128, TT], BF16) for j in range(2)]
        qbb = [Buf(f"q{j}") for j in range(2)]
        pT = [sbt(f"pT{j}", [128, TT], BF16) for j in range(2)]
        pTb = [Buf(f"pT{j}") for j in range(2)]
        rden = sbt("rden", [128, TT], F32)
        rdenb = Buf("rden")
        xres = [sbt(f"xres{j}", [128, TT], F32) for j in range(2)]
        xresb = [Buf(f"xres{j}") for j in range(2)]
        pt = [pst(f"pt{j}") for j in range(2)]
        ptb = [Buf(f"pt{j}") for j in range(2)]
        pr = pst("pr")
        prb = Buf("pr")
        ps = [pst(f"ps{j}") for j in range(2)]
        psb = [Buf(f"ps{j}") for j in range(2)]
        po = pst("po")
        pob = Buf("po")
        pd = pst("pd")
        pdb = Buf("pd")
        cnt = {"pt": 0, "sqh": 0, "tgb": 0, "q": 0, "ps": 0, "xres": 0}

        def rot(name, nbuf=2):
            v = cnt[name] % nbuf
            cnt[name] += 1
            return v

        def p1_tiles(be):
            t = [(NLAT + be * CTX, CTX, NB, 0, None)]
            for q in range(SEQ // TT):
                t.append((be * SEQ + q * TT, TT, be, CTX + q * TT, q * TT))
            return t

        def p2_tiles(be):
            t = []
            if ctx_out:
                t.append((NLAT + be * CTX, CTX, NB, None, None))
            for q in range(SEQ // TT):
                t.append((be * SEQ + q * TT, TT, be, q * TT, q))
            return t

        in_seq = []
        out_seq = []
        for be in C.be_list:
            for _ in p1_tiles(be):
                for c in range(4):
                    in_seq.append((wainv[:, :, A_KOFF + c * 128:A_KOFF + (c + 1) * 128], lambda sl: sl[:, :, 0:128]))
                in_seq.append((wainv[:, :, A_VOFF:A_VOFF + 512], lambda sl: sl[:, :, :]))
            for _ in p2_tiles(be):
                for hd in range(16):
                    in_seq.append((wainv[:, :, hd * 128:(hd + 1) * 128], lambda sl: sl[:, :, 0:128]))
                for m in range(KC):
                    out_seq.append((C.waob[m].rearrange("p (j c) -> p j c", c=128), lambda sl: sl[:, :, :]))
        win_s = WStream(S, wsl, in_seq, C.wainb_b)
        wo_s = WStream(S, wos, out_seq, C.waob_b)

        def qk_finish(pi, n, gcol, rope_off, dst_ap, dst_buf):
            j = rot("sqh")
            S.op("act", [ptb[pi]], [sqhb[j]], lambda e: e.activation(out=sqh[j][:, :n], in_=pt[pi][:, :n], func=AF.Square))
            _mm(S, kit.psn[:, :n], onesh[:], sqh[j][:, :n], True, True, [sqhb[j], cb], [kit.psnb])
            S.op("act", [kit.psnb], [rsb],
                 lambda e: e.activation(out=rs[:, :n], in_=kit.psn[:, :n], func=AF.Ln, bias=C.eps_sb[:, 0:1], scale=1.0))
            S.op("act", [rsb], [rsb], lambda e: e.activation(out=rs[:, :n], in_=rs[:, :n], func=AF.Exp, scale=-0.5))
            if rope_off is None:
                S.op("dve", [ptb[pi], rsb, cb], [dst_buf],
                     lambda e: e.scalar_tensor_tensor(dst_ap, pt[pi][:, :n], gqk[:, gcol:gcol + 1], rs[:, :n], ALU.mult, ALU.mult))
                return
            t = rot("tgb")
            S.op("dve", [ptb[pi], rsb, cb], [tgbb[t]],
                 lambda e: e.scalar_tensor_tensor(tgb[t][:, :n], pt[pi][:, :n], gqk[:, gcol:gcol + 1], rs[:, :n], ALU.mult, ALU.mult))
            _mm(S, pr[:, :n], rotm[:], tgb[t][:, :n], True, True, [tgbb[t], cb], [prb])
            S.op("dve", [tgbb[t], cb], [u1b], lambda e: e.tensor_tensor(u1[:, :n], tgb[t][:, :n], ropeC[:, rope_off:rope_off + n], ALU.mult))
            S.op("dve", [prb, cb], [u2b], lambda e: e.tensor_tensor(u2[:, :n], pr[:, :n], ropeS[:, rope_off:rope_off + n], ALU.mult))
            S.op("dve", [u1b, u2b], [dst_buf], lambda e: e.tensor_tensor(dst_ap, u1[:, :n], u2[:, :n], ALU.add))

        for be in C.be_list:
            for (col0, n, r, key_off, rope_off) in p1_tiles(be):
                kit.emit(C, col0, n, r, 1, h, hb)
                for c in range(4):
                    sl = win_s.get()
                    pi = rot("pt")
                    for k in range(KC):
                        _mm(S, pt[pi][:, :n], wsl[sl][:, k, 0:128], h[:, k, :n], k == 0, k == KC - 1, [win_s.bufs[sl], hb], [ptb[pi]])
                    sset = c // 2
                    qk_finish(pi, n, 2 + sset, rope_off, KT[sset][:, c % 2, key_off:key_off + n], KTb[sset])
                sl = win_s.get()
                for sub in range(n // 128):
                    pi = rot("pt")
                    for k in range(KC):
                        _mm(S, pt[pi][:, :], h[:, k, sub * 128:(sub + 1) * 128], wsl[sl][:, k, :], k == 0, k == KC - 1,
                            [win_s.bufs[sl], hb], [ptb[pi]])
                    kt = key_off // 128 + sub
                    S.op("act", [ptb[pi]], [Vb], lambda e, pi=pi, kt=kt: e.copy(V[:, kt, :], pt[pi][:, :]), self_dep=False)
            for (col0, n, r, rope_off, qt) in p2_tiles(be):
                is_ctx = (qt is None)
                kit.emit(C, col0, n, r, 1, h, hb)
                for hd in range(16):
                    sset = 0 if hd < 8 else 1
                    kv = (hd % 8) // 4
                    sl = win_s.get()
                    pi = rot("pt")
                    for k in range(KC):
                        _mm(S, pt[pi][:, :n], wsl[sl][:, k, 0:128], h[:, k, :n], k == 0, k == KC - 1, [win_s.bufs[sl], hb], [ptb[pi]])
                    qi = rot("q")
                    qk_finish(pi, n, sset, rope_off, qb_[qi][:, :n], qbb[qi])
                    keys = []
                    if is_ctx:
                        keys = [(0, 0, n, []), (1, 0, n, [])]
                    elif sset == 0:
                        keys = [(kt, 0, n, []) for kt in range(NKT)]
                    else:
                        keys = [(0, 0, n, []), (1, 0, n, [])]
                        qb0 = qt * 4
                        for kb in range(max(0, qb0 - 1), min(15, qb0 + 4) + 1):
                            lo = max(kb - 1, qb0)
                            hi = min(kb + 1, qb0 + 3)
                            if lo > hi:
                                continue
                            mk = []
                            for qbk in range(lo, hi + 1):
                                if qbk == kb + 1:
                                    mk.append(((qbk - lo) * 128, 0))
                                elif qbk == kb - 1:
                                    mk.append(((qbk - lo) * 128, 1))
                            keys.append((2 + kb, (lo - qb0) * 128, (hi - qb0 + 1) * 128, mk))
                    slot_a = {}

                    def emit_s(ki):
                        kt, c0, c1, mk = keys[ki]
                        w_ = c1 - c0
                        a = rot("ps")
                        slot_a[ki] = a
                        _mm(S, ps[a][:, :w_], KT[sset][:, kv, kt * 128:(kt + 1) * 128], qb_[qi][:, c0:c1], True, True,
                            [KTb[sset], qbb[qi]], [psb[a]])
                        S.op("act", [psb[a]], [pTb[a]],
                             lambda e: e.activation(out=pT[a][:, :w_], in_=ps[a][:, :w_], func=AF.Exp, scale=SCALE))
                        for (off, which) in mk:
                            S.op("pool", [pTb[a], cb], [pTb[a]],
                                 lambda e, off=off, which=which: e.tensor_tensor(pT[a][:, off:off + 128], pT[a][:, off:off + 128],
                                                                                 masks[:, which, :], ALU.mult))

                    def emit_pv(ki):
                        kt, c0, c1, mk = keys[ki]
                        w_ = c1 - c0
                        a = slot_a[ki]
                        vcol = sset * 256 + kv * 128
                        _mm(S, po[:, c0:c1], V[:, kt, vcol:vcol + 128], pT[a][:, :w_], ki == 0, ki == len(keys) - 1, [Vb, pTb[a]], [pob])
                        _mm(S, pd[:, c0:c1], ones1[:], pT[a][:, :w_], ki == 0, ki == len(keys) - 1, [cb, pTb[a]], [pdb])

                    emit_s(0)
                    for ki in range(len(keys)):
                        if ki + 1 < len(keys):
                            emit_s(ki + 1)
                        emit_pv(ki)
                    if sset == 1:
                        S.op("dve", [pdb, cb], [rdenb],
                             lambda e, hd=hd: e.tensor_scalar(rden[:, :n], pd[:, :n], esink[:, hd - 8:hd - 7], None, ALU.add))
                        S.op("dve", [rdenb], [rdenb], lambda e: e.reciprocal(rden[:, :n], rden[:, :n]))
                    else:
                        S.op("dve", [pdb], [rdenb], lambda e: e.reciprocal(rden[:, :n], pd[:, :n]))
                    S.op("dve", [pob, rdenb], [OTb], lambda e, hd=hd: e.tensor_tensor(OT[:, hd, :n], po[:, :n], rden[:, :n], ALU.mult))
                for m in range(KC):
                    sl = wo_s.get()
                    pi = rot("pt")
                    for hd in range(16):
                        _mm(S, pt[pi][:, :n], wos[sl][:, hd, :], OT[:, hd, :n], hd == 0, hd == 15, [wo_s.bufs[sl], OTb], [ptb[pi]])
                    xi = rot("xres")
                    S.dma("pool", xres[xi][:, :n], stv[:, m, col0:col0 + n], [C.st_b], [xresb[xi]])
                    S.op("dve", [ptb[pi], xresb[xi], C.hg_b], [xresb[xi]],
                         lambda e, pi=pi, xi=xi, m=m: e.scalar_tensor_tensor(xres[xi][:, :n], pt[pi][:, :n], C.hg[:, 1, m, r:r + 1],
                                                                            xres[xi][:, :n], ALU.mult, ALU.add))
                    S.dma("pool", stv[:, m, col0:col0 + n], xres[xi][:, :n], [xresb[xi]], [C.st_b], partial=True)
        S.barrier()


def _hy_sets(l):
    sets = [(0, SEQ)]
    if l == 1:
        sets.append((1, CTX))
    return sets


def _hy_seq_info(C, nset, be):
    if nset == 0:
        return be, be * SEQ, be
    return NB + be, NLAT + be * CTX, NB


def _hy_dft(C, n):
    nch = n // 128
    t = {}
    for nm in ("cf", "sf", "si"):
        t[nm] = C.win(f"dft_{nm}{n}", [nch, 128, nch * 128], BF16)
    for nm in ("cr", "sr"):
        t[nm] = C.win(f"dft_{nm}{n}", [n, n], BF16)
    return t


def emit_hyena(C, l):
    nc, S = C.nc, C.S
    i = l // 2
    sets = _hy_sets(l)
    w_in = C.win(f"w_hy_in{i}", [D, 3 * D])
    w_out = C.win(f"w_hy_out{i}", [D, D])
    for k in range(KC):
        S.dma("bg", C.whinb[k * 128:(k + 1) * 128, :], w_in[k * 128:(k + 1) * 128, :], [], [C.whinb_b], partial=True)
    wo_v = w_out.rearrange("(j p) (m c) -> m p j c", p=128, c=128)
    for m in range(KC):
        S.dma("bg", C.waob[m].rearrange("p (j c) -> p j c", c=128), wo_v[m], [], [C.waob_b], partial=True)
    emit_hy_filter(C, l)
    emit_hy_inproj(C, l)
    emit_hy_conv(C, l)
    emit_hy_outproj(C, l)


def _dft_forward(C, S, fw, u, ub, n, psA, psAb, psB, psBb, consumer, need_a=True):
    nch = n // 128
    for fc in range(nch):
        sl = fw.get()
        a = fc % 2
        if need_a:
            for sc in range(nch):
                _mm(S, psA[a][:, :], fw.slots[sl][:, 0, sc * 128:(sc + 1) * 128], u[:, sc, :], sc == 0, sc == nch - 1,
                    [fw.bufs[sl], ub], [psAb[a]])
        for sc in range(nch):
            _mm(S, psB[a][:, :], fw.slots[sl][:, 1, sc * 128:(sc + 1) * 128], u[:, sc, :], sc == 0, sc == nch - 1,
                [fw.bufs[sl], ub], [psBb[a]])
        consumer(fc, a)


def emit_hy_filter(C, l):
    nc, S = C.nc, C.S
    i = l // 2
    w1_d = C.win(f"hf_w1_{i}", [33, 64])
    w2_d = C.win(f"hf_w2_{i}", [64, 64])
    w3_d = C.win(f"hf_w3_{i}", [64, 4 * D])
    hfv_d = C.win(f"hf_vec_{i}", [64, 4])
    bias_d = C.win(f"hy_bias_{i}", [128, 2, D])
    absd_d = C.win("hy_absdelta", [128, D])
    for (nset, n) in _hy_sets(l):
        nch = n // 128
        NN = 2 * n
        dft = _hy_dft(C, n)
        feats_d = C.win(f"hy_feats{n}", [33, n])
        negt_d = C.win(f"hy_negt{n}", [128, nch])
        wf_d = C.win(f"hy_wf{n}", [128, 2, nch])
        W = min(512, n)
        with ExitStack() as es:
            sbt = lambda nm, sh, dt_: es.enter_context(C.sbuf("hf_" + nm, sh, dt_))
            pst = lambda nm: es.enter_context(C.psum("hf_" + nm, [128, 512], F32))
            cb = Buf("hf_const")
            w1 = sbt("w1", [33, 64], F32)
            w2 = sbt("w2", [64, 64], F32)
            w3 = sbt("w3", [64, 2, 512], F32)
            w3b = Buf("w3")
            hfv = sbt("hfv", [64, 4], F32)
            sc = sbt("sc", [64, 4], F32)
            feats = sbt("feats", [33, n], F32)
            bias = sbt("bias", [128, 512], F32)
            absd = sbt("absd", [128, 512], F32)
            negt = sbt("negt", [128, nch], F32)
            wf = sbt("wf", [128, 2, nch], F32)
            ones1 = sbt("ones1", [128, 128], BF16)
            for dst, src in ((w1, w1_d), (w2, w2_d), (hfv, hfv_d), (feats, feats_d), (negt, negt_d), (wf, wf_d)):
                S.dma("sp", dst[:], src, [], [cb], partial=True)
            S.op("dve", [cb], [cb], lambda e: e.memset(ones1[:], 1.0))
            for j in range(2):
                S.op("dve", [cb], [cb], lambda e, j=j: e.tensor_scalar(sc[:, 2 * j:2 * j + 1], hfv[:, 2 * j + 1:2 * j + 2], 1.0 / 3.0, None, ALU.mult))
                S.op("dve", [cb], [cb], lambda e, j=j: e.tensor_tensor(sc[:, 2 * j + 1:2 * j + 2], sc[:, 2 * j:2 * j + 1], hfv[:, 2 * j:2 * j + 1], ALU.mult))
            h1 = sbt("h1", [64, n], F32)
            h1b = Buf("h1")
            h2 = sbt("h2", [64, n], F32)
            h2b = Buf("h2")
            s3 = sbt("s3", [64, W], F32)
            s3b = Buf("s3")
            tq = sbt("tq", [64, W], F32)
            tqb = Buf("tq")
            pz = pst("pz")
            pzb = Buf("pz")
            for (lhs, src, srcb, dst, dstb, j) in ((w1, feats, cb, h1, h1b, 0), (w2, h1, h1b, h2, h2b, 1)):
                for c0 in range(0, n, W):
                    _mm(S, pz[0:64, :W], lhs[:], src[:, c0:c0 + W], True, True, [cb, srcb], [pzb])
                    S.op("act", [pzb, cb], [s3b], lambda e, j=j: e.activation(out=s3[:], in_=pz[0:64, :W], func=AF.Sin,
                                                                              bias=sc[:, 2 * j + 1:2 * j + 2], scale=sc[:, 2 * j:2 * j + 1]))
                    S.op("dve", [s3b], [tqb], lambda e: e.tensor_tensor(tq[:], s3[:], s3[:], ALU.mult))
                    S.op("dve", [tqb], [tqb], lambda e: e.tensor_scalar(tq[:], tq[:], -4.0, 3.0, ALU.mult, ALU.add))
                    S.op("dve", [tqb, s3b], [dstb], lambda e, dst=dst, c0=c0: e.tensor_tensor(dst[:, c0:c0 + W], s3[:], tq[:], ALU.mult))
            kf = sbt("kf", [128, nch, 512], F32)
            kfb = Buf("kf")
            kb = sbt("kb", [128, nch, 512], F32)
            kbb = Buf("kb")
            eT = sbt("e", [128, nch, 512], BF16)
            eb = Buf("e")
            gT = sbt("g", [128, nch, 512], BF16)
            gb = Buf("g")
            kre2 = [sbt(f"kre{j}", [128, 512], F32) for j in range(2)]
            kre2b = [Buf(f"kre{j}") for j in range(2)]
            kim2 = [sbt(f"kim{j}", [128, 512], F32) for j in range(2)]
            kim2b = [Buf(f"kim{j}") for j in range(2)]
            krs = sbt("krs", [128, 512], F32)
            krsb = Buf("krs")
            dec = sbt("dec", [128, 512], F32)
            decb = Buf("dec")
            sq = [sbt(f"sq{j}", [128, 512], BF16) for j in range(2)]
            sqb = [Buf(f"sq{j}") for j in range(2)]
            scl = sbt("scl", [128, 512], F32)
            sclb = Buf("scl")
            tmp = sbt("tmp", [128, 512], F32)
            tmpb = Buf("tmp")
            slots = [sbt(f"dsl{j}", [128, 2, nch * 128], BF16) for j in range(2)]
            pk = [pst(f"pk{j}") for j in range(2)]
            pkb = [Buf(f"pk{j}") for j in range(2)]
            pn = pst("pn")
            pnb = Buf("pn")
            psA = [pst(f"pa{j}") for j in range(2)]
            psAb = [Buf(f"pa{j}") for j in range(2)]
            psB = [pst(f"pb{j}") for j in range(2)]
            psBb = [Buf(f"pb{j}") for j in range(2)]
            seq = []
            for o in range(2):
                for dt in range(4):
                    for rep in range(2):
                        for fc in range(nch):
                            seq.append([(dft["cf"][fc], lambda sl: sl[:, 0, :]), (dft["sf"][fc], lambda sl: sl[:, 1, :])])
            fw = WStream(S, slots, seq, Buf("dftc"))
            nsq = [0]
            for o in range(2):
                for dt in range(4):
                    dsl = slice(dt * 512, (dt + 1) * 512)
                    ob = Buf("hf_odt")
                    for dr in range(2):
                        c0 = o * 2 * D + dr * D + dt * 512
                        S.dma("sp", w3[:, dr, :], w3_d[:, c0:c0 + 512], [], [w3b], partial=(dr == 1))
                    S.dma("sp", bias[:], bias_d[:, o, dsl], [], [cb], partial=True)
                    S.dma("sp", absd[:], absd_d[:, dsl], [], [cb], partial=True)
                    for lc in range(nch):
                        S.op("act", [cb], [decb], lambda e, lc=lc: e.activation(out=dec[:], in_=absd[:], func=AF.Exp, scale=negt[:, lc:lc + 1]))
                        for dr, (kt_, ktb_) in enumerate(((kf, kfb), (kb, kbb))):
                            a = (2 * lc + dr) % 2
                            _mm(S, pk[a][:, :], h2[:, lc * 128:(lc + 1) * 128], w3[:, dr, :], True, True, [h2b, w3b], [pkb[a]])
                            S.op("dve", [pkb[a], decb], [ktb_], lambda e, kt_=kt_, a=a, lc=lc: e.tensor_tensor(kt_[:, lc, :], pk[a][:, :], dec[:], ALU.mult))
                    S.op("dve", [kbb], [kbb], lambda e: e.memset(kb[0:1, 0, :], 0.0))
                    tot = 2 * nch
                    cnt_ = 0
                    for (kt_, ktb_) in ((kf, kfb), (kb, kbb)):
                        for lc in range(nch):
                            j = nsq[0] % 2
                            nsq[0] += 1
                            S.op("act", [ktb_], [sqb[j]], lambda e, kt_=kt_, lc=lc, j=j: e.activation(out=sq[j][:], in_=kt_[:, lc, :], func=AF.Square))
                            _mm(S, pn[:, :], ones1[:], sq[j][:], cnt_ == 0, cnt_ == tot - 1, [sqb[j], cb], [pnb])
                            cnt_ += 1
                    S.op("act", [pnb], [sclb], lambda e: e.activation(out=scl[:], in_=pn[:, :], func=AF.Sqrt, bias=C.eps_sb[:, 0:1], scale=1.0))
                    S.op("dve", [sclb], [sclb], lambda e: e.reciprocal(scl[:], scl[:]))
                    for lc in range(nch):
                        S.op("dve", [kfb, kbb], [tmpb], lambda e, lc=lc: e.tensor_tensor(tmp[:], kf[:, lc, :], kb[:, lc, :], ALU.add))
                        S.op("dve", [tmpb, sclb], [eb], lambda e, lc=lc: e.tensor_tensor(eT[:, lc, :], tmp[:], scl[:], ALU.mult))
                        S.op("dve", [kfb, kbb], [tmpb], lambda e, lc=lc: e.tensor_tensor(tmp[:], kf[:, lc, :], kb[:, lc, :], ALU.subtract))
                        S.op("dve", [tmpb, sclb], [gb], lambda e, lc=lc: e.tensor_tensor(gT[:, lc, :], tmp[:], scl[:], ALU.mult))

                    ktb = C.ktab_b[nset]

                    def cons_e(fc, a, o=o, dt=dt):
                        j = fc % 2
                        S.op("dve", [psAb[a], cb], [tmpb], lambda e: e.tensor_tensor(tmp[:], psA[a][:, :], bias[:], ALU.add))
                        S.op("dve", [tmpb, cb], [kre2b[j]], lambda e: e.tensor_scalar(kre2[j][:], tmp[:], wf[:, 0, fc:fc + 1], None, ALU.mult))
                        if fc == 0:
                            S.op("dve", [kre2b[j]], [krsb], lambda e: e.tensor_copy(krs[:], kre2[j][:]))
                            S.op("dve", [psBb[a], cb], [tmpb], lambda e: e.tensor_tensor(tmp[0:1, :], psB[a][0:1, :], bias[0:1, :], ALU.add))
                            S.op("dve", [tmpb], [krsb], lambda e: e.tensor_scalar(krs[0:1, :], tmp[0:1, :], 1.0 / NN, None, ALU.mult))
                            S.dma("sp", C.krs0[nset, o, dt], krs[:], [krsb], [ktb], partial=True)
                        S.dma("sp", C.ktab[nset, o, 0, dt][:, fc * 512:(fc + 1) * 512], kre2[j][:], [kre2b[j]], [ktb], partial=True)

                    def cons_g(fc, a, o=o, dt=dt):
                        j = fc % 2
                        S.op("dve", [psBb[a], cb], [kim2b[j]], lambda e: e.tensor_scalar(kim2[j][:], psB[a][:, :], wf[:, 1, fc:fc + 1], None, ALU.mult))
                        if fc == 0:
                            S.op("dve", [kim2b[j]], [kim2b[j]], lambda e: e.memset(kim2[j][0:1, :], 0.0))
                        S.dma("sp", C.ktab[nset, o, 1, dt][:, fc * 512:(fc + 1) * 512], kim2[j][:], [kim2b[j]], [ktb], partial=True)

                    _dft_forward(C, S, fw, eT, eb, n, psA, psAb, psB, psBb, cons_e, need_a=True)
                    _dft_forward(C, S, fw, gT, gb, n, psA, psAb, psB, psBb, cons_g, need_a=False)
            S.barrier()


def emit_hy_inproj(C, l):
    nc, S = C.nc, C.S
    i = l // 2
    bin_d = C.win(f"hy_bin_{i}", [128, 48])
    wc_d = C.win(f"hy_wc_{i}", [128, 3, 48])
    bc_d = C.win(f"hy_bc_{i}", [128, 48])
    ident_d = C.win("ident_bf", [128, 128], BF16)
    whv = C.whinb.rearrange("(k p) n -> p k n", p=128)
    for (nset, n) in _hy_sets(l):
        nch = n // 128
        W = min(512, n)
        with ExitStack() as es:
            sbt = lambda nm, sh, dt_: es.enter_context(C.sbuf("hi_" + nm, sh, dt_))
            pst = lambda nm: es.enter_context(C.psum("hi_" + nm, [128, 512], F32))
            kit = NormKit(C, es, "hi")
            cb = Buf("hi_const")
            binv = sbt("bin", [128, 48], F32)
            wcv = sbt("wc", [128, 3, 48], F32)
            bcv = sbt("bc", [128, 48], F32)
            ident = sbt("ident", [128, 128], BF16)
            for dst, src in ((binv, bin_d), (wcv, wc_d), (bcv, bc_d), (ident, ident_d)):
                S.dma("sp", dst[:], src, [], [cb], partial=True)
            h = sbt("h", [128, KC, n], BF16)
            hb = Buf("h")
            pf2 = [sbt(f"pf{j}", [128, n + 2], F32) for j in range(2)]
            pf2b = [Buf(f"pf{j}") for j in range(2)]
            pc = sbt("pc", [128, n], F32)
            pcb = Buf("pc")
            pcb16 = sbt("pc16", [128, n], BF16)
            pc16b = Buf("pc16")
            stg = sbt("stg", [128, nch, 512], BF16)
            stgb = Buf("stg")
            wsl = [sbt(f"w{j}", [128, KC, 128], BF16) for j in range(2)]
            pp = [pst(f"pp{j}") for j in range(2)]
            ppb = [Buf(f"pp{j}") for j in range(2)]
            ptr = [es.enter_context(C.psum(f"hi_ptr{j}", [128, 512], BF16)) for j in range(2)]
            ptrb = [Buf(f"ptr{j}") for j in range(2)]
            for j in range(2):
                S.op("dve", [], [pf2b[j]], lambda e, j=j: e.memset(pf2[j][:], 0.0))
            seq = []
            for be in C.be_list:
                for c in range(48):
                    seq.append((whv[:, :, c * 128:(c + 1) * 128], lambda sl: sl[:, :, :]))
            ws = WStream(S, wsl, seq, C.whinb_b)
            npp = [0]
            ntr = [0]
            for be in C.be_list:
                q, col0, r = _hy_seq_info(C, nset, be)
                for t0 in range(0, n, W):
                    kit.emit(C, col0 + t0, W, r, 1, h, hb, hoff=t0)
                for c in range(48):
                    sl = ws.get()
                    pf = pf2[c % 2]
                    pfb = pf2b[c % 2]
                    for t0 in range(0, n, W):
                        a = npp[0] % 2
                        npp[0] += 1
                        for k in range(KC):
                            _mm(S, pp[a][:, :W], wsl[sl][:, k, :], h[:, k, t0:t0 + W], k == 0, k == KC - 1, [ws.bufs[sl], hb], [ppb[a]])
                        S.op("act", [ppb[a], cb], [pfb], lambda e, a=a, t0=t0, c=c: e.activation(out=pf[:, 1 + t0:1 + t0 + W], in_=pp[a][:, :W],
                                                                                              func=AF.Identity, bias=binv[:, c:c + 1], scale=1.0),
                             self_dep=False)
                    S.op("act", [pfb, cb], [pcb], lambda e, c=c: e.activation(out=pc[:], in_=pf[:, 1:n + 1], func=AF.Identity,
                                                                             bias=bcv[:, c:c + 1], scale=wcv[:, 1, c:c + 1]))
                    S.op("dve", [pfb, pcb, cb], [pcb], lambda e, c=c: e.scalar_tensor_tensor(pc[:], pf[:, 0:n], wcv[:, 0, c:c + 1], pc[:], ALU.mult, ALU.add))
                    S.op("dve", [pfb, pcb, cb], [pc16b], lambda e, c=c: e.scalar_tensor_tensor(pcb16[:], pf[:, 2:n + 2], wcv[:, 2, c:c + 1], pc[:], ALU.mult, ALU.add))
                    if c >= 32:
                        S.dma("pool", C.hy_x2T[q, c - 32][:, :n], pcb16[:], [pc16b], [C.hy_x2T_b], partial=True)
                        continue
                    dcol = (c % 4) * 128
                    for g0 in range(0, nch, 4):
                        a = ntr[0] % 2
                        ntr[0] += 1
                        ng = min(4, nch - g0)
                        for gi in range(ng):
                            tcn = g0 + gi
                            S.op("pe", [pc16b, cb], [ptrb[a]], lambda e, a=a, gi=gi, tcn=tcn: e.transpose(ptr[a][:, gi * 128:(gi + 1) * 128],
                                                                                                      pcb16[:, tcn * 128:(tcn + 1) * 128], ident[:]))
                        S.op("act", [ptrb[a]], [stgb], lambda e, a=a, g0=g0, ng=ng, dcol=dcol: e.copy(
                            stg[:, g0:g0 + ng, dcol:dcol + 128], ptr[a][:, :ng * 128].rearrange("p (g c) -> p g c", c=128)), self_dep=False)
                    if c % 4 == 3:
                        dt = (c % 16) // 4
                        dst = C.hy_u if c < 16 else C.hy_x1
                        dstb = (C.hy_u_b if c < 16 else C.hy_x1_b)[(q, dt)]
                        S.dma("pool", dst[q, dt][:, :nch * 512].rearrange("p (c d) -> p c d", d=512), stg[:], [stgb], [dstb], partial=True)
            S.barrier()


def emit_hy_conv(C, l):
    nc, S = C.nc, C.S
    for (nset, n) in _hy_sets(l):
        nch = n // 128
        W = min(256, n)
        dft = _hy_dft(C, n)
        crv = dft["cr"].rearrange("(fk p) t -> p fk t", p=128)
        srv = dft["sr"].rearrange("(fk p) t -> p fk t", p=128)
        with ExitStack() as es:
            sbt = lambda nm, sh, dt_: es.enter_context(C.sbuf("hc_" + nm, sh, dt_))
            pst = lambda nm: es.enter_context(C.psum("hc_" + nm, [128, 512], F32))
            kre = sbt("kre", [128, nch, 512], F32)
            kim = sbt("kim", [128, nch, 512], F32)
            krs = sbt("krs", [128, 512], F32)
            ktb = Buf("hc_ktab")
            u = sbt("u", [128, nch, 512], BF16)
            ub = Buf("u")
            P = sbt("P", [128, nch, 512], BF16)
            Pb = Buf("P")
            Q = sbt("Q", [128, nch, 512], BF16)
            Qb = Buf("Q")
            t1 = sbt("t1", [128, 512], F32)
            t1b = Buf("t1")
            t2 = sbt("t2", [128, 512], F32)
            t2b = Buf("t2")
            t3 = sbt("t3", [128, 512], F32)
            t3b = Buf("t3")
            t4 = sbt("t4", [128, 512], F32)
            t4b = Buf("t4")
            xg = [sbt(f"xg{j}", [128, 512], BF16) for j in range(2)]
            xgb = [Buf(f"xg{j}") for j in range(2)]
            zo = [sbt(f"zo{j}", [128, 512], BF16) for j in range(2)]
            zob = [Buf(f"zo{j}") for j in range(2)]
            slots = [sbt(f"dsl{j}", [128, 2, nch * 128], BF16) for j in range(2)]
            wslots = [sbt(f"wsl{j}", [128, 2, nch, W], BF16) for j in range(2)]
            psA = [pst(f"pa{j}") for j in range(2)]
            psAb = [Buf(f"pa{j}") for j in range(2)]
            psB = [pst(f"pb{j}") for j in range(2)]
            psBb = [Buf(f"pb{j}") for j in range(2)]
            psY = [pst(f"py{j}") for j in range(2)]
            psYb = [Buf(f"py{j}") for j in range(2)]
            seqs = [_hy_seq_info(C, nset, be)[0] for be in C.be_list]
            seq = []
            wseq = []
            for dt in range(4):
                for o in range(2):
                    for q in seqs:
                        for fc in range(nch):
                            seq.append([(dft["cf"][fc], lambda sl: sl[:, 0, :]), (dft["sf"][fc], lambda sl: sl[:, 1, :])])
                        if o == 0:
                            for tc in range(nch):
                                seq.append([(dft["cf"][tc], lambda sl: sl[:, 0, :]), (dft["si"][tc], lambda sl: sl[:, 1, :])])
                        else:
                            for t0 in range(0, n, W):
                                wseq.append([(crv[:, :, t0:t0 + W], lambda sl: sl[:, 0, :, :]), (srv[:, :, t0:t0 + W], lambda sl: sl[:, 1, :, :])])
            fw = WStream(S, slots, seq, Buf("dftc"))
            ww = WStream(S, wslots, wseq, Buf("dftw"))
            cnt = {"xg": 0, "zo": 0, "py": 0}

            def rot(nm):
                v = cnt[nm] % 2
                cnt[nm] += 1
                return v

            for dt in range(4):
                for o in range(2):
                    S.dma("sp", kre[:], C.ktab[nset, o, 0, dt][:, :nch * 512].rearrange("p (c d) -> p c d", d=512), [C.ktab_b[nset]], [ktb])
                    S.dma("sp", kim[:], C.ktab[nset, o, 1, dt][:, :nch * 512].rearrange("p (c d) -> p c d", d=512), [C.ktab_b[nset]], [ktb], partial=True)
                    S.dma("sp", krs[:], C.krs0[nset, o, dt], [C.ktab_b[nset]], [ktb], partial=True)
                    for q in seqs:
                        usrc = C.hy_u[q, dt][:, :nch * 512].rearrange("p (c d) -> p c d", d=512)
                        S.dma("sp", u[:], usrc, [C.hy_u_b[(q, dt)]], [ub])

                        def cons(fc, a):
                            krx = krs[:] if fc == 0 else kre[:, fc, :]
                            S.op("dve", [psAb[a], ktb], [t1b], lambda e: e.tensor_tensor(t1[:], psA[a][:, :], kre[:, fc, :], ALU.mult))
                            S.op("dve", [psBb[a], ktb], [t2b], lambda e: e.tensor_tensor(t2[:], psB[a][:, :], kim[:, fc, :], ALU.mult))
                            S.op("pool", [t1b, t2b], [Pb], lambda e: e.tensor_tensor(P[:, fc, :], t1[:], t2[:], ALU.add))
                            S.op("dve", [psBb[a], ktb], [t3b], lambda e: e.tensor_tensor(t3[:], psB[a][:, :], krx, ALU.mult))
                            S.op("dve", [psAb[a], ktb], [t4b], lambda e: e.tensor_tensor(t4[:], psA[a][:, :], kim[:, fc, :], ALU.mult))
                            S.op("pool", [t3b, t4b], [Qb], lambda e: e.tensor_tensor(Q[:, fc, :], t3[:], t4[:], ALU.subtract))

                        _dft_forward(C, S, fw, u, ub, n, psA, psAb, psB, psBb, cons, need_a=True)
                        if o == 0:
                            for tc in range(nch):
                                sl = fw.get()
                                a = rot("py")
                                for fk in range(nch):
                                    _mm(S, psY[a][:, :], slots[sl][:, 0, fk * 128:(fk + 1) * 128], P[:, fk, :], fk == 0, False,
                                        [fw.bufs[sl], Pb], [psYb[a]])
                                for fk in range(nch):
                                    _mm(S, psY[a][:, :], slots[sl][:, 1, fk * 128:(fk + 1) * 128], Q[:, fk, :], False, fk == nch - 1,
                                        [fw.bufs[sl], Qb], [psYb[a]])
                                xi = rot("xg")
                                S.dma("pool", xg[xi][:], C.hy_x1[q, dt][:, tc * 512:(tc + 1) * 512], [C.hy_x1_b[(q, dt)]], [xgb[xi]])
                                zi = rot("zo")
                                S.op("dve", [psYb[a], xgb[xi]], [zob[zi]], lambda e: e.tensor_tensor(zo[zi][:], psY[a][:, :], xg[xi][:], ALU.mult))
                                S.dma("pool", C.hy_u[q, dt][:, tc * 512:(tc + 1) * 512], zo[zi][:], [zob[zi]], [C.hy_u_b[(q, dt)]], partial=True)
                        else:
                            for t0 in range(0, n, W):
                                sl = ww.get()
                                for dc in range(4):
                                    a = rot("py")
                                    for fk in range(nch):
                                        _mm(S, psY[a][:, :W], P[:, fk, dc * 128:(dc + 1) * 128], wslots[sl][:, 0, fk, :], fk == 0, False,
                                            [ww.bufs[sl], Pb], [psYb[a]])
                                    for fk in range(nch):
                                        _mm(S, psY[a][:, :W], Q[:, fk, dc * 128:(dc + 1) * 128], wslots[sl][:, 1, fk, :], False, fk == nch - 1,
                                            [ww.bufs[sl], Qb], [psYb[a]])
                                    xi = rot("xg")
                                    S.dma("pool", xg[xi][:, :W], C.hy_x2T[q, dt * 4 + dc][:, t0:t0 + W], [C.hy_x2T_b], [xgb[xi]])
                                    zi = rot("zo")
                                    S.op("dve", [psYb[a], xgb[xi]], [zob[zi]], lambda e: e.tensor_tensor(zo[zi][:, :W], psY[a][:, :W], xg[xi][:, :W], ALU.mult))
                                    S.dma("pool", C.hy_z2T[q, dt * 4 + dc][:, t0:t0 + W], zo[zi][:, :W], [zob[zi]], [C.hy_z2T_b], partial=True)
            S.barrier()


def emit_hy_outproj(C, l):
    nc, S = C.nc, C.S
    i = l // 2
    bo_d = C.win(f"hy_bout_{i}", [128, KC])
    stv = C.st.rearrange("(k p) t -> p k t", p=128)
    with ExitStack() as es:
        sbt = lambda nm, sh, dt_: es.enter_context(C.sbuf("ho_" + nm, sh, dt_))
        pst = lambda nm: es.enter_context(C.psum("ho_" + nm, [128, 512], F32))
        bo = sbt("bo", [128, KC], F32)
        cb = Buf("ho_const")
        S.dma("sp", bo[:], bo_d, [], [cb])
        z = sbt("z", [128, KC, TT], BF16)
        zb = Buf("z")
        wos = [sbt(f"wo{j}", [128, KC, 128], BF16) for j in range(2)]
        xres = [sbt(f"xres{j}", [128, TT], F32) for j in range(2)]
        xresb = [Buf(f"xres{j}") for j in range(2)]
        yb_ = sbt("y", [128, TT], F32)
        ybb = Buf("y")
        pt = [pst(f"pt{j}") for j in range(2)]
        ptb = [Buf(f"pt{j}") for j in range(2)]
        tiles = []
        for (nset, n) in _hy_sets(l):
            W = min(TT, n)
            for be in C.be_list:
                q, col0, r = _hy_seq_info(C, nset, be)
                for t0 in range(0, n, W):
                    tiles.append((q, col0, r, t0, W))
        seq = []
        for _ in tiles:
            for m in range(KC):
                seq.append((C.waob[m].rearrange("p (j c) -> p j c", c=128), lambda sl: sl[:, :, :]))
        ws = WStream(S, wos, seq, C.waob_b)
        n_ = [0]
        for (q, col0, r, t0, W) in tiles:
            S.dma("sp", z[:, :, :W], C.hy_z2T[q][:, :, t0:t0 + W].rearrange("k p t -> p k t"), [C.hy_z2T_b], [zb])
            for m in range(KC):
                sl = ws.get()
                a = n_[0] % 2
                n_[0] += 1
                for k in range(KC):
                    _mm(S, pt[a][:, :W], wos[sl][:, k, :], z[:, k, :W], k == 0, k == KC - 1, [ws.bufs[sl], zb], [ptb[a]])
                S.dma("pool", xres[a][:, :W], stv[:, m, col0 + t0:col0 + t0 + W], [C.st_b], [xresb[a]])
                S.op("act", [ptb[a], cb], [ybb], lambda e: e.activation(out=yb_[:, :W], in_=pt[a][:, :W], func=AF.Identity, bias=bo[:, m:m + 1], scale=1.0))
                S.op("dve", [ybb, xresb[a], C.hg_b], [xresb[a]],
                     lambda e: e.scalar_tensor_tensor(xres[a][:, :W], yb_[:, :W], C.hg[:, 1, m, r:r + 1], xres[a][:, :W], ALU.mult, ALU.add))
                S.dma("pool", stv[:, m, col0 + t0:col0 + t0 + W], xres[a][:, :W], [xresb[a]], [C.st_b], partial=True)
        S.barrier()


def build_program(phases, dbg=False, be_list=None):
    nc = bass.Bass("TRN2", target_bir_lowering=False)
    C = Ctx()
    C.nc = nc
    C.debug = dbg
    C.be_list = list(range(NB)) if be_list is None else be_list
    dt = nc.dram_tensor
    C.xin = dt("xin", [D, NTOK], F32, kind="ExternalInput").ap()
    C.cT = dt("cT", [128, KC, NR], F32, kind="ExternalInput").ap()
    C.st = dt("st", [D, NTOK], F32, kind="ExternalOutput").ap()
    C.winb = [dt(f"winb{i}", [D, 2 * DFF], BF16, kind="Internal").ap() for i in range(2)]
    C.woutb = [dt(f"woutb{i}", [KC, 128, DFF], BF16, kind="Internal").ap() for i in range(2)]
    C.winb_b = [Buf(f"winb{i}") for i in range(2)]
    C.woutb_b = [Buf(f"woutb{i}") for i in range(2)]
    C.st_b = Buf("st")
    C.wainb = dt("wainb", [D, 3072], BF16, kind="Internal").ap()
    C.waob = dt("waob", [KC, 128, D], BF16, kind="Internal").ap()
    C.whinb = dt("whinb", [D, 3 * D], BF16, kind="Internal").ap()
    C.whinb_b = Buf("whinb")
    NQ = 2 * NB
    C.hy_u = dt("hy_u", [NQ, 4, 128, KC * 512], BF16, kind="Internal").ap()
    C.hy_x1 = dt("hy_x1", [NQ, 4, 128, KC * 512], BF16, kind="Internal").ap()
    C.hy_x2T = dt("hy_x2T", [NQ, KC, 128, SEQ], BF16, kind="Internal").ap()
    C.hy_z2T = dt("hy_z2T", [NQ, KC, 128, SEQ], BF16, kind="Internal").ap()
    C.ktab = dt("hy_ktab", [2, 2, 2, 4, 128, KC * 512], F32, kind="Internal").ap()
    C.krs0 = dt("hy_krs0", [2, 2, 4, 128, 512], F32, kind="Internal").ap()
    C.hy_u_b = _BufMap("hy_u")
    C.hy_x1_b = _BufMap("hy_x1")
    C.hy_x2T_b = Buf("hy_x2T")
    C.hy_z2T_b = Buf("hy_z2T")
    C.ktab_b = [Buf("ktab0"), Buf("ktab1")]
    C.wainb_b = Buf("wainb")
    C.waob_b = Buf("waob")

    with ExitStack() as es:
        nc.allow_low_precision("bf16 matmuls with fp32 accumulation")
        S = Sched(nc, es)
        C.S = S
        sb = lambda name, shape, dtype: es.enter_context(C.sbuf(name, shape, dtype))
        C.ones_bf = sb("ones_bf", [128, 128], BF16)
        C.const_b = Buf("const")
        C.s_sb = sb("s_sb", [128, KC, NR], F32)
        C.s_b = Buf("s")
        C.modv = sb("modv", [128, NMOD * KC, NR], F32)
        C.modv_b = Buf("modv")
        C.gs = sb("gs", [128, 3, KC, NR], F32)
        C.gs_b = Buf("gs")
        C.hg = sb("hg", [128, 3, KC, NR], F32)
        C.hg_b = Buf("hg")

        S.op("dve", [], [C.const_b], lambda e: e.memset(C.ones_bf[:], 1.0 / D))
        C.eps_sb = sb("eps_sb", [128, 2], F32)
        S.op("dve", [], [C.const_b], lambda e: e.memset(C.eps_sb[:], EPS))
        S.dma("sp", C.s_sb[:], C.cT, [], [C.s_b])
        S.op("act", [C.s_b], [C.s_b], lambda e: e.activation(out=C.s_sb[:], in_=C.s_sb[:], func=AF.Silu))
        C.s_bf = sb("s_bf", [128, KC, NR], BF16)
        S.op("dve", [C.s_b], [C.s_b], lambda e: e.tensor_copy(C.s_bf[:], C.s_sb[:]))
        for k in range(KC):
            S.dma("pool", C.st[k * 128:(k + 1) * 128, :], C.xin[k * 128:(k + 1) * 128, :], [], [C.st_b], partial=True)

        lat_tiles = [(t * TT, t // (SEQ // TT)) for t in range(NLAT // TT)]
        ctx_tile = [(NLAT + i * TT, NB) for i in range(NB * CTX // TT)]
        for ph in phases:
            kind = ph[0]
            if kind == "conv":
                emit_convert_ffn(C, ph[1], ph[2], ph[3])
            elif kind == "mod":
                emit_mod(C, ph[1])
            elif kind == "hyena":
                emit_hyena(C, ph[1])
            elif kind == "attn":
                emit_attn(C, ph[1])
            elif kind == "ffn":
                _, l, w, slot, with_ctx, ntl = ph
                tiles = lat_tiles[:ntl] + (ctx_tile if with_ctx else [])
                emit_ffn(C, l, w, slot, tiles)
        S.barrier(include_bg=True)
        C.n_ops = S.nops
    return nc, C


def _const_tables():
    import ml_dtypes
    f32 = np.float32
    t = {}
    rows = SEQ // 64
    r, col = np.meshgrid(np.arange(rows), np.arange(64), indexing="ij")
    r = r.reshape(-1).astype(f32)
    col = col.reshape(-1).astype(f32)
    half = HD // 2
    inv = (f32(10000.0) ** (-np.arange(0, half, 2, dtype=f32) / f32(half))).astype(f32)
    ang = np.concatenate([r[:, None] * inv, col[:, None] * inv], axis=-1).astype(f32)
    cs = np.stack([np.cos(ang), np.sin(ang)]).astype(f32)
    cs = np.concatenate([cs, cs], axis=-1)
    t["rope_cs"] = np.ascontiguousarray(cs.transpose(0, 2, 1))
    rm = np.zeros((128, 128), f32)
    for m in range(64):
        rm[m + 64, m] = -1.0
        rm[m, m + 64] = 1.0
    t["rotm"] = rm.astype(ml_dtypes.bfloat16)
    kj = np.arange(128)[:, None]
    qi = np.arange(128)[None, :]
    t["wmask"] = np.ascontiguousarray(np.stack([(kj >= qi), (kj <= qi)], axis=1).astype(f32)).astype(ml_dtypes.bfloat16)
    return t


_HY_CACHE = {}


def _hyena_tables(n):
    if n in _HY_CACHE:
        return _HY_CACHE[n]
    import ml_dtypes
    f32 = np.float32
    bf = ml_dtypes.bfloat16
    t = {}
    nch = n // 128
    N = 2 * n
    tt = np.linspace(0.0, 1.0, n, dtype=f32)
    w = (f32(2.0 * math.pi / n) * np.arange(n, dtype=f32))[:, None]
    bands = np.linspace(1e-4, 15, 16, dtype=f32)
    ang = (w * bands[None, :]).astype(f32)
    feats = np.concatenate([tt[:, None], np.cos(ang), -np.sin(ang)], axis=-1).astype(f32)
    t[f"hy_feats{n}"] = np.ascontiguousarray(feats.T)
    t[f"hy_negt{n}"] = np.ascontiguousarray((-tt).reshape(nch, 128).T)
    wfv = np.full(n, 2.0 / N, f32)
    wfv[0] = 1.0 / N
    wf = wfv.reshape(nch, 128).T
    t[f"hy_wf{n}"] = np.ascontiguousarray(np.stack([wf, -wf], axis=1))
    max_decay = math.log(1e-2) / 0.3
    min_decay = math.log(1e-2) / 1.5
    deltas = np.abs(np.linspace(min_decay, max_decay, D, dtype=f32))
    t["hy_absdelta"] = np.ascontiguousarray(np.broadcast_to(deltas[None, :], (128, D)))
    idx = np.arange(n, dtype=np.int64)
    prod = (idx[:, None] * idx[None, :]) % N
    angm = prod.astype(np.float64) * (2.0 * math.pi / N)
    Cm = np.cos(angm)
    Sm = np.sin(angm)
    Sm[0, :] = np.where(idx % 2 == 0, 1.0, -1.0)
    Cb = Cm.astype(f32).astype(bf)
    Sb = Sm.astype(f32).astype(bf)
    C4 = Cb.reshape(nch, 128, nch, 128)
    S4 = Sb.reshape(nch, 128, nch, 128)
    t[f"dft_cf{n}"] = np.ascontiguousarray(C4.transpose(2, 1, 0, 3)).reshape(nch, 128, nch * 128)
    t[f"dft_sf{n}"] = np.ascontiguousarray(S4.transpose(0, 3, 2, 1)).reshape(nch, 128, nch * 128)
    t[f"dft_si{n}"] = np.ascontiguousarray(S4.transpose(2, 1, 0, 3)).reshape(nch, 128, nch * 128)
    t[f"dft_cr{n}"] = Cb
    t[f"dft_sr{n}"] = Sb
    t["ident_bf"] = np.eye(128, dtype=f32).astype(bf)
    _HY_CACHE[n] = t
    return t


def _prep_inputs(inputs, layers=range(DEPTH)):
    x = inputs["x"]
    ctx = inputs["ctx"]
    c = inputs["c"]
    c_ctx = inputs["c_ctx"]
    shared = {}
    for l in layers:
        shared[f"w_mod{l}"] = inputs["w_mod"][l]
        shared[f"b_mod{l}"] = np.ascontiguousarray(inputs["b_mod"][l].reshape(NMOD * KC, 128).T)
        shared[f"g_norm{l}"] = np.ascontiguousarray(inputs["g_norm"][l].reshape(3, KC, 128).transpose(2, 0, 1))
        for w in range(2):
            shared[f"w_ffn_in{l}_{w}"] = inputs["w_ffn_in"][l, w]
            shared[f"w_ffn_out{l}_{w}"] = inputs["w_ffn_out"][l, w]
    perm = np.concatenate([np.arange(0, HD, 2), np.arange(1, HD, 2)])
    for l in layers:
        if l % 2 == 0:
            i = l // 2
            wi = inputs["w_attn_in"][i]
            qcols = wi[:, :2048].reshape(D, 16, HD)[:, :, perm].reshape(D, 2048)
            kA = wi[:, 2048:2304].reshape(D, 2, HD)[:, :, perm].reshape(D, 256)
            vA = wi[:, 2304:2560]
            kB = wi[:, 2560:2816].reshape(D, 2, HD)[:, :, perm].reshape(D, 256)
            vB = wi[:, 2816:3072]
            shared[f"w_attn_in{i}"] = np.ascontiguousarray(np.concatenate([qcols, kA, kB, vA, vB], axis=1))
            shared[f"w_attn_out{i}"] = inputs["w_attn_out"][i]
            gq = inputs["g_q"][i][:, perm]
            gk = inputs["g_k"][i][:, perm]
            shared[f"gqk{i}"] = np.ascontiguousarray(np.stack([gq[0], gq[1], gk[0], gk[1]], axis=1))
            shared[f"sink{i}"] = np.ascontiguousarray(np.broadcast_to(inputs["sink"][i][None, :], (128, 8)))
    for l in layers:
        if l % 2 == 1:
            i = l // 2
            shared[f"w_hy_in{i}"] = inputs["w_hy_in"][i]
            shared[f"w_hy_out{i}"] = inputs["w_hy_out"][i]
            shared[f"hf_w1_{i}"] = inputs["hf_w1"][i]
            shared[f"hf_w2_{i}"] = inputs["hf_w2"][i]
            shared[f"hf_w3_{i}"] = inputs["hf_w3"][i]
            shared[f"hf_vec_{i}"] = np.ascontiguousarray(np.stack(
                [inputs["hf_b1"][i], inputs["hf_freq1"][i], inputs["hf_b2"][i], inputs["hf_freq2"][i]], axis=1))
            shared[f"hy_bias_{i}"] = np.ascontiguousarray(np.broadcast_to(inputs["hy_bias"][i][None], (128, 2, D)))
            shared[f"hy_bin_{i}"] = np.ascontiguousarray(inputs["b_hy_in"][i].reshape(48, 128).T)
            shared[f"hy_wc_{i}"] = np.ascontiguousarray(inputs["w_hy_conv"][i].reshape(3, 48, 128).transpose(2, 0, 1))
            shared[f"hy_bc_{i}"] = np.ascontiguousarray(inputs["b_hy_conv"][i].reshape(48, 128).T)
            shared[f"hy_bout_{i}"] = np.ascontiguousarray(inputs["b_hy_out"][i].reshape(KC, 128).T)
    shared.update(_const_tables())
    shared.update(_hyena_tables(SEQ))
    if 1 in layers:
        shared.update(_hyena_tables(CTX))
    maps = []
    for core in range(NCORES):
        b0 = core * NB
        xin = np.empty((D, NTOK), np.float32)
        for be in range(NB):
            xin[:, be * SEQ:(be + 1) * SEQ] = x[b0 + be].T
            xin[:, NLAT + be * CTX:NLAT + (be + 1) * CTX] = ctx[b0 + be].T
        cT = np.zeros((128, KC, NR), np.float32)
        for be in range(NB):
            cT[:, :, be] = c[b0 + be].reshape(KC, 128).T
        cT[:, :, NB] = c_ctx.reshape(KC, 128).T
        m = dict(shared)
        m["xin"] = xin
        m["cT"] = cT
        maps.append(m)
    return maps


def full_phases(ntl=NB * 4):
    ph = []
    ph.append(("conv", 0, 0, 0))
    for l in range(DEPTH):
        ph.append(("mod", l))
        ph.append(("conv", l, 1, 1))
        ph.append(("ffn", l, 0, 0, l <= 2, ntl))
        if l + 1 < DEPTH:
            ph.append(("conv", l + 1, 0, 0))
        ph.append(("attn" if l % 2 == 0 else "hyena", l))
        ph.append(("ffn", l, 1, 1, l < 2, ntl))
    return ph


def kernel(**inputs):
    maps = _prep_inputs(inputs)
    nc, C = build_program(full_phases())
    names = list(C.decl.keys()) + ["xin", "cT"]
    res = run_bass_kernel_spmd(nc, [{k: m[k] for k in names} for m in maps], core_ids=list(range(NCORES)))
    out = np.empty((16, SEQ, D), np.float32)
    for core in range(NCORES):
        st = res.results[core]["st"]
        for be in range(NB):
            out[core * NB + be] = st[:, be * SEQ:(be + 1) * SEQ].T
    return out
```
